# Optimizing a Trainium2 kernel written in Bass

```python
import jax
import jax.numpy as jnp
from jax import lax
import numpy as np

D_MODEL = 2048
BATCH = 8
SEQ = 4096
DEPTH = 1

GRID_W = 64
CTX_LEN = 256
MIX_WIDTH = D_MODEL
RWKV_WIDTH = D_MODEL // 2
RWKV_HEAD = 64
RWKV_HEADS = RWKV_WIDTH // RWKV_HEAD
W_LORA = 64
A_LORA = 64
G_LORA = 128
GMLP_WIDTH = MIX_WIDTH - RWKV_WIDTH
GMLP_GROUPS = 16
GMLP_GROUP = GMLP_WIDTH // GMLP_GROUPS
CHUNK = 128
D_FF = 5632
N_MOD = 9
RWKV_IN = 3 * RWKV_WIDTH + W_LORA + A_LORA + G_LORA
IN_WIDTH = RWKV_IN + 2 * GMLP_WIDTH
RWKV_SPLITS = (RWKV_WIDTH, 2 * RWKV_WIDTH, 3 * RWKV_WIDTH,
               3 * RWKV_WIDTH + W_LORA, 3 * RWKV_WIDTH + W_LORA + A_LORA)
ALPHA = (2.0 * DEPTH) ** 0.25
BETA = (8.0 * DEPTH) ** -0.25
LN_EPS = 1e-5
GN_EPS = 64e-5

kernel_name = "hymba_rwkv7_gmlp_macaron_deepnorm_dit"


def layer_norm(x, g, b, eps=LN_EPS):
    xf = x.astype(jnp.float32)
    mu = jnp.mean(xf, axis=-1, keepdims=True)
    var = jnp.mean(jnp.square(xf - mu), axis=-1, keepdims=True)
    return ((xf - mu) * lax.rsqrt(var + eps) * g + b).astype(x.dtype)


def modulate(h, mod, j):
    return h * (1.0 + mod[3 * j + 1]) + mod[3 * j], mod[3 * j + 2]


def swiglu(h, wi, wo):
    gate, up = jnp.split(h @ wi, 2, axis=-1)
    return (jax.nn.silu(gate) * up) @ wo


def heads(t):
    return t.reshape(t.shape[0], t.shape[1], RWKV_HEADS, RWKV_HEAD)


def grid_shift(p, rows):
    b, s, ch = p.shape
    q = p.reshape(b, rows, GRID_W, ch // 4, 4)
    zc = jnp.zeros_like(q[:, :, :1, :, 0])
    zr = jnp.zeros_like(q[:, :1, :, :, 0])
    left = jnp.concatenate([zc, q[:, :, :-1, :, 0]], axis=2)
    right = jnp.concatenate([q[:, :, 1:, :, 1], zc], axis=2)
    up = jnp.concatenate([zr, q[:, :-1, :, :, 2]], axis=1)
    down = jnp.concatenate([q[:, 1:, :, :, 3], zr], axis=1)
    return jnp.stack([left, right, up, down], axis=-1).reshape(b, s, ch)


def seq_shift(p):
    prev = jnp.pad(p, ((0, 0), (1, 0), (0, 0)))[:, :-1]
    nxt = jnp.pad(p, ((0, 0), (0, 1), (0, 0)))[:, 1:]
    even = (jnp.arange(p.shape[-1]) % 2) == 0
    return jnp.where(even, prev, nxt)


def rwkv_prepare(rw, shifted, mu, w0, w_up, a0, a_up, g_up, k_k, k_a):
    xs = rw + (shifted - rw) * mu
    r, k, v, w_lo, a_lo, g_lo = jnp.split(xs, RWKV_SPLITS, axis=-1)
    g = jax.nn.sigmoid(g_lo) @ g_up
    kk = heads(k * k_k).astype(jnp.float32)
    kk = kk / jnp.maximum(jnp.sqrt(jnp.sum(kk * kk, axis=-1, keepdims=True)), 1e-12)
    dirs = []
    for d in range(2):
        w = -jax.nn.softplus(-(w0[d] + jnp.tanh(w_lo) @ w_up[d])) - 0.5
        iclr = jax.nn.sigmoid(a0[d] + a_lo @ a_up[d])
        k_d = k * (1.0 + (iclr - 1.0) * k_a)
        decay = heads(jnp.exp(-jnp.exp(w.astype(jnp.float32))))
        dirs.append((decay, heads(k_d), -kk, kk * heads(iclr).astype(jnp.float32)))
    return heads(r), heads(v), g, dirs


def wkv_scan(r, v, decay, k, a, b, s0, reverse):
    def step(s, inp):
        r_t, v_t, w_t, k_t, a_t, b_t = inp
        sa = jnp.einsum('bhvk,bhk->bhv', s, a_t)
        s = (s * w_t[:, :, None, :] + sa[..., None] * b_t[:, :, None, :]
             + v_t[..., None] * k_t[:, :, None, :])
        return s, jnp.einsum('bhvk,bhk->bhv', s, r_t)
    xs = tuple(jnp.swapaxes(t.astype(jnp.float32), 0, 1) for t in (r, v, decay, k, a, b))
    s_fin, ys = lax.scan(step, s0, xs, reverse=reverse)
    return s_fin, jnp.swapaxes(ys, 0, 1)


def rwkv_output(r, v, g, dirs, ys, r_k, gn_g, gn_b, dtype):
    y = ys[0] + ys[1]
    mu = jnp.mean(y, axis=-1, keepdims=True)
    var = jnp.mean(jnp.square(y - mu), axis=-1, keepdims=True)
    bsz, t = y.shape[0], y.shape[1]
    yn = ((y - mu) * lax.rsqrt(var + GN_EPS)).reshape(bsz, t, RWKV_WIDTH) * gn_g + gn_b
    rf, vf = r.astype(jnp.float32), v.astype(jnp.float32)
    bonus = sum(jnp.sum(rf * dd[1].astype(jnp.float32) * r_k, axis=-1, keepdims=True) * vf
                for dd in dirs)
    out = (yn + bonus.reshape(bsz, t, RWKV_WIDTH)) * g.astype(jnp.float32)
    return out.astype(dtype)


def gmlp_mix(gm, ln_g, ln_b, ws, bs):
    bsz, t, _ = gm.shape
    u, v = jnp.split(jax.nn.gelu(gm), 2, axis=-1)
    v = v.reshape(bsz, t // CHUNK, CHUNK, GMLP_GROUPS, GMLP_GROUP)
    v = layer_norm(v, ln_g.reshape(GMLP_GROUPS, GMLP_GROUP), ln_b.reshape(GMLP_GROUPS, GMLP_GROUP))
    mixed = jnp.einsum('gpq,bnqgd->bnpgd', ws, v) + bs.T[None, None, :, :, None]
    return u * mixed.reshape(bsz, t, GMLP_WIDTH)


def token_mixer(h_x, h_c, rows, with_ctx_out, w_in, mu_shift, w0, w_up, a0, a_up, g_up,
                k_k, k_a, r_k, gn_g, gn_b, gm_ln_g, gm_ln_b, gm_ws, gm_bs, w_out):
    p_x = h_x @ w_in
    p_c = h_c @ w_in
    rw_x, gm_x = p_x[..., :RWKV_IN], p_x[..., RWKV_IN:]
    rw_c, gm_c = p_c[..., :RWKV_IN], p_c[..., RWKV_IN:]
    rparams = (mu_shift, w0, w_up, a0, a_up, g_up, k_k, k_a)
    r_c, v_c, g_c, dirs_c = rwkv_prepare(rw_c, seq_shift(rw_c), *rparams)
    r_x, v_x, g_x, dirs_x = rwkv_prepare(rw_x, grid_shift(rw_x, rows), *rparams)
    s_zero = jnp.zeros((h_x.shape[0], RWKV_HEADS, RWKV_HEAD, RWKV_HEAD), jnp.float32)
    ys_x, ys_c = [], []
    for d in range(2):
        s_c, y_c = wkv_scan(r_c, v_c, *dirs_c[d], s_zero, reverse=(d == 1))
        _, y_x = wkv_scan(r_x, v_x, *dirs_x[d], s_c, reverse=(d == 1))
        ys_c.append(y_c)
        ys_x.append(y_x)
    out_x = jnp.concatenate([
        rwkv_output(r_x, v_x, g_x, dirs_x, ys_x, r_k, gn_g, gn_b, h_x.dtype),
        gmlp_mix(gm_x, gm_ln_g, gm_ln_b, gm_ws, gm_bs)], axis=-1) @ w_out
    out_c = None
    if with_ctx_out:
        out_c = jnp.concatenate([
            rwkv_output(r_c, v_c, g_c, dirs_c, ys_c, r_k, gn_g, gn_b, h_c.dtype),
            gmlp_mix(gm_c, gm_ln_g, gm_ln_b, gm_ws, gm_bs)], axis=-1) @ w_out
    return out_x, out_c


def setup_inputs(seed: int = 0) -> dict:
    key = jax.random.key(seed)
    ks = jax.random.split(key, 32)
    f32 = jnp.float32
    L, D = DEPTH, D_MODEL

    def nrm(k, shape, scale):
        return jax.random.normal(k, shape, f32) * scale

    w0_base = jnp.linspace(-6.0, -1.0, RWKV_WIDTH, dtype=f32)
    return {
        "x": nrm(ks[0], (BATCH, SEQ, D), 1.0),
        "c": nrm(ks[1], (BATCH, D), 1.0),
        "ctx": nrm(ks[2], (BATCH, CTX_LEN, D), 1.0),
        "c_ctx": nrm(ks[3], (D,), 1.0),
        "w_ada": nrm(ks[4], (L, D, N_MOD * D), 0.5 * D ** -0.5),
        "b_ada": nrm(ks[5], (L, N_MOD * D), 0.02),
        "ln_g": 1.0 + nrm(ks[6], (L, 3, D), 0.02),
        "ln_b": nrm(ks[7], (L, 3, D), 0.02),
        "ffn_a_wi": nrm(ks[8], (L, D, 2 * D_FF), D ** -0.5),
        "ffn_a_wo": nrm(ks[9], (L, D_FF, D), BETA * D_FF ** -0.5),
        "ffn_b_wi": nrm(ks[10], (L, D, 2 * D_FF), D ** -0.5),
        "ffn_b_wo": nrm(ks[11], (L, D_FF, D), BETA * D_FF ** -0.5),
        "w_in": nrm(ks[12], (L, D, IN_WIDTH), D ** -0.5),
        "mu_shift": jax.random.uniform(ks[13], (L, RWKV_IN), f32),
        "w0": w0_base + nrm(ks[14], (L, 2, RWKV_WIDTH), 0.1),
        "w_up": nrm(ks[15], (L, 2, W_LORA, RWKV_WIDTH), 0.1 * W_LORA ** -0.5),
        "a0": nrm(ks[16], (L, 2, RWKV_WIDTH), 0.1),
        "a_up": nrm(ks[17], (L, 2, A_LORA, RWKV_WIDTH), A_LORA ** -0.5),
        "g_up": nrm(ks[18], (L, G_LORA, RWKV_WIDTH), G_LORA ** -0.5),
        "k_k": 0.85 + nrm(ks[19], (L, RWKV_WIDTH), 0.02),
        "k_a": 1.0 + nrm(ks[20], (L, RWKV_WIDTH), 0.02),
        "r_k": nrm(ks[21], (L, RWKV_HEADS, RWKV_HEAD), 0.1),
        "gn_g": 1.0 + nrm(ks[22], (L, RWKV_WIDTH), 0.02),
        "gn_b": nrm(ks[23], (L, RWKV_WIDTH), 0.02),
        "gm_ln_g": 1.0 + nrm(ks[24], (L, GMLP_WIDTH), 0.02),
        "gm_ln_b": nrm(ks[25], (L, GMLP_WIDTH), 0.02),
        "gm_ws": nrm(ks[26], (L, GMLP_GROUPS, CHUNK, CHUNK), CHUNK ** -0.5),
        "gm_bs": 1.0 + nrm(ks[27], (L, GMLP_GROUPS, CHUNK), 0.02),
        "w_out": nrm(ks[28], (L, MIX_WIDTH, D), BETA * MIX_WIDTH ** -0.5),
    }


def reference(x, c, ctx, c_ctx, w_ada, b_ada, ln_g, ln_b, ffn_a_wi, ffn_a_wo, ffn_b_wi,
              ffn_b_wo, w_in, mu_shift, w0, w_up, a0, a_up, g_up, k_k, k_a, r_k, gn_g, gn_b,
              gm_ln_g, gm_ln_b, gm_ws, gm_bs, w_out):
    bsz, seq_len, d = x.shape
    rows = seq_len // GRID_W
    cx = ctx
    for i in range(DEPTH):
        last = i == DEPTH - 1
        mod_x = (jax.nn.silu(c) @ w_ada[i] + b_ada[i]).reshape(bsz, N_MOD, d).transpose(1, 0, 2)[:, :, None, :]
        mod_c = (jax.nn.silu(c_ctx) @ w_ada[i] + b_ada[i]).reshape(N_MOD, 1, 1, d)

        hx, gx = modulate(x, mod_x, 0)
        hc, gc = modulate(cx, mod_c, 0)
        x = layer_norm(ALPHA * x + 0.5 * gx * swiglu(hx, ffn_a_wi[i], ffn_a_wo[i]), ln_g[i, 0], ln_b[i, 0])
        cx = layer_norm(ALPHA * cx + 0.5 * gc * swiglu(hc, ffn_a_wi[i], ffn_a_wo[i]), ln_g[i, 0], ln_b[i, 0])

        hx, gx = modulate(x, mod_x, 1)
        hc, gc = modulate(cx, mod_c, 1)
        out_x, out_c = token_mixer(hx, hc, rows, not last, w_in[i], mu_shift[i], w0[i], w_up[i],
                                   a0[i], a_up[i], g_up[i], k_k[i], k_a[i], r_k[i], gn_g[i], gn_b[i],
                                   gm_ln_g[i], gm_ln_b[i], gm_ws[i], gm_bs[i], w_out[i])
        x = layer_norm(ALPHA * x + gx * out_x, ln_g[i, 1], ln_b[i, 1])
        if not last:
            cx = layer_norm(ALPHA * cx + gc * out_c, ln_g[i, 1], ln_b[i, 1])

        hx, gx = modulate(x, mod_x, 2)
        x = layer_norm(ALPHA * x + 0.5 * gx * swiglu(hx, ffn_b_wi[i], ffn_b_wo[i]), ln_g[i, 2], ln_b[i, 2])
        if not last:
            hc, gc = modulate(cx, mod_c, 2)
            cx = layer_norm(ALPHA * cx + 0.5 * gc * swiglu(hc, ffn_b_wi[i], ffn_b_wo[i]), ln_g[i, 2], ln_b[i, 2])
    return x
```

```python
import numpy as np
from contextlib import ExitStack
import concourse.bass as bass
import concourse.mybir as mybir
from concourse.bass_utils import run_bass_kernel_spmd

F32 = mybir.dt.float32
BF16 = mybir.dt.bfloat16
F32R = mybir.dt.float32r
AF = mybir.ActivationFunctionType
ALU = mybir.AluOpType

D = 2048
T = 4096
CT = 256
TT = T + CT
DFF = 5632
NMOD = 9
RW = 1024
RWIN = 3328
INW = 5376
ALPHA = 2.0 ** 0.25
LN_EPS = 1e-5
GN_EPS = 64e-5
NCORES = 8

VC = {}
_o = 0
for _n, _w in [("c", 16), ("cctx", 16), ("b_ada", 144), ("ln_g", 48), ("ln_b", 48), ("mu", 26),
               ("w0", 16), ("a0", 16), ("k_k", 8), ("k_a", 8), ("r_k", 8), ("gn_g", 8), ("gn_b", 8),
               ("sel4", 4), ("sel2", 2)]:
    VC[_n] = (_o, _w)
    _o += _w
NV = _o

ENGS = ["pe", "act", "dve", "pool", "sp"]


class Tok:
    __slots__ = ("sk", "n")

    def __init__(self, sk):
        self.sk = sk
        self.n = None


class Prog:
    EPOCH = 30000

    def __init__(self, nc, es):
        self.nc = nc
        self.es = es
        self.ops = {e: [] for e in ENGS}
        self.cnt = {}
        self.cur = {}
        self.lastw = {}
        self.readers = {}
        self.semh = {}
        self.latest = {}

    def _tok(self, sk, sig):
        t = self.cur.get(sk)
        if t is None:
            t = Tok(sk)
            self.cur[sk] = t
        if sig:
            self.cnt[sk] = self.cnt.get(sk, 0) + 1
            t.n = self.cnt[sk]
            self.cur[sk] = None
            self.latest[sk] = t
        return t

    def _deps(self, reads, writes, own_group=None):
        deps = []
        for k in reads:
            t = self.lastw.get(k)
            if t is not None:
                deps.append(t)
        for k in writes:
            t = self.lastw.get(k)
            if t is not None and not (own_group is not None and t.sk == own_group):
                deps.append(t)
            r = self.readers.get(k)
            if r:
                deps.extend(r.values())
        return deps

    def _commit(self, tok, reads, writes):
        for k in reads:
            self.readers.setdefault(k, {})[tok.sk] = tok
        for k in writes:
            self.lastw[k] = tok
            self.readers[k] = {}

    def op(self, eng, fn, reads=(), writes=(), sig=True):
        deps = self._deps(reads, writes)
        tok = self._tok(eng, sig)
        self.ops[eng].append((deps, fn, tok if sig else None, 1))
        self._commit(tok, reads, writes)
        return tok

    def dma(self, q, out, in_, reads=(), writes=(), group=None):
        if group is None:
            group = writes[0] if writes else reads[0]
        sk = ("dma", group)
        deps = self._deps(reads, writes, own_group=sk)
        tok = self._tok(sk, True)
        self.ops[q].append((deps, (lambda e, o=out, i=in_: e.dma_start(out=o, in_=i)), tok, 16))
        self._commit(tok, reads, writes)
        return tok

    def barrier(self):
        toks = [t for t in self.latest.values()]
        for e in ENGS:
            self.ops[e].append((list(toks), None, None, 0))

    def _semval(self, tok, per):
        ep = (tok.n - 1) // per
        v = (tok.n - 1) % per + 1
        key = (tok.sk, ep)
        h = self.semh.get(key)
        if h is None:
            h = self.es.enter_context(self.nc.semaphore("s%d" % len(self.semh)))
            self.semh[key] = h
        return key, h, v

    def emit(self, block):
        def per_of(sk):
            return self.EPOCH // 16 if isinstance(sk, tuple) else self.EPOCH

        def run(e, name):
            waited = {}
            for deps, fn, tok, inc in self.ops[name]:
                need = {}
                for t in deps:
                    if t.n is None:
                        raise RuntimeError("unsignaled dependency on %s" % (t.sk,))
                    if t.sk == name and name == "pe":
                        continue
                    key, h, v = self._semval(t, per_of(t.sk))
                    if waited.get(key, 0) >= v:
                        continue
                    if need.get(key, (None, 0))[1] < v:
                        need[key] = (h, v)
                for key, (h, v) in need.items():
                    e.wait_ge(h, v * (16 if isinstance(key[0], tuple) else 1))
                    waited[key] = v
                if fn is None:
                    continue
                ins = fn(e)
                if tok is not None:
                    key, h, v = self._semval(tok, per_of(tok.sk))
                    ins.then_inc(h, inc)

        for name in ENGS:
            for deps, fn, tok, inc in self.ops[name]:
                for t in deps:
                    if t.n is not None:
                        self._semval(t, per_of(t.sk))
                if tok is not None:
                    self._semval(tok, per_of(tok.sk))

        @block.tensor
        def _(e):
            run(e, "pe")

        @block.scalar
        def _(e):
            run(e, "act")

        @block.vector
        def _(e):
            run(e, "dve")

        @block.gpsimd
        def _(e):
            run(e, "pool")

        @block.sync
        def _(e):
            run(e, "sp")


class Carver:
    def __init__(self, pool, nwords):
        self.pool = pool
        self.n = nwords
        self.off = 0

    def mark(self):
        return self.off

    def reset(self, m):
        self.peak = max(getattr(self, "peak", 0), self.off)
        self.off = m

    def f32(self, n, dt=None):
        n = (n + 1) // 2 * 2
        a = self.pool[:, self.off:self.off + n]
        self.off += n
        assert self.off <= self.n, "SBUF pool overflow %d > %d" % (self.off, self.n)
        return a

    def bf16(self, n):
        w = (n + 3) // 4 * 2
        a = self.pool[:, self.off:self.off + w].bitcast(BF16)
        self.off += w
        assert self.off <= self.n, "SBUF pool overflow %d > %d" % (self.off, self.n)
        return a[:, 0:n]


def r3(ap, **kw):
    return ap.rearrange("p (a b) -> p a b", **kw)


class Ctx:
    pass


def build(stage=99):
    nc = bass.Bass("TRN2", target_bir_lowering=False)
    C = Ctx()
    C.nc = nc
    dt_in = lambda n, s: nc.dram_tensor(n, s, F32, kind="ExternalInput").ap()
    C.xT = dt_in("xT", [D, TT])
    C.vecs = dt_in("vecs", [128, NV])
    C.w_ada = dt_in("w_ada", [D, NMOD * D])
    C.wa_i = dt_in("ffn_a_wi", [D, 2 * DFF])
    C.wa_o = dt_in("ffn_a_wo", [DFF, D])
    C.wb_i = dt_in("ffn_b_wi", [D, 2 * DFF])
    C.wb_o = dt_in("ffn_b_wo", [DFF, D])
    C.w_in = dt_in("w_in", [D, INW])
    C.w_out = dt_in("w_out", [D, D])
    C.gm_ln_g = dt_in("gm_ln_g", [1024])
    C.gm_ln_b = dt_in("gm_ln_b", [1024])
    C.gm_bs = dt_in("gm_bs", [2048])
    C.gm_wsT = dt_in("gm_wsT", [128, 16, 128])
    C.cmat = dt_in("cmat", [128, 7, 128])
    C.lora = dt_in("lora", [128, 2, 2, 1024])
    C.g_up = dt_in("g_up", [128, 1024])
    sc = lambda n, s, d: nc.dram_tensor(n, s, d).ap()
    C.wa_i_b = sc("wa_i_b", [D, 2 * DFF], BF16)
    C.wa_o_b = sc("wa_o_b", [DFF, D], BF16)
    C.wb_i_b = sc("wb_i_b", [D, 2 * DFF], BF16)
    C.wb_o_b = sc("wb_o_b", [DFF, D], BF16)
    C.w_in_b = sc("w_in_b", [D, INW], BF16)
    C.w_out_b = sc("w_out_b", [D, D], BF16)
    dbg = lambda n, s, d, st: (nc.dram_tensor("dbg", s, d, kind="ExternalOutput").ap() if stage == st
                                else sc(n, s, d))
    C.x1T = dbg("x1T", [D, TT], F32, 1)
    C.pT = dbg("pT", [RWIN, TT], F32, 2)
    C.catT = dbg("catT", [D, T], BF16, 5)
    C.xsT = dbg("xsT", [RWIN, TT], F32, 3)
    C.yTs = [dbg("yT0", [RW, T], F32, 4), sc("yT1", [RW, T], F32)]
    C.bonT = [sc("bonT0", [RW, T], F32), sc("bonT1", [RW, T], F32)]
    C.gTs = sc("gTs", [RW, T], F32)
    C.x2T = dbg("x2T", [D, T], F32, 6)
    C.outT = nc.dram_tensor("outT", [D, T], F32, kind="ExternalOutput").ap() if stage >= 99 else sc("outT", [D, T], F32)

    with ExitStack() as es:
        NCW = 9 * 256
        cpool = es.enter_context(nc.sbuf_tensor("cpool", [128, NCW], F32))
        C.ps = [es.enter_context(nc.psum_tensor("ps%d" % i, [128, 512], F32)) for i in range(8)]
        p = Prog(nc, es)
        C.p = p
        C.AC = Carver(cpool, NCW)
        st = {"k": 0}

        def run_phase(fn, kib, rkib=0):
            st["k"] += 1
            cm = nc.sbuf_tensor("ph%d" % st["k"], [128, kib * 256], F32)
            C.A = Carver(cm.__enter__(), kib * 256)
            cr = None
            if rkib:
                cr = nc.sbuf_tensor("rp%d" % st["k"], [128, rkib * 256], F32R)
                C.AR = Carver(cr.__enter__(), rkib * 256)
            fn()
            print("phase", st["k"], "pool use KiB", max(C.A.off, getattr(C.A, "peak", 0)) / 256.0,
                  (max(C.AR.off, getattr(C.AR, "peak", 0)) / 256.0 if rkib else 0))
            if cr is not None:
                cr.__exit__(None, None, None)
            cm.__exit__(None, None, None)
            p.barrier()

        def ph0():
            phase_consts(C)
            phase_cast(C)
            phase_mod(C)
        run_phase(ph0, 130)
        tiles = [(512 * i, 512, "x") for i in range(8)] + [(T, CT, "c")]
        run_phase(lambda: ffn_phase(C, "fa", C.xT, C.x1T, tiles, C.wa_i_b, C.wa_o_b, 0, 0), 196)
        if stage >= 2:
            run_phase(lambda: proj_phase(C), 184)
        if stage >= 3:
            run_phase(lambda: shift_phase(C), 40)
        if stage >= 4:
            def mix():
                mixer_consts(C)
                m1, m2 = C.A.mark(), C.AR.mark()
                for d in range(2 if stage >= 5 else 1):
                    mixer_pass(C, d)
                    C.A.reset(m1)
                    C.AR.reset(m2)
                    p.barrier()
            run_phase(mix, 60, 136)
        if stage >= 5:
            run_phase(lambda: rwkv_out_phase(C), 60)
        if stage >= 6:
            run_phase(lambda: wout_phase(C), 160)
        if stage >= 99:
            tiles_b = [(512 * i, 512, "x") for i in range(8)]
            run_phase(lambda: ffn_phase(C, "fb", C.x2T, C.outT, tiles_b, C.wb_i_b, C.wb_o_b, 2, 2), 196)
        p.barrier()
        block = es.enter_context(nc.Block())
        p.emit(block)
    return nc


def phase_consts(C):
    p, A = C.p, C.AC
    C.vt = A.f32(NV)
    p.dma("sp", C.vt, C.vecs, writes=["vecs"])
    C.ones = A.f32(128)
    C.epsln = A.f32(2)[:, 0:1]
    p.op("pool", lambda e: e.memset(C.epsln, LN_EPS / (ALPHA * ALPHA)), writes=["epsc"])
    p.op("pool", lambda e: e.memset(C.ones, 1.0), writes=["ones"])
    C.epsg = A.f32(2)[:, 0:1]
    p.op("pool", lambda e: e.memset(C.epsg, LN_EPS), writes=["epsc2"])


def vcol(C, name, i=0, n=1):
    o, w = VC[name]
    return C.vt[:, o + i:o + i + n]


def phase_cast(C):
    p = C.p
    C.cast_keys = {}

    def cast(name, src, dst, rows, piece):
        keys = []
        for r0 in range(0, rows, piece):
            k = ("w", name, r0)
            p.dma("pool", dst[r0:r0 + piece, :], src[r0:r0 + piece, :], writes=[k], group=("cast", name))
            keys.append(k)
        C.cast_keys[name] = keys

    C.cast = cast


def phase_mod(C):
    p, A, AC = C.p, C.A, C.AC
    s_bf = r3(AC.bf16(32), b=2)
    ctmp = AC.f32(32)
    C.modx = None
    oc, _ = VC["c"]
    p.op("act", lambda e: e.activation(out=ctmp, in_=C.vt[:, oc:oc + 32], func=AF.Silu),
         reads=["vecs"], writes=["ctmp"])
    p.op("dve", lambda e: e.tensor_copy(out=s_bf[:, :, 0], in_=ctmp[:, 0:16]), reads=["ctmp"], writes=["s_bf0"])
    p.op("dve", lambda e: e.tensor_copy(out=s_bf[:, :, 1], in_=ctmp[:, 16:32]), reads=["ctmp"], writes=["s_bf1"])
    wad = [r3(A.bf16(16 * 2048), b=2048) for _ in range(2)]
    src = C.w_ada.rearrange("(kc p) f -> p kc f", p=128)
    psm = C.ps[0]
    for s in range(NMOD):
        w = wad[s % 2]
        for h in range(2):
            p.dma("pool", w[:, 8 * h:8 * h + 8, :], src[:, 8 * h:8 * h + 8, s * D:(s + 1) * D],
                  writes=[("wad", s % 2)])
        for j in range(16):
            m = s * 16 + j
            for kc in range(16):
                p.op("pe", lambda e, w=w, kc=kc, j=j, m=m: e.matmul(
                    psm[:, 2 * m:2 * m + 2], w[:, kc, j * 128:(j + 1) * 128], s_bf[:, kc, :],
                    start=(kc == 0), stop=(kc == 15)),
                    reads=[("wad", s % 2), "s_bf0", "s_bf1"], writes=["psmod"], sig=(kc == 15))
    C.cast("wa_i", C.wa_i, C.wa_i_b, D, 128)
    C.cast("wa_o", C.wa_o, C.wa_o_b, DFF, 128)
    C.cast("w_in", C.w_in, C.w_in_b, D, 128)
    C.cast("w_out", C.w_out, C.w_out_b, D, 128)
    C.cast("wb_i", C.wb_i, C.wb_i_b, D, 128)
    C.cast("wb_o", C.wb_o, C.wb_o_b, DFF, 128)
    C.mod = {}
    ob, _ = VC["b_ada"]
    pv = psm[:, 0:288].rearrange("p (m two) -> p m two", two=2)
    for idx, nm in enumerate(["x", "c"]):
        mt = AC.f32(144)
        p.op("dve", lambda e, mt=mt, idx=idx: e.tensor_tensor(out=mt, in0=pv[:, :, idx], in1=C.vt[:, ob:ob + 144],
                                                              op=ALU.add),
             reads=["psmod", "vecs"], writes=[("mod", nm)])
        der = AC.f32(16 * 6)
        for j in range(3):
            p.op("dve", lambda e, der=der, mt=mt, j=j: e.tensor_scalar_add(
                out=der[:, 16 * j:16 * j + 16], in0=mt[:, (3 * j + 1) * 16:(3 * j + 2) * 16], scalar1=1.0),
                reads=[("mod", nm)], writes=[("der", nm, j)])
            gs = (0.5 if j != 1 else 1.0) / ALPHA
            p.op("dve", lambda e, der=der, mt=mt, j=j, gs=gs: e.tensor_scalar_mul(
                out=der[:, 48 + 16 * j:48 + 16 * j + 16], in0=mt[:, (3 * j + 2) * 16:(3 * j + 3) * 16], scalar1=gs),
                reads=[("mod", nm)], writes=[("derg", nm, j)])
        C.mod[nm] = (mt, der)


def mod_aps(C, nm, j):
    mt, der = C.mod[nm]
    shift = mt[:, (3 * j) * 16:(3 * j + 1) * 16]
    sc1p = der[:, 16 * j:16 * j + 16]
    gate = der[:, 48 + 16 * j:48 + 16 * j + 16]
    keys = [("mod", nm), ("der", nm, j), ("derg", nm, j)]
    return shift, sc1p, gate, keys


def ffn_phase(C, tag, src, dst, tiles, wi_b, wo_b, j, lnj):
    p, A, nc = C.p, C.A, C.nc
    W = Ctx()
    W.hT = [r3(A.bf16(16 * 512), b=512) for _ in range(2)]
    W.gT = r3(A.bf16(44 * 512), b=512)
    W.z = r3(A.f32(16 * 512), b=512)
    W.ws = [A.bf16(16384) for _ in range(2)]
    W.xst = [A.f32(512) for _ in range(2)]
    W.sg = [A.f32(512) for _ in range(2)]
    W.sq = [A.f32(512) for _ in range(2)]
    W.mean = A.f32(512)
    W.var = A.f32(512)
    W.rstd = A.f32(512)
    W.ost = [A.f32(512) for _ in range(2)]
    wi_v = wi_b.rearrange("(kc p) f -> p kc f", p=128)
    wo_v = wo_b.rearrange("(kc p) f -> p kc f", p=128)
    wname_i = {"fa": "wa_i", "fb": "wb_i"}[tag]
    wname_o = {"fa": "wa_o", "fb": "wb_o"}[tag]
    wkeys_i = C.cast_keys[wname_i]
    wkeys_o = C.cast_keys[wname_o]
    og, _ = VC["ln_g"]
    ol, _ = VC["ln_b"]
    eps_p = LN_EPS / (ALPHA * ALPHA)
    st = {"w": 0, "q": 0, "x": 0, "o": 0}

    def modulate(ti):
        t0, nt, nm = tiles[ti]
        shift, sc1p, gate, mk = mod_aps(C, nm, j)
        hT = W.hT[ti % 2]
        for dc in range(16):
            xs = W.xst[st["x"] % 2]
            xk = ("xst", st["x"] % 2)
            st["x"] += 1
            p.dma("pool", xs[:, :nt], src[dc * 128:(dc + 1) * 128, t0:t0 + nt],
                  reads=[("dst", "w3", ti, dc)], writes=[xk])
            if dc % 2 == 0:
                p.op("act", lambda e, xs=xs, hT=hT, dc=dc, nt=nt: e.activation(
                    out=hT[:, dc, :nt], in_=xs[:, :nt], func=AF.Identity,
                    scale=sc1p[:, dc:dc + 1], bias=shift[:, dc:dc + 1]),
                    reads=[xk] + mk, writes=[("hT", ti % 2, dc)])
            else:
                p.op("dve", lambda e, xs=xs, hT=hT, dc=dc, nt=nt: e.tensor_scalar(
                    out=hT[:, dc, :nt], in0=xs[:, :nt], scalar1=sc1p[:, dc:dc + 1], scalar2=shift[:, dc:dc + 1],
                    op0=ALU.mult, op1=ALU.add),
                    reads=[xk] + mk, writes=[("hT", ti % 2, dc)])

    def up(ti):
        t0, nt, nm = tiles[ti]
        hT = W.hT[ti % 2]
        hk = [("hT", ti % 2, dc) for dc in range(16)]
        for fb in range(11):
            si = st["w"] % 2
            st["w"] += 1
            slot = W.ws[si].rearrange("p (w k f) -> p w k f", w=2, k=16)
            for w_ in range(2):
                c0 = w_ * DFF + fb * 512
                p.dma("sp", slot[:, w_], wi_v[:, :, c0:c0 + 512], reads=wkeys_i, writes=[("ws", si)])
            for jj in range(4):
                q = st["q"]
                st["q"] += 1
                pg, pu = C.ps[2 * (q % 2)], C.ps[2 * (q % 2) + 1]
                for w_, ps in ((0, pg), (1, pu)):
                    for kc in range(16):
                        p.op("pe", lambda e, ps=ps, slot=slot, w_=w_, kc=kc, jj=jj, hT=hT, nt=nt: e.matmul(
                            ps[:, :nt], slot[:, w_, kc, jj * 128:(jj + 1) * 128], hT[:, kc, :nt],
                            start=(kc == 0), stop=(kc == 15)),
                            reads=[("ws", si)] + (hk if kc == 0 else []), writes=[("ps", 2 * (q % 2) + w_)],
                            sig=(kc == 15))
                sg = W.sg[q % 2]
                p.op("act", lambda e, sg=sg, pg=pg, nt=nt: e.activation(out=sg[:, :nt], in_=pg[:, :nt], func=AF.Silu),
                     reads=[("ps", 2 * (q % 2))], writes=[("sg", q % 2)])
                fc = fb * 4 + jj
                p.op("dve", lambda e, sg=sg, pu=pu, fc=fc, nt=nt: e.tensor_tensor(
                    out=W.gT[:, fc, :nt], in0=sg[:, :nt], in1=pu[:, :nt], op=ALU.mult),
                    reads=[("sg", q % 2), ("ps", 2 * (q % 2) + 1)], writes=[("gT", fc)])

    def down(ti):
        t0, nt, nm = tiles[ti]
        shift, sc1p, gate, mk = mod_aps(C, nm, j)
        gk = [("gT", fc) for fc in range(44)]
        pend = None

        def stats(oc, nt=nt):
            p.op("pe", lambda e: e.matmul(C.ps[6][:, :nt], C.ones, W.z[:, oc, :nt], start=(oc == 0), stop=(oc == 15)),
                 reads=["ones", ("z", oc)], writes=[("ps", 6)], sig=(oc == 15))
            p.op("pe", lambda e: e.matmul(C.ps[7][:, :nt], C.ones, W.sq[oc % 2][:, :nt], start=(oc == 0),
                                          stop=(oc == 15)),
                 reads=["ones", ("sq", oc % 2)], writes=[("ps", 7)], sig=True)

        for ob in range(8):
            si = st["w"] % 2
            st["w"] += 1
            slot = W.ws[si][:, 0:44 * 256].rearrange("p (k f) -> p k f", k=44)
            for h in range(2):
                p.dma("sp", slot[:, 22 * h:22 * h + 22, :], wo_v[:, 22 * h:22 * h + 22, ob * 256:(ob + 1) * 256],
                      reads=wkeys_o, writes=[("ws", si)])
            for o2 in range(2):
                oc = ob * 2 + o2
                p.dma("pool", W.z[:, oc, :nt], src[oc * 128:(oc + 1) * 128, t0:t0 + nt],
                      reads=[("dst", "w3", ti, oc)], writes=[("z", oc)])
                ps = C.ps[4 + oc % 2]
                for kc in range(44):
                    p.op("pe", lambda e, ps=ps, slot=slot, kc=kc, o2=o2, nt=nt: e.matmul(
                        ps[:, :nt], slot[:, kc, o2 * 128:(o2 + 1) * 128], W.gT[:, kc, :nt],
                        start=(kc == 0), stop=(kc == 43)),
                        reads=[("ws", si)] + (gk if kc == 0 else []), writes=[("ps", 4 + oc % 2)], sig=(kc == 43))
                if pend is not None:
                    stats(pend)
                p.op("dve", lambda e, ps=ps, oc=oc, nt=nt: e.scalar_tensor_tensor(
                    out=W.z[:, oc, :nt], in0=ps[:, :nt], scalar=gate[:, oc:oc + 1], in1=W.z[:, oc, :nt],
                    op0=ALU.mult, op1=ALU.add),
                    reads=[("ps", 4 + oc % 2)] + mk, writes=[("z", oc)])
                p.op("act", lambda e, oc=oc, nt=nt: e.activation(out=W.sq[oc % 2][:, :nt], in_=W.z[:, oc, :nt],
                                                                  func=AF.Square),
                     reads=[("z", oc)], writes=[("sq", oc % 2)])
                pend = oc
        stats(pend)
        inv = 1.0 / D
        p.op("act", lambda e: e.activation(out=W.mean[:, :nt], in_=C.ps[6][:, :nt], func=AF.Copy, scale=inv),
             reads=[("ps", 6)], writes=["mean"])
        p.op("act", lambda e: e.activation(out=W.var[:, :nt], in_=C.ps[6][:, :nt], func=AF.Square, scale=inv),
             reads=[("ps", 6)], writes=["var"])
        p.op("dve", lambda e: e.scalar_tensor_tensor(out=W.var[:, :nt], in0=C.ps[7][:, :nt], scalar=inv,
                                                     in1=W.var[:, :nt], op0=ALU.mult, op1=ALU.subtract),
             reads=[("ps", 7), "var"], writes=["var"])
        p.op("act", lambda e: e.activation(out=W.rstd[:, :nt], in_=W.var[:, :nt], func=AF.Ln, bias=C.epsln, scale=1.0),
             reads=["var", "epsc"], writes=["rstd"])
        p.op("act", lambda e: e.activation(out=W.rstd[:, :nt], in_=W.rstd[:, :nt], func=AF.Exp, scale=-0.5),
             reads=["rstd"], writes=["rstd"])
        for oc in range(16):
            p.op("pool", lambda e, oc=oc: e.tensor_tensor(out=W.z[:, oc, :nt], in0=W.z[:, oc, :nt], in1=W.mean[:, :nt],
                                                          op=ALU.subtract),
                 reads=[("z", oc), "mean"], writes=[("z", oc)])
            p.op("dve", lambda e, oc=oc: e.tensor_tensor(out=W.z[:, oc, :nt], in0=W.z[:, oc, :nt], in1=W.rstd[:, :nt],
                                                         op=ALU.mult),
                 reads=[("z", oc), "rstd"], writes=[("z", oc)])
            oi = st["o"] % 2
            st["o"] += 1
            ot = W.ost[oi]
            gcol = C.vt[:, og + lnj * 16 + oc:og + lnj * 16 + oc + 1]
            bcol = C.vt[:, ol + lnj * 16 + oc:ol + lnj * 16 + oc + 1]
            p.op("act", lambda e, oc=oc, ot=ot, gcol=gcol, bcol=bcol: e.activation(
                out=ot[:, :nt], in_=W.z[:, oc, :nt], func=AF.Identity, scale=gcol, bias=bcol),
                reads=[("z", oc), "vecs"], writes=[("ost", oi)])
            p.dma("pool", dst[oc * 128:(oc + 1) * 128, t0:t0 + nt], ot[:, :nt],
                  reads=[("ost", oi)], writes=[("dst", tag, ti, oc)], group=("ostd", oi))

    modulate(0)
    for ti in range(len(tiles)):
        up(ti)
        if ti + 1 < len(tiles):
            modulate(ti + 1)
        down(ti)


def proj_phase(C):
    p, A = C.p, C.A
    W = Ctx()
    W.hT = [r3(A.bf16(16 * 512), b=512) for _ in range(2)]
    W.ws = [r3(A.bf16(16 * 512), b=512) for _ in range(4)]
    W.xst = [A.f32(512) for _ in range(2)]
    W.ost = [A.f32(512) for _ in range(3)]
    W.uT = r3(A.f32(8 * 512), b=512)
    W.vg = [A.f32(1024) for _ in range(4)]
    W.sqv = A.f32(1024)
    W.vn = [A.bf16(1024) for _ in range(2)]
    W.gmo = r3(A.bf16(8 * 512), b=512)
    W.st = [A.f32(16) for _ in range(6)]
    W.tmp = [A.f32(128) for _ in range(2)]
    lng = A.f32(1024)
    lnb = A.f32(1024)
    bsb = r3(A.f32(16 * 128), b=128)
    wsT = r3(A.bf16(16 * 128), b=128)
    p.dma("sp", lng, C.gm_ln_g.to_broadcast([128, 1024]) if False else bcast_rows(C.gm_ln_g, 1024), writes=["lng"])
    p.dma("sp", lnb, bcast_rows(C.gm_ln_b, 1024), writes=["lnb"])
    p.dma("sp", bsb, bcast_rows(C.gm_bs, 2048).rearrange("p (a b) -> p a b", b=128), writes=["bsb"])
    p.dma("pool", wsT, C.gm_wsT, writes=["wsT"])
    w_v = C.w_in_b.rearrange("(kc p) f -> p kc f", p=128)
    wkeys = C.cast_keys["w_in"]
    shift, sc1p, gate, mkx = None, None, None, None
    tiles = [(512 * i, 512, "x") for i in range(8)] + [(T, CT, "c")]
    st = {"w": 0, "x": 0, "o": 0, "ps": 0}

    def modulate(ti):
        t0, nt, nm = tiles[ti]
        shift, sc1p, gate, mk = mod_aps(C, nm, 1)
        hT = W.hT[ti % 2]
        for dc in range(16):
            xs = W.xst[st["x"] % 2]
            xk = ("xst", st["x"] % 2)
            st["x"] += 1
            p.dma("pool", xs[:, :nt], C.x1T[dc * 128:(dc + 1) * 128, t0:t0 + nt],
                  reads=[("dst", "fa", ti, dc)], writes=[xk])
            if dc % 2 == 0:
                p.op("act", lambda e, xs=xs, hT=hT, dc=dc, nt=nt, sc1p=sc1p, shift=shift: e.activation(
                    out=hT[:, dc, :nt], in_=xs[:, :nt], func=AF.Identity,
                    scale=sc1p[:, dc:dc + 1], bias=shift[:, dc:dc + 1]),
                    reads=[xk] + mk, writes=[("hT", ti % 2, dc)])
            else:
                p.op("dve", lambda e, xs=xs, hT=hT, dc=dc, nt=nt, sc1p=sc1p, shift=shift: e.tensor_scalar(
                    out=hT[:, dc, :nt], in0=xs[:, :nt], scalar1=sc1p[:, dc:dc + 1], scalar2=shift[:, dc:dc + 1],
                    op0=ALU.mult, op1=ALU.add),
                    reads=[xk] + mk, writes=[("hT", ti % 2, dc)])

    def load_w(f0, nf):
        si = st["w"] % 4
        st["w"] += 1
        slot = W.ws[si]
        p.dma("sp", slot[:, :, :nf], w_v[:, :, f0:f0 + nf], reads=wkeys, writes=[("pws", si)])
        return slot, ("pws", si)

    def nextps():
        i = st["ps"] % 8
        st["ps"] += 1
        return C.ps[i], ("ps", i)

    def tile(ti):
        t0, nt, nm = tiles[ti]
        hT = W.hT[ti % 2]
        hk = [("hT", ti % 2, dc) for dc in range(16)]
        nfm = 34 if nm == "x" else 26
        fc = 0
        while fc < nfm:
            nchunk = min(4, nfm - fc) if fc != 24 else 2
            slot, sk = load_w(fc * 128, nchunk * 128)
            for jj in range(nchunk):
                ps, pk = nextps()
                for kc in range(16):
                    p.op("pe", lambda e, ps=ps, slot=slot, kc=kc, jj=jj, nt=nt: e.matmul(
                        ps[:, :nt], slot[:, kc, jj * 128:(jj + 1) * 128], hT[:, kc, :nt],
                        start=(kc == 0), stop=(kc == 15)),
                        reads=[sk] + (hk if kc == 0 else []), writes=[pk], sig=(kc == 15))
                f = fc + jj
                if f < 26:
                    oi = st["o"] % 3
                    st["o"] += 1
                    ot = W.ost[oi]
                    p.op("act", lambda e, ot=ot, ps=ps, nt=nt: e.activation(out=ot[:, :nt], in_=ps[:, :nt],
                                                                          func=AF.Copy),
                         reads=[pk], writes=[("post", oi)])
                    p.dma("pool", C.pT[f * 128:(f + 1) * 128, t0:t0 + nt], ot[:, :nt],
                          reads=[("post", oi)], writes=[("pT", ti, f)], group=("postd", oi))
                else:
                    p.op("act", lambda e, ps=ps, f=f, nt=nt: e.activation(
                        out=W.uT[:, f - 26, :nt], in_=ps[:, :nt], func=AF.Gelu_apprx_tanh),
                        reads=[pk], writes=[("uT", f - 26)])
            fc += nchunk
        if nm != "x":
            return
        vs = [load_w(4352 + 512 * h, 512) for h in range(2)]
        for h in range(2):
            slot, sk = vs[h]
            for tb in range(4):
                ps, pk = nextps()
                for kc in range(16):
                    p.op("pe", lambda e, ps=ps, slot=slot, kc=kc, tb=tb: e.matmul(
                        ps[:, :], hT[:, kc, tb * 128:(tb + 1) * 128], slot[:, kc, :],
                        start=(kc == 0), stop=(kc == 15)),
                        reads=[sk] + (hk if kc == 0 else []), writes=[pk], sig=(kc == 15))
                p.op("act", lambda e, ps=ps, tb=tb, h=h: e.activation(
                    out=W.vg[tb][:, h * 512:(h + 1) * 512], in_=ps[:, :], func=AF.Gelu_apprx_tanh),
                    reads=[pk], writes=[("vg", tb, h)])
        for tb in range(4):
            vg = W.vg[tb]
            vg3 = vg.rearrange("p (g d) -> p g d", d=64)
            s1, s2, mean, var, rstd, nm_ = W.st
            vk = [("vg", tb, 0), ("vg", tb, 1)]
            p.op("dve", lambda e, vg3=vg3: e.tensor_reduce(out=s1, in_=vg3, axis=mybir.AxisListType.X, op=ALU.add),
                 reads=vk, writes=["gs1"])
            p.op("act", lambda e, vg=vg: e.activation(out=W.sqv, in_=vg, func=AF.Square), reads=vk, writes=["sqv"])
            p.op("dve", lambda e: e.tensor_reduce(out=s2, in_=W.sqv.rearrange("p (g d) -> p g d", d=64),
                                                  axis=mybir.AxisListType.X, op=ALU.add),
                 reads=["sqv"], writes=["gs2"])
            p.op("dve", lambda e: e.tensor_scalar_mul(out=mean, in0=s1, scalar1=1.0 / 64), reads=["gs1"],
                 writes=["gmean"])
            p.op("dve", lambda e: e.tensor_tensor(out=var, in0=mean, in1=mean, op=ALU.mult), reads=["gmean"],
                 writes=["gvar"])
            p.op("dve", lambda e: e.scalar_tensor_tensor(out=var, in0=s2, scalar=1.0 / 64, in1=var, op0=ALU.mult,
                                                         op1=ALU.subtract),
                 reads=["gs2", "gvar"], writes=["gvar"])
            p.op("act", lambda e: e.activation(out=rstd, in_=var, func=AF.Ln, bias=C.epsg, scale=1.0),
                 reads=["gvar", "epsc2"], writes=["grstd"])
            p.op("act", lambda e: e.activation(out=rstd, in_=rstd, func=AF.Exp, scale=-0.5), reads=["grstd"],
                 writes=["grstd"])
            mb = mean.rearrange("p (g o) -> p g o", o=1).to_broadcast([128, 16, 64])
            rb = rstd.rearrange("p (g o) -> p g o", o=1).to_broadcast([128, 16, 64])
            p.op("pool", lambda e, vg3=vg3, mb=mb: e.tensor_tensor(out=vg3, in0=vg3, in1=mb, op=ALU.subtract),
                 reads=vk + ["gmean"], writes=vk)
            p.op("dve", lambda e, vg3=vg3, rb=rb: e.tensor_tensor(out=vg3, in0=vg3, in1=rb, op=ALU.mult),
                 reads=vk + ["grstd"], writes=vk)
            p.op("pool", lambda e, vg=vg: e.tensor_tensor(out=vg, in0=vg, in1=lng, op=ALU.mult),
                 reads=vk + ["lng"], writes=vk)
            vn = W.vn[tb % 2]
            p.op("dve", lambda e, vg=vg, vn=vn: e.tensor_tensor(out=vn, in0=vg, in1=lnb, op=ALU.add),
                 reads=vk + ["lnb"], writes=[("vn", tb % 2)])
            for g in range(16):
                if g % 4 == 0:
                    ps, pk = nextps()
                po = ps[:, (g % 4) * 128:(g % 4 + 1) * 128]
                p.op("pe", lambda e, po=po, vn=vn, g=g: e.matmul(
                    po, vn[:, (g // 2) * 128:(g // 2 + 1) * 128], wsT[:, g, :], start=True, stop=True),
                    reads=[("vn", tb % 2), "wsT"], writes=[pk])
                if g % 4 == 3:
                    for hh in range(2):
                        rows = slice(64 * hh, 64 * hh + 64)
                        for pair in range(2):
                            gg = g - 3 + 2 * pair + hh
                            tmp = W.tmp[hh]
                            pslice = ps[rows, (gg % 4) * 128:(gg % 4 + 1) * 128]
                            p.op("dve", lambda e, tmp=tmp, pslice=pslice, rows=rows, gg=gg: e.tensor_tensor(
                                out=tmp[rows, :], in0=pslice, in1=bsb[rows, gg, :], op=ALU.add),
                                reads=[pk, "bsb"], writes=[("gtmp", hh)])
                            p.op("pool", lambda e, tmp=tmp, rows=rows, gg=gg, tb=tb: e.tensor_tensor(
                                out=W.gmo[rows, gg // 2, tb * 128:(tb + 1) * 128], in0=tmp[rows, :],
                                in1=W.uT[rows, gg // 2, tb * 128:(tb + 1) * 128], op=ALU.mult),
                                reads=[("gtmp", hh), ("uT", gg // 2)], writes=[("gmo", gg // 2)])
        p.dma("pool", C.catT[1024:2048, t0:t0 + nt].rearrange("(cc p) t -> p cc t", p=128), W.gmo[:, :, :nt],
              reads=[("gmo", c) for c in range(8)], writes=[("catg", ti)], group="gmod")

    modulate(0)
    for ti in range(len(tiles)):
        if ti + 1 < len(tiles):
            modulate(ti + 1)
        tile(ti)


def bcast_rows(ap1d, n):
    return bass.AP(ap1d.tensor, ap1d.offset, [[0, 128], [1, n]])


def shift_phase(C):
    p, A = C.p, C.A
    X = [A.f32(640) for _ in range(3)]
    XS = [A.f32(512) for _ in range(3)]
    om, _ = VC["mu"]
    o4, _ = VC["sel4"]
    o2, _ = VC["sel2"]
    omm = A.f32(26)
    m4 = A.f32(26 * 4).rearrange("p (f j) -> p f j", j=4)
    m2 = A.f32(26 * 2).rearrange("p (f j) -> p f j", j=2)
    mu = C.vt[:, om:om + 26]
    p.op("dve", lambda e: e.tensor_scalar(out=omm, in0=mu, scalar1=-1.0, scalar2=1.0, op0=ALU.mult, op1=ALU.add),
         reads=["vecs"], writes=["omm"])
    for jj in range(4):
        p.op("dve", lambda e, jj=jj: e.tensor_scalar_mul(out=m4[:, :, jj], in0=mu, scalar1=C.vt[:, o4 + jj:o4 + jj + 1]),
             reads=["vecs"], writes=[("m4", jj)])
    for jj in range(2):
        p.op("dve", lambda e, jj=jj: e.tensor_scalar_mul(out=m2[:, :, jj], in0=mu, scalar1=C.vt[:, o2 + jj:o2 + jj + 1]),
             reads=["vecs"], writes=[("m2", jj)])
    mk = ["omm"] + [("m4", j) for j in range(4)] + [("m2", j) for j in range(2)]
    n = 0
    for blk in range(9):
        for fc in range(26):
            xi = n % 3
            n += 1
            x, xs = X[xi], XS[xi]
            xk, sk = ("shx", xi), ("shs", xi)
            if blk < 8:
                t0 = blk * 512
                lo = max(t0 - 64, 0)
                hi = min(t0 + 576, T)
                p.dma("sp", x[:, lo - (t0 - 64):hi - (t0 - 64)], C.pT[fc * 128:(fc + 1) * 128, lo:hi],
                      reads=[("pT", ti, fc) for ti in range(max(blk - 1, 0), min(blk + 2, 8))], writes=[xk])
                nt = 512
                cur = x[:, 64:576]
                p.op("act", lambda e, xs=xs, cur=cur, fc=fc: e.activation(out=xs, in_=cur, func=AF.Copy,
                                                                           scale=omm[:, fc:fc + 1]),
                     reads=[xk] + mk, writes=[sk])
                xs3 = xs.rearrange("p (r c) -> p r c", c=64)
                cur3 = cur.rearrange("p (r c) -> p r c", c=64)
                p.op("dve", lambda e, xs3=xs3, cur3=cur3, fc=fc: e.scalar_tensor_tensor(
                    out=xs3[:, :, 1:64], in0=cur3[:, :, 0:63], scalar=m4[:, fc, 0:1], in1=xs3[:, :, 1:64],
                    op0=ALU.mult, op1=ALU.add), reads=[xk, sk], writes=[sk])
                p.op("dve", lambda e, xs3=xs3, cur3=cur3, fc=fc: e.scalar_tensor_tensor(
                    out=xs3[:, :, 0:63], in0=cur3[:, :, 1:64], scalar=m4[:, fc, 1:2], in1=xs3[:, :, 0:63],
                    op0=ALU.mult, op1=ALU.add), reads=[xk, sk], writes=[sk])
                l0 = 64 if blk == 0 else 0
                p.op("dve", lambda e, xs=xs, x=x, fc=fc, l0=l0: e.scalar_tensor_tensor(
                    out=xs[:, l0:512], in0=x[:, l0:512], scalar=m4[:, fc, 2:3], in1=xs[:, l0:512],
                    op0=ALU.mult, op1=ALU.add), reads=[xk, sk], writes=[sk])
                h0 = 448 if blk == 7 else 512
                p.op("dve", lambda e, xs=xs, x=x, fc=fc, h0=h0: e.scalar_tensor_tensor(
                    out=xs[:, 0:h0], in0=x[:, 128:128 + h0], scalar=m4[:, fc, 3:4], in1=xs[:, 0:h0],
                    op0=ALU.mult, op1=ALU.add), reads=[xk, sk], writes=[sk])
            else:
                t0, nt = T, CT
                p.dma("sp", x[:, 0:CT], C.pT[fc * 128:(fc + 1) * 128, T:TT], reads=[("pT", 8, fc)], writes=[xk])
                cur = x[:, 0:CT]
                p.op("act", lambda e, xs=xs, cur=cur, fc=fc: e.activation(out=xs[:, 0:CT], in_=cur, func=AF.Copy,
                                                                           scale=omm[:, fc:fc + 1]),
                     reads=[xk] + mk, writes=[sk])
                p.op("dve", lambda e, xs=xs, x=x, fc=fc: e.scalar_tensor_tensor(
                    out=xs[:, 1:CT], in0=x[:, 0:CT - 1], scalar=m2[:, fc, 0:1], in1=xs[:, 1:CT],
                    op0=ALU.mult, op1=ALU.add), reads=[xk, sk], writes=[sk])
                p.op("dve", lambda e, xs=xs, x=x, fc=fc: e.scalar_tensor_tensor(
                    out=xs[:, 0:CT - 1], in0=x[:, 1:CT], scalar=m2[:, fc, 1:2], in1=xs[:, 0:CT - 1],
                    op0=ALU.mult, op1=ALU.add), reads=[xk, sk], writes=[sk])
            p.dma("pool", C.xsT[fc * 128:(fc + 1) * 128, t0:t0 + nt], xs[:, :nt], reads=[sk],
                  writes=[("xsT", blk, fc)], group=("shsd", xi))


def bc3(ap, n):
    return ap.rearrange("p (o m) -> p o m", o=1).to_broadcast([128, n, ap.shape[1]])


def mixer_consts(C):
    p, A, AR, AC = C.p, C.A, C.AR, C.AC
    M = Ctx()
    C.M = M
    cm = AC.f32(7 * 128).rearrange("p (a b) -> p a b", b=128)
    p.dma("sp", cm, C.cmat, writes=["cmat"])
    M.ident = AR.f32(128)
    p.op("act", lambda e: e.activation(out=M.ident, in_=cm[:, 0, :], func=AF.Copy), reads=["cmat"], writes=["ident"])
    M.ml, M.mu, M.mui, M.bones = cm[:, 1, :], cm[:, 2, :], cm[:, 3, :], cm[:, 4, :]
    M.rm01 = cm[:, 6, 0:1]
    M.segm = A.f32(512)
    p.op("dve", lambda e: e.tensor_copy(out=M.segm.rearrange("p (c t) -> p c t", t=64), in_=bc3(cm[:, 5, 0:64], 8)),
         reads=["cmat"], writes=["segm"])
    M.gup = A.bf16(1024)
    p.dma("pool", M.gup, C.g_up, writes=["gup"])
    M.par = {}
    for nm in ["k_k", "k_a", "r_k"]:
        o, w = VC[nm]
        t = A.f32(512)
        p.op("dve", lambda e, t=t, o=o: e.tensor_copy(
            out=t.rearrange("p (c t) -> p c t", t=64),
            in_=C.vt[:, o:o + 8].rearrange("p (c o) -> p c o", o=1).to_broadcast([128, 8, 64])),
            reads=["vecs"], writes=[("par", nm)])
        M.par[nm] = t
    t = A.f32(512)
    p.op("dve", lambda e, t=t: e.tensor_scalar(out=t, in0=M.par["k_a"], scalar1=-1.0, scalar2=1.0, op0=ALU.mult,
                                               op1=ALU.add), reads=[("par", "k_a")], writes=[("par", "omka")])
    M.par["omka"] = t
    M.eps30 = AC.f32(2)[:, 0:1]
    p.op("pool", lambda e: e.memset(M.eps30, 1e-30), writes=["eps30"])
    M.epsgn = AC.f32(2)[:, 0:1]
    p.op("pool", lambda e: e.memset(M.epsgn, GN_EPS), writes=["epsgn"])


def mixer_pass(C, d):
    p, A, AR, M = C.p, C.A, C.AR, C.M
    rev = (d == 1)
    c0 = float(np.exp(-0.5))
    W = Ctx()
    lw = A.bf16(2 * 1024).rearrange("p (w f) -> p w f", w=2)
    p.dma("pool", lw, C.lora[:, d], writes=["lw"])
    par = dict(M.par)
    for nm in ["w0", "a0"]:
        o, w = VC[nm]
        t = A.f32(512)
        p.op("dve", lambda e, t=t, o=o: e.tensor_copy(
            out=t.rearrange("p (c t) -> p c t", t=64),
            in_=C.vt[:, o + 8 * d:o + 8 * d + 8].rearrange("p (c o) -> p c o", o=1).to_broadcast([128, 8, 64])),
            reads=["vecs"], writes=[("par", nm)])
        par[nm] = t
    f3 = lambda n=512: A.f32(n)
    big = lambda dt=None: (AR if dt is F32R else A).f32(1024).rearrange("p (c m) -> p c m", m=128)
    v3 = lambda ap: ap.rearrange("p (c t) -> p c t", t=64)
    EX = [dict(Ax=big(F32R), Rx=big(F32R), b2=big(F32R), k2=big(F32R), v2=big(F32R), gam=A.f32(8))
          for _ in range(3)]
    PR = [dict(AakT=big(F32R), RtF=big(F32R)) for _ in range(2)]
    RbT, RkT = big(F32R), big(F32R)
    Bstk, Kstk, Vstk = big(F32R), big(F32R), big(F32R)
    S = [big(F32R), big(F32R)]
    St = [big(F32R), big(F32R)]
    RtT = big(F32R)
    Pt, Ut = big(F32R), big(F32R)
    H = [big(F32R), big(F32R)]
    yT = f3()
    ld = dict(r=f3(), k=f3(), v=f3(), x24=A.f32(64), x25=A.f32(64))
    lo24 = A.bf16(64)
    sg25 = A.bf16(64)
    tmp = {n: f3() for n in ["s", "ic", "kk", "nkk", "kd", "bb", "rk", "sR", "cs", "csm", "Er", "Eb", "t"]}
    gst = f3()
    for h in H:
        p.op("act", lambda e, h=h: e.activation(
            out=h, in_=M.bones.rearrange("p (o m) -> p o m", o=1).to_broadcast([128, 8, 128]), func=AF.Copy,
            scale=0.0), reads=["cmat"], writes=[("H", 0 if h is H[0] else 1)])
    chunks = [(T + 64 * i, True) for i in range(4)] + [(64 * i, False) for i in range(64)]
    if rev:
        chunks = [(T + 64 * i, True) for i in reversed(range(4))] + [(64 * i, False) for i in reversed(range(64))]
    NCH = len(chunks)
    st = {"ps": 0}
    bm3 = M.bones.rearrange("p (h t) -> p h t", t=64)
    bm4 = M.bones.rearrange("p (o h t) -> p o h t", o=1, t=64).to_broadcast([128, 8, 2, 64])

    def R3(ap):
        a = v3(ap)
        return a[:, :, ::-1] if rev else a

    def nextps():
        i = st["ps"] % 8
        st["ps"] += 1
        return C.ps[i], ("ps", i)

    def mm8(lhs_fn, rhs_fn, reads, n=128, extra=None):
        terms = [(lhs_fn, rhs_fn)] + (extra or [])
        out = []
        for g in range(2):
            ps, pk = nextps()
            for c4 in range(4):
                cc = 4 * g + c4
                for ti, (lf, rf) in enumerate(terms):
                    p.op("pe", lambda e, ps=ps, c4=c4, cc=cc, lf=lf, rf=rf, ti=ti: e.matmul(
                        ps[:, c4 * n:(c4 + 1) * n], lf(cc), rf(cc), start=(ti == 0), stop=(ti == len(terms) - 1)),
                        reads=reads, writes=[pk], sig=(ti == len(terms) - 1))
            out.append((ps, pk))
        return out

    def grp(t, g):
        return t[:, 4 * g:4 * g + 4, :]

    def psv(ps, n=128):
        return ps[:, 0:4 * n].rearrange("p (c m) -> p c m", m=n)

    def prep(ci):
        col, is_ctx = chunks[ci]
        ex = EX[ci % 3]
        ek = ("ex", ci % 3)
        blk = (col // 512) if not is_ctx else 8
        xkeys = lambda f0: [("xsT", blk, f0 + c) for c in range(8)]
        for nm, r0 in (("r", 0), ("k", 1024), ("v", 2048)):
            p.dma("sp", v3(ld[nm]), C.xsT[r0:r0 + 1024, col:col + 64].rearrange("(c p) t -> p c t", p=128),
                  reads=xkeys(r0 // 128), writes=[("ld", nm)])
        p.dma("sp", ld["x24"], C.xsT[3072:3200, col:col + 64], reads=[("xsT", blk, 24)], writes=[("ld", "x24")])
        p.op("act", lambda e: e.activation(out=lo24[0:64, :], in_=ld["x24"][0:64, :], func=AF.Tanh),
             reads=[("ld", "x24")], writes=["lo24a"])
        p.op("dve", lambda e: e.tensor_copy(out=lo24[64:128, :], in_=ld["x24"][64:128, :]),
             reads=[("ld", "x24")], writes=["lo24b"])
        yield
        for nm, wi, pn in (("s", 0, "w0"), ("ic", 1, "a0")):
            pss = mm8(lambda cc, wi=wi: lw[:, wi, cc * 128:(cc + 1) * 128], lambda cc: lo24[:, :],
                      ["lw", "lo24a", "lo24b"], n=64)
            for g, (ps, pk) in enumerate(pss):
                dst = tmp[nm][:, 256 * g:256 * g + 256]
                p.op("dve", lambda e, ps=ps, dst=dst, g=g, pn=pn: e.tensor_tensor(
                    out=dst, in0=ps[:, 0:256], in1=par[pn][:, 256 * g:256 * g + 256], op=ALU.add),
                    reads=[pk, ("par", pn)], writes=[("tmp", nm, g)])
            p.op("act", lambda e, nm=nm: e.activation(out=tmp[nm], in_=tmp[nm], func=AF.Sigmoid),
                 reads=[("tmp", nm, 0), ("tmp", nm, 1)], writes=[("tmp", nm)])
        yield
        p.op("dve", lambda e: e.tensor_tensor(out=tmp["kk"], in0=ld["k"], in1=par["k_k"], op=ALU.mult),
             reads=[("ld", "k"), ("par", "k_k")], writes=[("tmp", "kk")])
        p.op("act", lambda e: e.activation(out=tmp["t"], in_=tmp["kk"], func=AF.Square),
             reads=[("tmp", "kk")], writes=[("tmp", "t")])
        pss = mm8(lambda cc: M.bones, lambda cc: tmp["t"][:, cc * 64:(cc + 1) * 64], ["cmat", ("tmp", "t")], n=64)
        for g, (ps, pk) in enumerate(pss):
            dst = tmp["nkk"][:, 256 * g:256 * g + 256]
            p.op("act", lambda e, ps=ps, dst=dst: e.activation(out=dst, in_=ps[:, 0:256], func=AF.Ln, bias=M.eps30,
                                                              scale=1.0),
                 reads=[pk, "eps30"], writes=[("tmp", "nkk", g)])
        p.op("act", lambda e: e.activation(out=tmp["nkk"], in_=tmp["nkk"], func=AF.Exp, scale=-0.5),
             reads=[("tmp", "nkk", 0), ("tmp", "nkk", 1)], writes=[("tmp", "nkk")])
        p.op("dve", lambda e: e.tensor_tensor(out=tmp["kk"], in0=tmp["kk"], in1=tmp["nkk"], op=ALU.mult),
             reads=[("tmp", "kk"), ("tmp", "nkk")], writes=[("tmp", "kk")])
        p.op("act", lambda e: e.mul(out=tmp["nkk"], in_=tmp["kk"], mul=-1.0),
             reads=[("tmp", "kk")], writes=[("tmp", "nkk")])
        yield
        p.op("dve", lambda e: e.tensor_tensor(out=tmp["kd"], in0=tmp["ic"], in1=par["k_a"], op=ALU.mult),
             reads=[("tmp", "ic"), ("par", "k_a")], writes=[("tmp", "kd")])
        p.op("pool", lambda e: e.tensor_tensor(out=tmp["kd"], in0=tmp["kd"], in1=par["omka"], op=ALU.add),
             reads=[("tmp", "kd"), ("par", "omka")], writes=[("tmp", "kd")])
        p.op("pool", lambda e: e.tensor_tensor(out=tmp["kd"], in0=tmp["kd"], in1=ld["k"], op=ALU.mult),
             reads=[("tmp", "kd"), ("ld", "k")], writes=[("tmp", "kd")])
        p.op("pool", lambda e: e.tensor_tensor(out=tmp["bb"], in0=tmp["kk"], in1=tmp["ic"], op=ALU.mult),
             reads=[("tmp", "kk"), ("tmp", "ic")], writes=[("tmp", "bb")])
        yield
        if not is_ctx:
            p.op("pool", lambda e: e.tensor_tensor(out=tmp["rk"], in0=ld["r"], in1=par["r_k"], op=ALU.mult),
                 reads=[("ld", "r"), ("par", "r_k")], writes=[("tmp", "rk")])
            p.op("pool", lambda e: e.tensor_tensor(out=tmp["rk"], in0=tmp["rk"], in1=tmp["kd"], op=ALU.mult),
                 reads=[("tmp", "rk"), ("tmp", "kd")], writes=[("tmp", "rk")])
            pss = mm8(lambda cc: M.bones, lambda cc: tmp["rk"][:, cc * 64:(cc + 1) * 64], ["cmat", ("tmp", "rk")],
                      n=64)
            for g, (ps, pk) in enumerate(pss):
                p.op("dve", lambda e, ps=ps, g=g: e.tensor_tensor(
                    out=gst[:, 256 * g:256 * g + 256], in0=ps[:, 0:256], in1=ld["v"][:, 256 * g:256 * g + 256],
                    op=ALU.mult), reads=[pk, ("ld", "v")], writes=[("gst", g)])
            p.dma("pool", C.bonT[d][:, col:col + 64].rearrange("(c p) t -> p c t", p=128), v3(gst),
                  reads=[("gst", 0), ("gst", 1)], writes=[("bonT", d, ci)], group="gstd")
            if d == 0:
                p.dma("sp", ld["x25"], C.xsT[3200:3328, col:col + 64], reads=[("xsT", blk, 25)],
                      writes=[("ld", "x25")])
                p.op("act", lambda e: e.activation(out=sg25, in_=ld["x25"], func=AF.Sigmoid),
                     reads=[("ld", "x25")], writes=["sg25"])
                pss = mm8(lambda cc: M.gup[:, cc * 128:(cc + 1) * 128], lambda cc: sg25[:, :], ["gup", "sg25"], n=64)
                for g, (ps, pk) in enumerate(pss):
                    p.op("act", lambda e, ps=ps, g=g: e.activation(out=gst[:, 256 * g:256 * g + 256],
                                                                   in_=ps[:, 0:256], func=AF.Copy),
                         reads=[pk], writes=[("gst", g)])
                p.dma("pool", C.gTs[:, col:col + 64].rearrange("(c p) t -> p c t", p=128), v3(gst),
                      reads=[("gst", 0), ("gst", 1)], writes=[("gTs", ci)], group="gstd")
            yield
        if rev:
            p.op("pool", lambda e: e.tensor_copy(out=v3(tmp["sR"]), in_=R3(tmp["s"])), reads=[("tmp", "s")],
                 writes=[("tmp", "sR")])
            sR, sRk = tmp["sR"], ("tmp", "sR")
        else:
            sR, sRk = tmp["s"], ("tmp", "s")
        p.op("dve", lambda e: e.tensor_tensor_scan(out=tmp["cs"], data0=M.segm, data1=sR, initial=0.0,
                                                   op0=ALU.mult, op1=ALU.add),
             reads=[sRk, "segm"], writes=[("tmp", "cs")])
        p.op("act", lambda e: e.activation(out=tmp["Er"], in_=tmp["cs"], func=AF.Exp, scale=-c0),
             reads=[("tmp", "cs")], writes=[("tmp", "Er")])
        p.op("act", lambda e: e.activation(out=tmp["Eb"], in_=tmp["cs"], func=AF.Exp, scale=c0),
             reads=[("tmp", "cs")], writes=[("tmp", "Eb")])
        p.op("pool", lambda e: e.tensor_tensor(out=tmp["csm"], in0=tmp["cs"], in1=sR, op=ALU.subtract),
             reads=[("tmp", "cs"), sRk], writes=[("tmp", "csm")])
        p.op("act", lambda e: e.activation(out=tmp["csm"], in_=tmp["csm"], func=AF.Exp, scale=-c0),
             reads=[("tmp", "csm")], writes=[("tmp", "csm")])
        yield
        e4 = lambda t: t.rearrange("p c (h t) -> p c h t", t=64)
        b4 = lambda ap: v3(ap).rearrange("p c (o t) -> p c o t", o=1).to_broadcast([128, 8, 2, 64])
        p.op("pool", lambda e: e.tensor_tensor(out=v3(tmp["t"]), in0=R3(tmp["nkk"]), in1=v3(tmp["csm"]), op=ALU.mult),
             reads=[("tmp", "nkk"), ("tmp", "csm")], writes=[("tmp", "t")])
        p.op("pool", lambda e: e.tensor_tensor(out=e4(ex["Ax"]), in0=b4(tmp["t"]), in1=bm4, op=ALU.mult),
             reads=[("tmp", "t"), "cmat"], writes=[ek + ("Ax",)])
        b4r = lambda ap: R3(ap).rearrange("p c (o t) -> p c o t", o=1).to_broadcast([128, 8, 2, 64])
        p.op("dve", lambda e: e.tensor_tensor(out=e4(ex["b2"]), in0=b4r(tmp["bb"]), in1=b4(tmp["Eb"]), op=ALU.mult),
             reads=[("tmp", "bb"), ("tmp", "Eb")], writes=[ek + ("b",)])
        p.op("pool", lambda e: e.tensor_tensor(out=e4(ex["k2"]), in0=b4r(tmp["kd"]), in1=b4(tmp["Eb"]), op=ALU.mult),
             reads=[("tmp", "kd"), ("tmp", "Eb")], writes=[ek + ("k",)])
        p.op("dve", lambda e: e.tensor_copy(out=e4(ex["v2"]), in_=b4r(ld["v"])), reads=[("ld", "v")],
             writes=[ek + ("v",)])
        p.op("pool", lambda e: e.tensor_copy(out=ex["gam"], in_=v3(tmp["Er"])[:, :, 63]),
             reads=[("tmp", "Er")], writes=[ek + ("gam",)])
        if not is_ctx:
            p.op("pool", lambda e: e.tensor_tensor(out=v3(tmp["t"]), in0=R3(ld["r"]), in1=v3(tmp["Er"]), op=ALU.mult),
                 reads=[("ld", "r"), ("tmp", "Er")], writes=[("tmp", "t")])
            p.op("pool", lambda e: e.tensor_tensor(out=e4(ex["Rx"]), in0=b4(tmp["t"]), in1=bm4, op=ALU.mult),
                 reads=[("tmp", "t"), "cmat"], writes=[ek + ("Rx",)])
        yield

    def stage_ab(ci):
        col, is_ctx = chunks[ci]
        ex = EX[ci % 3]
        ek = ("ex", ci % 3)
        pr = PR[ci % 2]
        qk = ("pr", ci % 2)
        Ax = lambda cc: ex["Ax"][:, cc, :]
        Bb = lambda cc: ex["b2"][:, cc, :]
        Kb = lambda cc: ex["k2"][:, cc, :]

        def masked(pss, dst, mask, dk):
            for g, (ps, pk) in enumerate(pss):
                p.op("dve", lambda e, ps=ps, g=g: e.tensor_tensor(
                    out=grp(dst, g), in0=psv(ps), in1=mask.rearrange("p (o m) -> p o m", o=1).to_broadcast(
                        [128, 4, 128]), op=ALU.mult), reads=[pk, "cmat"], writes=[dk + (g,)])

        masked(mm8(Ax, Bb, [ek + ("Ax",), ek + ("b",)]), S[0], M.ml, ("S", 0))
        yield
        masked(mm8(Bb, Ax, [ek + ("Ax",), ek + ("b",)]), St[0], M.mu, ("St", 0))
        yield
        masked(mm8(Kb, Ax, [ek + ("Ax",), ek + ("k",)]), pr["AakT"], M.mu, qk + ("AakT",))
        rts = [RtT, pr["RtF"]]
        rtk = [("RtT",), qk + ("RtF",)]
        for g in range(2):
            p.op("pool", lambda e, g=g: e.tensor_tensor(
                out=grp(rts[0], g), in0=grp(St[0], g).bitcast(F32),
                in1=M.ident.bitcast(F32).rearrange("p (o m) -> p o m", o=1).to_broadcast([128, 4, 128]), op=ALU.add),
                reads=[("St", 0, g), "ident"], writes=[rtk[0] + (g,)])
        yield
        for j in range(1, 6):
            a, b = (j - 1) % 2, j % 2
            sk_prev = [("S", a, 0), ("S", a, 1), ("St", a, 0), ("St", a, 1)]
            pss = mm8(lambda cc, a=a: St[a][:, cc, :], lambda cc, a=a: S[a][:, cc, :], sk_prev)
            for g, (ps, pk) in enumerate(pss):
                p.op("act", lambda e, ps=ps, g=g, b=b: e.activation(out=grp(S[b], g), in_=psv(ps), func=AF.Copy),
                     reads=[pk], writes=[("S", b, g)])
            yield
            if j < 5:
                pss = mm8(lambda cc, a=a: S[a][:, cc, :], lambda cc, a=a: St[a][:, cc, :], sk_prev)
                for g, (ps, pk) in enumerate(pss):
                    p.op("act", lambda e, ps=ps, g=g, b=b: e.activation(out=grp(St[b], g), in_=psv(ps), func=AF.Copy),
                         reads=[pk], writes=[("St", b, g)])
                yield
            src, dst = rts[(j - 1) % 2], rts[j % 2]
            srck, dstk = rtk[(j - 1) % 2], rtk[j % 2]
            pss = mm8(lambda cc, b=b: S[b][:, cc, :], lambda cc, src=src: src[:, cc, :],
                      [("S", b, 0), ("S", b, 1), srck + (0,), srck + (1,)])
            for g, (ps, pk) in enumerate(pss):
                p.op("dve", lambda e, ps=ps, g=g, src=src, dst=dst: e.tensor_tensor(
                    out=grp(dst, g), in0=psv(ps), in1=grp(src, g).bitcast(F32), op=ALU.add),
                    reads=[pk, srck + (g,)], writes=[dstk + (g,)])
            yield

    def stage_c(ci):
        col, is_ctx = chunks[ci]
        ex = EX[ci % 3]
        ek = ("ex", ci % 3)
        pr = PR[ci % 2]
        qk = ("pr", ci % 2)
        Hc, Hn = H[ci % 2], H[(ci + 1) % 2]
        hk, hnk = ("H", ci % 2), ("H", (ci + 1) % 2)
        ident = lambda cc: M.ident
        two = lambda k: [k + (0,), k + (1,)]
        bonesb = M.bones.rearrange("p (o m) -> p o m", o=1).to_broadcast([128, 4, 128])

        def masked_ev(pss, dst, dk, mask=None):
            for g, (ps, pk) in enumerate(pss):
                p.op("dve", lambda e, ps=ps, g=g: e.tensor_tensor(
                    out=grp(dst, g), in0=psv(ps),
                    in1=(bonesb if mask is None else mask.rearrange("p (o m) -> p o m", o=1).to_broadcast(
                        [128, 4, 128])), op=ALU.mult), reads=[pk, "cmat"], writes=[dk + (g,)])

        masked_ev(mm8(lambda cc: ex["v2"][:, cc, :], ident, [ek + ("v",), "ident"]), Vstk, ("Vstk",))
        yield
        pss = mm8(lambda cc: ex["Ax"][:, cc, :], lambda cc: Hc[:, cc, :],
                  [ek + ("Ax",), hk] + two(qk + ("AakT",)) + two(("Vstk",)),
                  extra=[(lambda cc: pr["AakT"][:, cc, :], lambda cc: Vstk[:, cc, :])])
        for g, (ps, pk) in enumerate(pss):
            p.op("act", lambda e, ps=ps, g=g: e.activation(out=grp(Pt, g), in_=psv(ps), func=AF.Copy),
                 reads=[pk], writes=[("Pt", g)])
        yield
        masked_ev(mm8(lambda cc: ex["b2"][:, cc, :], ident, [ek + ("b",), "ident"]), Bstk, ("Bstk",))
        masked_ev(mm8(lambda cc: ex["k2"][:, cc, :], ident, [ek + ("k",), "ident"]), Kstk, ("Kstk",))
        if not is_ctx:
            Rx = lambda cc: ex["Rx"][:, cc, :]
            masked_ev(mm8(lambda cc: ex["b2"][:, cc, :], Rx, [ek + ("Rx",), ek + ("b",)]), RbT, ("RbT",), M.mui)
            masked_ev(mm8(lambda cc: ex["k2"][:, cc, :], Rx, [ek + ("Rx",), ek + ("k",)]), RkT, ("RkT",), M.mui)
        yield
        pss = mm8(lambda cc: pr["RtF"][:, cc, :], lambda cc: Pt[:, cc, :], two(qk + ("RtF",)) + two(("Pt",)))
        for g, (ps, pk) in enumerate(pss):
            p.op("dve", lambda e, ps=ps, g=g: e.tensor_copy(out=grp(Ut, g), in_=psv(ps)), reads=[pk],
                 writes=[("Ut", g)])
        yield
        pss = mm8(ident, lambda cc: Hc[:, cc, :], ["ident", hk] + two(("Bstk",)) + two(("Kstk",)) + two(("Ut",)) +
                  two(("Vstk",)),
                  extra=[(lambda cc: Bstk[:, cc, :], lambda cc: Ut[:, cc, :]),
                         (lambda cc: Kstk[:, cc, :], lambda cc: Vstk[:, cc, :])])
        for g, (ps, pk) in enumerate(pss):
            p.op("dve", lambda e, ps=ps, g=g: e.tensor_tensor(
                out=grp(Hn, g), in0=psv(ps),
                in1=ex["gam"][:, 4 * g:4 * g + 4].rearrange("p (c o) -> p c o", o=1).to_broadcast([128, 4, 128]),
                op=ALU.mult), reads=[pk, ek + ("gam",)], writes=[hnk])
        yield
        if not is_ctx:
            pss = mm8(lambda cc: Hc[:, cc, :], lambda cc: ex["Rx"][:, cc, :],
                      [hk, ek + ("Rx",)] + two(("Ut",)) + two(("RbT",)) + two(("RkT",)) + two(("Vstk",)),
                      extra=[(lambda cc: Ut[:, cc, :], lambda cc: RbT[:, cc, :]),
                             (lambda cc: Vstk[:, cc, :], lambda cc: RkT[:, cc, :])])
            y3 = v3(yT)
            for g, (ps, pk) in enumerate(pss):
                for hh in range(2):
                    rows = slice(64 * hh, 64 * hh + 64)
                    o = y3[rows, 4 * g:4 * g + 4, :]
                    if rev:
                        o = o[:, :, ::-1]
                    src = psv(ps)[rows, :, 64 * hh:64 * hh + 64]
                    if hh == 0:
                        p.op("dve", lambda e, o=o, src=src: e.tensor_copy(out=o, in_=src), reads=[pk],
                             writes=[("yT", g, hh)])
                    else:
                        p.op("act", lambda e, o=o, src=src: e.activation(out=o, in_=src, func=AF.Copy), reads=[pk],
                             writes=[("yT", g, hh)])
            p.dma("pool", C.yTs[d][:, col:col + 64].rearrange("(c p) t -> p c t", p=128), y3,
                  reads=[("yT", g, hh) for g in range(2) for hh in range(2)], writes=[("yTs", d, ci)], group="yTd")
        yield

    def drain(*gens):
        gens = [g for g in gens if g is not None]
        while gens:
            for g in list(gens):
                try:
                    next(g)
                except StopIteration:
                    gens.remove(g)

    drain(prep(0))
    drain(stage_ab(0), prep(1))
    for ci in range(NCH):
        drain(stage_c(ci),
              stage_ab(ci + 1) if ci + 1 < NCH else None,
              prep(ci + 2) if ci + 2 < NCH else None)


def rwkv_out_phase(C):
    p, A, M = C.p, C.A, C.M
    L = [{n: A.f32(512) for n in ["y0", "y1", "b0", "b1", "g"]} for _ in range(2)]
    sq = A.f32(512)
    mean, var, rstd = A.f32(512), A.f32(512), A.f32(512)
    ob = [A.bf16(512) for _ in range(2)]
    ogg, _ = VC["gn_g"]
    ogb, _ = VC["gn_b"]
    n = 0
    for ti in range(8):
        t0 = ti * 512
        cks = list(range(8 * ti + 4, 8 * ti + 12)) if False else None
        for cc in range(8):
            l = L[n % 2]
            lk = lambda nm, n=n: ("rl", n % 2, nm)
            rows = slice(cc * 128, (cc + 1) * 128)
            for nm, src, dep in (("y0", C.yTs[0], "yTs0"), ("y1", C.yTs[1], "yTs1"), ("b0", C.bonT[0], "bon0"),
                                 ("b1", C.bonT[1], "bon1"), ("g", C.gTs, "gTs")):
                p.dma("sp", l[nm], src[rows, t0:t0 + 512], reads=["mixdone"], writes=[lk(nm)])
            p.op("pool", lambda e, l=l: e.tensor_tensor(out=l["y0"], in0=l["y0"], in1=l["y1"], op=ALU.add),
                 reads=[lk("y0"), lk("y1")], writes=[lk("y0")])
            p.op("pool", lambda e, l=l: e.tensor_tensor(out=l["b0"], in0=l["b0"], in1=l["b1"], op=ALU.add),
                 reads=[lk("b0"), lk("b1")], writes=[lk("b0")])
            p.op("act", lambda e, l=l: e.activation(out=sq, in_=l["y0"], func=AF.Square), reads=[lk("y0")],
                 writes=["rsq"])
            p.op("pe", lambda e, l=l: e.matmul(C.ps[0][:, :], M.bones, l["y0"], start=True, stop=True),
                 reads=["cmat", lk("y0")], writes=[("ps", 0)])
            p.op("pe", lambda e: e.matmul(C.ps[1][:, :], M.bones, sq, start=True, stop=True),
                 reads=["cmat", "rsq"], writes=[("ps", 1)])
            inv = 1.0 / 64
            p.op("act", lambda e: e.activation(out=mean, in_=C.ps[0][:, :], func=AF.Copy, scale=inv),
                 reads=[("ps", 0)], writes=["rmean"])
            p.op("act", lambda e: e.activation(out=var, in_=C.ps[0][:, :], func=AF.Square, scale=inv),
                 reads=[("ps", 0)], writes=["rvar"])
            p.op("dve", lambda e: e.scalar_tensor_tensor(out=var, in0=C.ps[1][:, :], scalar=inv, in1=var,
                                                         op0=ALU.mult, op1=ALU.subtract),
                 reads=[("ps", 1), "rvar"], writes=["rvar"])
            p.op("act", lambda e: e.activation(out=rstd, in_=var, func=AF.Ln, bias=M.epsgn, scale=1.0),
                 reads=["rvar", "epsgn"], writes=["rrstd"])
            p.op("act", lambda e: e.activation(out=rstd, in_=rstd, func=AF.Exp, scale=-0.5), reads=["rrstd"],
                 writes=["rrstd"])
            p.op("pool", lambda e, l=l: e.tensor_tensor(out=l["y0"], in0=l["y0"], in1=mean, op=ALU.subtract),
                 reads=[lk("y0"), "rmean"], writes=[lk("y0")])
            p.op("dve", lambda e, l=l: e.tensor_tensor(out=l["y0"], in0=l["y0"], in1=rstd, op=ALU.mult),
                 reads=[lk("y0"), "rrstd"], writes=[lk("y0")])
            p.op("act", lambda e, l=l, cc=cc: e.activation(out=l["y0"], in_=l["y0"], func=AF.Identity,
                                                           scale=C.vt[:, ogg + cc:ogg + cc + 1],
                                                           bias=C.vt[:, ogb + cc:ogb + cc + 1]),
                 reads=[lk("y0"), "vecs"], writes=[lk("y0")])
            p.op("pool", lambda e, l=l: e.tensor_tensor(out=l["y0"], in0=l["y0"], in1=l["b0"], op=ALU.add),
                 reads=[lk("y0"), lk("b0")], writes=[lk("y0")])
            o = ob[n % 2]
            p.op("dve", lambda e, l=l, o=o: e.tensor_tensor(out=o, in0=l["y0"], in1=l["g"], op=ALU.mult),
                 reads=[lk("y0"), lk("g")], writes=[("rob", n % 2)])
            p.dma("pool", C.catT[rows, t0:t0 + 512], o, reads=[("rob", n % 2)], writes=[("catr", ti, cc)],
                  group=("robd", n % 2))
            n += 1


def wout_phase(C):
    p, A = C.p, C.A
    wo = r3(A.bf16(16 * 2048), b=2048)
    cat = [r3(A.bf16(16 * 512), b=512) for _ in range(2)]
    z = r3(A.f32(16 * 512), b=512)
    sq = [A.f32(512) for _ in range(2)]
    mean, var, rstd = A.f32(512), A.f32(512), A.f32(512)
    ost = [A.f32(512) for _ in range(2)]
    wv = C.w_out_b.rearrange("(kc p) f -> p kc f", p=128)
    for h in range(4):
        p.dma("sp", wo[:, 4 * h:4 * h + 4, :], wv[:, 4 * h:4 * h + 4, :], reads=C.cast_keys["w_out"],
              writes=[("wo", h)])
    wok = [("wo", h) for h in range(4)]
    shift, sc1p, gate, mk = mod_aps(C, "x", 1)
    og, _ = VC["ln_g"]
    ol, _ = VC["ln_b"]
    lnj = 1
    nt = 512
    oidx = 0
    for ti in range(8):
        t0 = ti * 512
        ct = cat[ti % 2]
        p.dma("sp", ct, C.catT[:, t0:t0 + 512].rearrange("(kc p) t -> p kc t", p=128),
              reads=[("catg", ti)] + [("catr", ti, cc) for cc in range(8)], writes=[("cat", ti % 2)])
        pend = None

        def stats(oc):
            p.op("pe", lambda e: e.matmul(C.ps[6][:, :], C.ones, z[:, oc, :], start=(oc == 0), stop=(oc == 15)),
                 reads=["ones", ("z", oc)], writes=[("ps", 6)], sig=(oc == 15))
            p.op("pe", lambda e: e.matmul(C.ps[7][:, :], C.ones, sq[oc % 2], start=(oc == 0), stop=(oc == 15)),
                 reads=["ones", ("sq", oc % 2)], writes=[("ps", 7)], sig=True)

        for oc in range(16):
            p.dma("pool", z[:, oc, :], C.x1T[oc * 128:(oc + 1) * 128, t0:t0 + 512],
                  reads=[("dst", "fa", ti, oc)], writes=[("z", oc)])
            ps = C.ps[4 + oc % 2]
            for kc in range(16):
                p.op("pe", lambda e, ps=ps, kc=kc, oc=oc, ct=ct: e.matmul(
                    ps[:, :], wo[:, kc, oc * 128:(oc + 1) * 128], ct[:, kc, :], start=(kc == 0), stop=(kc == 15)),
                    reads=(wok + [("cat", ti % 2)]) if kc == 0 else [], writes=[("ps", 4 + oc % 2)], sig=(kc == 15))
            if pend is not None:
                stats(pend)
            p.op("dve", lambda e, ps=ps, oc=oc: e.scalar_tensor_tensor(
                out=z[:, oc, :], in0=ps[:, :], scalar=gate[:, oc:oc + 1], in1=z[:, oc, :],
                op0=ALU.mult, op1=ALU.add), reads=[("ps", 4 + oc % 2)] + mk, writes=[("z", oc)])
            p.op("act", lambda e, oc=oc: e.activation(out=sq[oc % 2], in_=z[:, oc, :], func=AF.Square),
                 reads=[("z", oc)], writes=[("sq", oc % 2)])
            pend = oc
        stats(pend)
        inv = 1.0 / D
        p.op("act", lambda e: e.activation(out=mean, in_=C.ps[6][:, :], func=AF.Copy, scale=inv),
             reads=[("ps", 6)], writes=["mean"])
        p.op("act", lambda e: e.activation(out=var, in_=C.ps[6][:, :], func=AF.Square, scale=inv),
             reads=[("ps", 6)], writes=["var"])
        p.op("dve", lambda e: e.scalar_tensor_tensor(out=var, in0=C.ps[7][:, :], scalar=inv, in1=var,
                                                     op0=ALU.mult, op1=ALU.subtract),
             reads=[("ps", 7), "var"], writes=["var"])
        p.op("act", lambda e: e.activation(out=rstd, in_=var, func=AF.Ln, bias=C.epsln, scale=1.0),
             reads=["var", "epsc"], writes=["rstd"])
        p.op("act", lambda e: e.activation(out=rstd, in_=rstd, func=AF.Exp, scale=-0.5), reads=["rstd"],
             writes=["rstd"])
        for oc in range(16):
            p.op("pool", lambda e, oc=oc: e.tensor_tensor(out=z[:, oc, :], in0=z[:, oc, :], in1=mean, op=ALU.subtract),
                 reads=[("z", oc), "mean"], writes=[("z", oc)])
            p.op("dve", lambda e, oc=oc: e.tensor_tensor(out=z[:, oc, :], in0=z[:, oc, :], in1=rstd, op=ALU.mult),
                 reads=[("z", oc), "rstd"], writes=[("z", oc)])
            oi = oidx % 2
            oidx += 1
            ot = ost[oi]
            gcol = C.vt[:, og + lnj * 16 + oc:og + lnj * 16 + oc + 1]
            bcol = C.vt[:, ol + lnj * 16 + oc:ol + lnj * 16 + oc + 1]
            p.op("act", lambda e, oc=oc, ot=ot, gcol=gcol, bcol=bcol: e.activation(
                out=ot, in_=z[:, oc, :], func=AF.Identity, scale=gcol, bias=bcol),
                reads=[("z", oc), "vecs"], writes=[("ost", oi)])
            p.dma("pool", C.x2T[oc * 128:(oc + 1) * 128, t0:t0 + 512], ot,
                  reads=[("ost", oi)], writes=[("dst", "w3", ti, oc)], group=("ostd", oi))


_NC_CACHE = {}


def _pm(v):
    return np.ascontiguousarray(np.asarray(v, np.float32).reshape(-1, 128).T)


def _cmat():
    m = np.zeros((128, 7, 128), np.float32)
    r = np.arange(128)
    hb = (r[:, None] // 64) == (r[None, :] // 64)
    ti, tj = r[:, None] % 64, r[None, :] % 64
    m[:, 0, :] = np.eye(128)
    m[:, 1, :] = hb & (tj < ti)
    m[:, 2, :] = hb & (ti < tj)
    m[:, 3, :] = hb & (ti <= tj)
    m[:, 4, :] = hb
    m[:, 5, :] = 1.0
    m[:, 5, 0] = 0.0
    m[:64, 6, :] = 1.0
    return m


def _lora(inputs):
    m = np.zeros((128, 2, 2, 1024), np.float32)
    m[:64, :, 0, :] = np.transpose(inputs["w_up"][0], (1, 0, 2))
    m[64:, :, 1, :] = np.transpose(inputs["a_up"][0], (1, 0, 2))
    return m


def make_in_maps(inputs):
    x = np.asarray(inputs["x"], np.float32)
    ctx = np.asarray(inputs["ctx"], np.float32)
    maps = []
    shared = {
        "w_ada": np.ascontiguousarray(inputs["w_ada"][0], dtype=np.float32),
        "ffn_a_wi": np.ascontiguousarray(inputs["ffn_a_wi"][0], dtype=np.float32),
        "ffn_a_wo": np.ascontiguousarray(inputs["ffn_a_wo"][0], dtype=np.float32),
        "ffn_b_wi": np.ascontiguousarray(inputs["ffn_b_wi"][0], dtype=np.float32),
        "ffn_b_wo": np.ascontiguousarray(inputs["ffn_b_wo"][0], dtype=np.float32),
        "w_in": np.ascontiguousarray(inputs["w_in"][0], dtype=np.float32),
        "w_out": np.ascontiguousarray(inputs["w_out"][0], dtype=np.float32),
        "gm_ln_g": np.ascontiguousarray(inputs["gm_ln_g"][0], dtype=np.float32),
        "gm_ln_b": np.ascontiguousarray(inputs["gm_ln_b"][0], dtype=np.float32),
        "gm_bs": np.ascontiguousarray(inputs["gm_bs"][0].reshape(-1), dtype=np.float32),
        "gm_wsT": np.ascontiguousarray(np.transpose(inputs["gm_ws"][0], (2, 0, 1)), dtype=np.float32),
        "cmat": _cmat(),
        "lora": _lora(inputs),
        "g_up": np.ascontiguousarray(inputs["g_up"][0], dtype=np.float32),
    }
    for b in range(NCORES):
        vec = np.zeros((128, NV), np.float32)

        def put(name, arr):
            o, w = VC[name]
            assert arr.shape == (128, w), (name, arr.shape)
            vec[:, o:o + w] = arr

        put("c", _pm(inputs["c"][b]))
        put("cctx", _pm(inputs["c_ctx"]))
        put("b_ada", _pm(inputs["b_ada"][0]))
        put("ln_g", _pm(inputs["ln_g"][0].reshape(-1)))
        put("ln_b", _pm(inputs["ln_b"][0].reshape(-1)))
        put("mu", _pm(inputs["mu_shift"][0]))
        put("w0", _pm(inputs["w0"][0].reshape(-1)))
        put("a0", _pm(inputs["a0"][0].reshape(-1)))
        put("k_k", _pm(inputs["k_k"][0]))
        put("k_a", _pm(inputs["k_a"][0]))
        put("r_k", _pm(inputs["r_k"][0].reshape(-1)))
        put("gn_g", _pm(inputs["gn_g"][0]))
        put("gn_b", _pm(inputs["gn_b"][0]))
        put("sel4", (np.arange(128)[:, None] % 4 == np.arange(4)[None, :]).astype(np.float32))
        put("sel2", (np.arange(128)[:, None] % 2 == np.arange(2)[None, :]).astype(np.float32))
        xT = np.empty((D, TT), np.float32)
        xT[:, :T] = x[b].T
        xT[:, T:] = ctx[b].T
        m = dict(shared)
        m["xT"] = xT
        m["vecs"] = vec
        maps.append(m)
    return maps


def kernel(**inputs):
    if "nc" not in _NC_CACHE:
        _NC_CACHE["nc"] = build()
    nc = _NC_CACHE["nc"]
    maps = make_in_maps(inputs)
    res = run_bass_kernel_spmd(nc, maps, core_ids=list(range(NCORES)))
    out = np.empty((NCORES, T, D), np.float32)
    for b in range(NCORES):
        out[b] = res.results[b]["outT"].T
    return out
```

```python
import numpy as np
from contextlib import ExitStack
import concourse.bass as bass
import concourse.mybir as mybir
from concourse.bass_utils import run_bass_kernel_spmd

F32 = mybir.dt.float32
BF16 = mybir.dt.bfloat16
F32R = mybir.dt.float32r
AF = mybir.ActivationFunctionType
ALU = mybir.AluOpType

D = 2048
T = 4096
CT = 256
TT = T + CT
DFF = 5632
NMOD = 9
RW = 1024
RWIN = 3328
INW = 5376
ALPHA = 2.0 ** 0.25
LN_EPS = 1e-5
GN_EPS = 64e-5
NCORES = 8

VC = {}
_o = 0
for _n, _w in [("c", 16), ("cctx", 16), ("b_ada", 144), ("ln_g", 48), ("ln_b", 48), ("mu", 26),
               ("w0", 16), ("a0", 16), ("k_k", 8), ("k_a", 8), ("r_k", 8), ("gn_g", 8), ("gn_b", 8),
               ("sel4", 4), ("sel2", 2)]:
    VC[_n] = (_o, _w)
    _o += _w
NV = _o

ENGS = ["pe", "act", "dve", "pool", "sp"]


class Tok:
    __slots__ = ("sk", "n")

    def __init__(self, sk):
        self.sk = sk
        self.n = None


class Prog:
    EPOCH = 30000

    def __init__(self, nc, es):
        self.nc = nc
        self.es = es
        self.ops = {e: [] for e in ENGS}
        self.cnt = {}
        self.cur = {}
        self.lastw = {}
        self.readers = {}
        self.semh = {}
        self.latest = {}

    def _tok(self, sk, sig):
        t = self.cur.get(sk)
        if t is None:
            t = Tok(sk)
            self.cur[sk] = t
        if sig:
            self.cnt[sk] = self.cnt.get(sk, 0) + 1
            t.n = self.cnt[sk]
            self.cur[sk] = None
            self.latest[sk] = t
        return t

    def _deps(self, reads, writes, own_group=None):
        deps = []
        for k in reads:
            t = self.lastw.get(k)
            if t is not None:
                deps.append(t)
        for k in writes:
            t = self.lastw.get(k)
            if t is not None and not (own_group is not None and t.sk == own_group):
                deps.append(t)
            r = self.readers.get(k)
            if r:
                deps.extend(r.values())
        return deps

    def _commit(self, tok, reads, writes):
        for k in reads:
            self.readers.setdefault(k, {})[tok.sk] = tok
        for k in writes:
            self.lastw[k] = tok
            self.readers[k] = {}

    def op(self, eng, fn, reads=(), writes=(), sig=True):
        deps = self._deps(reads, writes)
        tok = self._tok(eng, sig)
        self.ops[eng].append((deps, fn, tok if sig else None, 1))
        self._commit(tok, reads, writes)
        return tok

    def dma(self, q, out, in_, reads=(), writes=(), group=None):
        if group is None:
            group = writes[0] if writes else reads[0]
        sk = ("dma", group)
        deps = self._deps(reads, writes, own_group=sk)
        tok = self._tok(sk, True)
        self.ops[q].append((deps, (lambda e, o=out, i=in_: e.dma_start(out=o, in_=i)), tok, 16))
        self._commit(tok, reads, writes)
        return tok

    def barrier(self):
        toks = [t for t in self.latest.values()]
        for e in ENGS:
            self.ops[e].append((list(toks), None, None, 0))

    def _semval(self, tok, per):
        ep = (tok.n - 1) // per
        v = (tok.n - 1) % per + 1
        key = (tok.sk, ep)
        h = self.semh.get(key)
        if h is None:
            h = self.es.enter_context(self.nc.semaphore("s%d" % len(self.semh)))
            self.semh[key] = h
        return key, h, v

    def emit(self, block):
        def per_of(sk):
            return self.EPOCH // 16 if isinstance(sk, tuple) else self.EPOCH

        def run(e, name):
            waited = {}
            for deps, fn, tok, inc in self.ops[name]:
                need = {}
                for t in deps:
                    if t.n is None:
                        raise RuntimeError("unsignaled dependency on %s" % (t.sk,))
                    if t.sk == name and name == "pe":
                        continue
                    key, h, v = self._semval(t, per_of(t.sk))
                    if waited.get(key, 0) >= v:
                        continue
                    if need.get(key, (None, 0))[1] < v:
                        need[key] = (h, v)
                for key, (h, v) in need.items():
                    e.wait_ge(h, v * (16 if isinstance(key[0], tuple) else 1))
                    waited[key] = v
                if fn is None:
                    continue
                ins = fn(e)
                if tok is not None:
                    key, h, v = self._semval(tok, per_of(tok.sk))
                    ins.then_inc(h, inc)

        for name in ENGS:
            for deps, fn, tok, inc in self.ops[name]:
                for t in deps:
                    if t.n is not None:
                        self._semval(t, per_of(t.sk))
                if tok is not None:
                    self._semval(tok, per_of(tok.sk))

        @block.tensor
        def _(e):
            run(e, "pe")

        @block.scalar
        def _(e):
            run(e, "act")

        @block.vector
        def _(e):
            run(e, "dve")

        @block.gpsimd
        def _(e):
            run(e, "pool")

        @block.sync
        def _(e):
            run(e, "sp")


class Carver:
    def __init__(self, pool, nwords):
        self.pool = pool
        self.n = nwords
        self.off = 0

    def mark(self):
        return self.off

    def reset(self, m):
        self.peak = max(getattr(self, "peak", 0), self.off)
        self.off = m

    def f32(self, n, dt=None):
        n = (n + 1) // 2 * 2
        a = self.pool[:, self.off:self.off + n]
        self.off += n
        assert self.off <= self.n, "SBUF pool overflow %d > %d" % (self.off, self.n)
        return a

    def bf16(self, n):
        w = (n + 3) // 4 * 2
        a = self.pool[:, self.off:self.off + w].bitcast(BF16)
        self.off += w
        assert self.off <= self.n, "SBUF pool overflow %d > %d" % (self.off, self.n)
        return a[:, 0:n]


def r3(ap, **kw):
    return ap.rearrange("p (a b) -> p a b", **kw)


class Ctx:
    pass


def build(stage=99):
    nc = bass.Bass("TRN2", target_bir_lowering=False)
    C = Ctx()
    C.nc = nc
    dt_in = lambda n, s: nc.dram_tensor(n, s, F32, kind="ExternalInput").ap()
    C.xT = dt_in("xT", [D, TT])
    C.vecs = dt_in("vecs", [128, NV])
    C.w_ada = dt_in("w_ada", [D, NMOD * D])
    C.wa_i = dt_in("ffn_a_wi", [D, 2 * DFF])
    C.wa_o = dt_in("ffn_a_wo", [DFF, D])
    C.wb_i = dt_in("ffn_b_wi", [D, 2 * DFF])
    C.wb_o = dt_in("ffn_b_wo", [DFF, D])
    C.w_in = dt_in("w_in", [D, INW])
    C.w_out = dt_in("w_out", [D, D])
    C.gm_ln_g = dt_in("gm_ln_g", [1024])
    C.gm_ln_b = dt_in("gm_ln_b", [1024])
    C.gm_bs = dt_in("gm_bs", [2048])
    C.gm_wsT = dt_in("gm_wsT", [128, 16, 128])
    C.cmat = dt_in("cmat", [128, 7, 128])
    C.lora = dt_in("lora", [128, 2, 2, 1024])
    C.g_up = dt_in("g_up", [128, 1024])
    sc = lambda n, s, d: nc.dram_tensor(n, s, d).ap()
    C.wa_i_b = sc("wa_i_b", [D, 2 * DFF], BF16)
    C.wa_o_b = sc("wa_o_b", [DFF, D], BF16)
    C.wb_i_b = sc("wb_i_b", [D, 2 * DFF], BF16)
    C.wb_o_b = sc("wb_o_b", [DFF, D], BF16)
    C.w_in_b = sc("w_in_b", [D, INW], BF16)
    C.w_out_b = sc("w_out_b", [D, D], BF16)
    dbg = lambda n, s, d, st: (nc.dram_tensor("dbg", s, d, kind="ExternalOutput").ap() if stage == st
                                else sc(n, s, d))
    C.x1T = dbg("x1T", [D, TT], F32, 1)
    C.pT = dbg("pT", [RWIN, TT], F32, 2)
    C.catT = dbg("catT", [D, T], BF16, 5)
    C.xsT = dbg("xsT", [RWIN, TT], F32, 3)
    C.yTs = [dbg("yT0", [RW, T], F32, 4), sc("yT1", [RW, T], F32)]
    C.bonT = [sc("bonT0", [RW, T], F32), sc("bonT1", [RW, T], F32)]
    C.gTs = sc("gTs", [RW, T], F32)
    C.x2T = dbg("x2T", [D, T], F32, 6)
    C.outT = nc.dram_tensor("outT", [D, T], F32, kind="ExternalOutput").ap() if stage >= 99 else sc("outT", [D, T], F32)

    with ExitStack() as es:
        NCW = 9 * 256
        cpool = es.enter_context(nc.sbuf_tensor("cpool", [128, NCW], F32))
        C.ps = [es.enter_context(nc.psum_tensor("ps%d" % i, [128, 512], F32)) for i in range(8)]
        p = Prog(nc, es)
        C.p = p
        C.AC = Carver(cpool, NCW)
        st = {"k": 0}

        def run_phase(fn, kib, rkib=0):
            st["k"] += 1
            cm = nc.sbuf_tensor("ph%d" % st["k"], [128, kib * 256], F32)
            C.A = Carver(cm.__enter__(), kib * 256)
            cr = None
            if rkib:
                cr = nc.sbuf_tensor("rp%d" % st["k"], [128, rkib * 512], BF16)
                C.AR = Carver(cr.__enter__(), rkib * 512)
            fn()
            print("phase", st["k"], "pool use KiB", max(C.A.off, getattr(C.A, "peak", 0)) / 256.0,
                  (max(C.AR.off, getattr(C.AR, "peak", 0)) / 512.0 if rkib else 0))
            if cr is not None:
                cr.__exit__(None, None, None)
            cm.__exit__(None, None, None)
            p.barrier()

        def ph0():
            phase_consts(C)
            phase_cast(C)
            phase_mod(C)
        run_phase(ph0, 130)
        tiles = [(512 * i, 512, "x") for i in range(8)] + [(T, CT, "c")]
        run_phase(lambda: ffn_phase(C, "fa", C.xT, C.x1T, tiles, C.wa_i_b, C.wa_o_b, 0, 0), 196)
        if stage >= 2:
            run_phase(lambda: proj_phase(C), 184)
        if stage >= 3:
            run_phase(lambda: shift_phase(C), 40)
        if stage >= 4:
            def mix():
                mixer_consts(C)
                m1, m2 = C.A.mark(), C.AR.mark()
                for d in range(2 if stage >= 5 else 1):
                    mixer_pass(C, d)
                    C.A.reset(m1)
                    C.AR.reset(m2)
                    p.barrier()
            run_phase(mix, 78, 70)
        if stage >= 5:
            run_phase(lambda: rwkv_out_phase(C), 60)
        if stage >= 6:
            run_phase(lambda: wout_phase(C), 160)
        if stage >= 99:
            tiles_b = [(512 * i, 512, "x") for i in range(8)]
            run_phase(lambda: ffn_phase(C, "fb", C.x2T, C.outT, tiles_b, C.wb_i_b, C.wb_o_b, 2, 2), 196)
        p.barrier()
        block = es.enter_context(nc.Block())
        p.emit(block)
    return nc


def phase_consts(C):
    p, A = C.p, C.AC
    C.vt = A.f32(NV)
    p.dma("sp", C.vt, C.vecs, writes=["vecs"])
    C.ones = A.f32(128)
    C.epsln = A.f32(2)[:, 0:1]
    p.op("pool", lambda e: e.memset(C.epsln, LN_EPS / (ALPHA * ALPHA)), writes=["epsc"])
    p.op("pool", lambda e: e.memset(C.ones, 1.0), writes=["ones"])
    C.epsg = A.f32(2)[:, 0:1]
    p.op("pool", lambda e: e.memset(C.epsg, LN_EPS), writes=["epsc2"])


def vcol(C, name, i=0, n=1):
    o, w = VC[name]
    return C.vt[:, o + i:o + i + n]


def phase_cast(C):
    p = C.p
    C.cast_keys = {}

    def cast(name, src, dst, rows, piece):
        keys = []
        for r0 in range(0, rows, piece):
            k = ("w", name, r0)
            p.dma("pool", dst[r0:r0 + piece, :], src[r0:r0 + piece, :], writes=[k], group=("cast", name))
            keys.append(k)
        C.cast_keys[name] = keys

    C.cast = cast


def phase_mod(C):
    p, A, AC = C.p, C.A, C.AC
    s_bf = r3(AC.bf16(32), b=2)
    ctmp = AC.f32(32)
    C.modx = None
    oc, _ = VC["c"]
    p.op("act", lambda e: e.activation(out=ctmp, in_=C.vt[:, oc:oc + 32], func=AF.Silu),
         reads=["vecs"], writes=["ctmp"])
    p.op("dve", lambda e: e.tensor_copy(out=s_bf[:, :, 0], in_=ctmp[:, 0:16]), reads=["ctmp"], writes=["s_bf0"])
    p.op("dve", lambda e: e.tensor_copy(out=s_bf[:, :, 1], in_=ctmp[:, 16:32]), reads=["ctmp"], writes=["s_bf1"])
    wad = [r3(A.bf16(16 * 2048), b=2048) for _ in range(2)]
    src = C.w_ada.rearrange("(kc p) f -> p kc f", p=128)
    psm = C.ps[0]
    for s in range(NMOD):
        w = wad[s % 2]
        for h in range(2):
            p.dma("pool", w[:, 8 * h:8 * h + 8, :], src[:, 8 * h:8 * h + 8, s * D:(s + 1) * D],
                  writes=[("wad", s % 2)])
        for j in range(16):
            m = s * 16 + j
            for kc in range(16):
                p.op("pe", lambda e, w=w, kc=kc, j=j, m=m: e.matmul(
                    psm[:, 2 * m:2 * m + 2], w[:, kc, j * 128:(j + 1) * 128], s_bf[:, kc, :],
                    start=(kc == 0), stop=(kc == 15)),
                    reads=[("wad", s % 2), "s_bf0", "s_bf1"], writes=["psmod"], sig=(kc == 15))
    C.cast("wa_i", C.wa_i, C.wa_i_b, D, 128)
    C.cast("wa_o", C.wa_o, C.wa_o_b, DFF, 128)
    C.cast("w_in", C.w_in, C.w_in_b, D, 128)
    C.cast("w_out", C.w_out, C.w_out_b, D, 128)
    C.cast("wb_i", C.wb_i, C.wb_i_b, D, 128)
    C.cast("wb_o", C.wb_o, C.wb_o_b, DFF, 128)
    C.mod = {}
    ob, _ = VC["b_ada"]
    pv = psm[:, 0:288].rearrange("p (m two) -> p m two", two=2)
    for idx, nm in enumerate(["x", "c"]):
        mt = AC.f32(144)
        p.op("dve", lambda e, mt=mt, idx=idx: e.tensor_tensor(out=mt, in0=pv[:, :, idx], in1=C.vt[:, ob:ob + 144],
                                                              op=ALU.add),
             reads=["psmod", "vecs"], writes=[("mod", nm)])
        der = AC.f32(16 * 6)
        for j in range(3):
            p.op("dve", lambda e, der=der, mt=mt, j=j: e.tensor_scalar_add(
                out=der[:, 16 * j:16 * j + 16], in0=mt[:, (3 * j + 1) * 16:(3 * j + 2) * 16], scalar1=1.0),
                reads=[("mod", nm)], writes=[("der", nm, j)])
            gs = (0.5 if j != 1 else 1.0) / ALPHA
            p.op("dve", lambda e, der=der, mt=mt, j=j, gs=gs: e.tensor_scalar_mul(
                out=der[:, 48 + 16 * j:48 + 16 * j + 16], in0=mt[:, (3 * j + 2) * 16:(3 * j + 3) * 16], scalar1=gs),
                reads=[("mod", nm)], writes=[("derg", nm, j)])
        C.mod[nm] = (mt, der)


def mod_aps(C, nm, j):
    mt, der = C.mod[nm]
    shift = mt[:, (3 * j) * 16:(3 * j + 1) * 16]
    sc1p = der[:, 16 * j:16 * j + 16]
    gate = der[:, 48 + 16 * j:48 + 16 * j + 16]
    keys = [("mod", nm), ("der", nm, j), ("derg", nm, j)]
    return shift, sc1p, gate, keys


def ffn_phase(C, tag, src, dst, tiles, wi_b, wo_b, j, lnj):
    p, A, nc = C.p, C.A, C.nc
    W = Ctx()
    W.hT = [r3(A.bf16(16 * 512), b=512) for _ in range(2)]
    W.gT = r3(A.bf16(44 * 512), b=512)
    W.z = r3(A.f32(16 * 512), b=512)
    W.ws = [A.bf16(16384) for _ in range(2)]
    W.xst = [A.f32(512) for _ in range(2)]
    W.sg = [A.f32(512) for _ in range(2)]
    W.sq = [A.f32(512) for _ in range(2)]
    W.mean = A.f32(512)
    W.var = A.f32(512)
    W.rstd = A.f32(512)
    W.ost = [A.f32(512) for _ in range(2)]
    wi_v = wi_b.rearrange("(kc p) f -> p kc f", p=128)
    wo_v = wo_b.rearrange("(kc p) f -> p kc f", p=128)
    wname_i = {"fa": "wa_i", "fb": "wb_i"}[tag]
    wname_o = {"fa": "wa_o", "fb": "wb_o"}[tag]
    wkeys_i = C.cast_keys[wname_i]
    wkeys_o = C.cast_keys[wname_o]
    og, _ = VC["ln_g"]
    ol, _ = VC["ln_b"]
    eps_p = LN_EPS / (ALPHA * ALPHA)
    st = {"w": 0, "q": 0, "x": 0, "o": 0}

    def modulate(ti):
        t0, nt, nm = tiles[ti]
        shift, sc1p, gate, mk = mod_aps(C, nm, j)
        hT = W.hT[ti % 2]
        for dc in range(16):
            xs = W.xst[st["x"] % 2]
            xk = ("xst", st["x"] % 2)
            st["x"] += 1
            p.dma("pool", xs[:, :nt], src[dc * 128:(dc + 1) * 128, t0:t0 + nt],
                  reads=[("dst", "w3", ti, dc)], writes=[xk])
            if dc % 2 == 0:
                p.op("act", lambda e, xs=xs, hT=hT, dc=dc, nt=nt: e.activation(
                    out=hT[:, dc, :nt], in_=xs[:, :nt], func=AF.Identity,
                    scale=sc1p[:, dc:dc + 1], bias=shift[:, dc:dc + 1]),
                    reads=[xk] + mk, writes=[("hT", ti % 2, dc)])
            else:
                p.op("dve", lambda e, xs=xs, hT=hT, dc=dc, nt=nt: e.tensor_scalar(
                    out=hT[:, dc, :nt], in0=xs[:, :nt], scalar1=sc1p[:, dc:dc + 1], scalar2=shift[:, dc:dc + 1],
                    op0=ALU.mult, op1=ALU.add),
                    reads=[xk] + mk, writes=[("hT", ti % 2, dc)])

    def up(ti):
        t0, nt, nm = tiles[ti]
        hT = W.hT[ti % 2]
        hk = [("hT", ti % 2, dc) for dc in range(16)]
        for fb in range(11):
            si = st["w"] % 2
            st["w"] += 1
            slot = W.ws[si].rearrange("p (w k f) -> p w k f", w=2, k=16)
            for w_ in range(2):
                c0 = w_ * DFF + fb * 512
                p.dma("sp", slot[:, w_], wi_v[:, :, c0:c0 + 512], reads=wkeys_i, writes=[("ws", si)])
            for jj in range(4):
                q = st["q"]
                st["q"] += 1
                pg, pu = C.ps[2 * (q % 2)], C.ps[2 * (q % 2) + 1]
                for w_, ps in ((0, pg), (1, pu)):
                    for kc in range(16):
                        p.op("pe", lambda e, ps=ps, slot=slot, w_=w_, kc=kc, jj=jj, hT=hT, nt=nt: e.matmul(
                            ps[:, :nt], slot[:, w_, kc, jj * 128:(jj + 1) * 128], hT[:, kc, :nt],
                            start=(kc == 0), stop=(kc == 15)),
                            reads=[("ws", si)] + (hk if kc == 0 else []), writes=[("ps", 2 * (q % 2) + w_)],
                            sig=(kc == 15))
                sg = W.sg[q % 2]
                p.op("act", lambda e, sg=sg, pg=pg, nt=nt: e.activation(out=sg[:, :nt], in_=pg[:, :nt], func=AF.Silu),
                     reads=[("ps", 2 * (q % 2))], writes=[("sg", q % 2)])
                fc = fb * 4 + jj
                p.op("dve", lambda e, sg=sg, pu=pu, fc=fc, nt=nt: e.tensor_tensor(
                    out=W.gT[:, fc, :nt], in0=sg[:, :nt], in1=pu[:, :nt], op=ALU.mult),
                    reads=[("sg", q % 2), ("ps", 2 * (q % 2) + 1)], writes=[("gT", fc)])

    def down(ti):
        t0, nt, nm = tiles[ti]
        shift, sc1p, gate, mk = mod_aps(C, nm, j)
        gk = [("gT", fc) for fc in range(44)]
        pend = None

        def stats(oc, nt=nt):
            p.op("pe", lambda e: e.matmul(C.ps[6][:, :nt], C.ones, W.z[:, oc, :nt], start=(oc == 0), stop=(oc == 15)),
                 reads=["ones", ("z", oc)], writes=[("ps", 6)], sig=(oc == 15))
            p.op("pe", lambda e: e.matmul(C.ps[7][:, :nt], C.ones, W.sq[oc % 2][:, :nt], start=(oc == 0),
                                          stop=(oc == 15)),
                 reads=["ones", ("sq", oc % 2)], writes=[("ps", 7)], sig=True)

        for ob in range(8):
            si = st["w"] % 2
            st["w"] += 1
            slot = W.ws[si][:, 0:44 * 256].rearrange("p (k f) -> p k f", k=44)
            for h in range(2):
                p.dma("sp", slot[:, 22 * h:22 * h + 22, :], wo_v[:, 22 * h:22 * h + 22, ob * 256:(ob + 1) * 256],
                      reads=wkeys_o, writes=[("ws", si)])
            for o2 in range(2):
                oc = ob * 2 + o2
                p.dma("pool", W.z[:, oc, :nt], src[oc * 128:(oc + 1) * 128, t0:t0 + nt],
                      reads=[("dst", "w3", ti, oc)], writes=[("z", oc)])
                ps = C.ps[4 + oc % 2]
                for kc in range(44):
                    p.op("pe", lambda e, ps=ps, slot=slot, kc=kc, o2=o2, nt=nt: e.matmul(
                        ps[:, :nt], slot[:, kc, o2 * 128:(o2 + 1) * 128], W.gT[:, kc, :nt],
                        start=(kc == 0), stop=(kc == 43)),
                        reads=[("ws", si)] + (gk if kc == 0 else []), writes=[("ps", 4 + oc % 2)], sig=(kc == 43))
                if pend is not None:
                    stats(pend)
                p.op("dve", lambda e, ps=ps, oc=oc, nt=nt: e.scalar_tensor_tensor(
                    out=W.z[:, oc, :nt], in0=ps[:, :nt], scalar=gate[:, oc:oc + 1], in1=W.z[:, oc, :nt],
                    op0=ALU.mult, op1=ALU.add),
                    reads=[("ps", 4 + oc % 2)] + mk, writes=[("z", oc)])
                p.op("act", lambda e, oc=oc, nt=nt: e.activation(out=W.sq[oc % 2][:, :nt], in_=W.z[:, oc, :nt],
                                                                  func=AF.Square),
                     reads=[("z", oc)], writes=[("sq", oc % 2)])
                pend = oc
        stats(pend)
        inv = 1.0 / D
        p.op("act", lambda e: e.activation(out=W.mean[:, :nt], in_=C.ps[6][:, :nt], func=AF.Copy, scale=inv),
             reads=[("ps", 6)], writes=["mean"])
        p.op("act", lambda e: e.activation(out=W.var[:, :nt], in_=C.ps[6][:, :nt], func=AF.Square, scale=inv),
             reads=[("ps", 6)], writes=["var"])
        p.op("dve", lambda e: e.scalar_tensor_tensor(out=W.var[:, :nt], in0=C.ps[7][:, :nt], scalar=inv,
                                                     in1=W.var[:, :nt], op0=ALU.mult, op1=ALU.subtract),
             reads=[("ps", 7), "var"], writes=["var"])
        p.op("act", lambda e: e.activation(out=W.rstd[:, :nt], in_=W.var[:, :nt], func=AF.Ln, bias=C.epsln, scale=1.0),
             reads=["var", "epsc"], writes=["rstd"])
        p.op("act", lambda e: e.activation(out=W.rstd[:, :nt], in_=W.rstd[:, :nt], func=AF.Exp, scale=-0.5),
             reads=["rstd"], writes=["rstd"])
        for oc in range(16):
            p.op("pool", lambda e, oc=oc: e.tensor_tensor(out=W.z[:, oc, :nt], in0=W.z[:, oc, :nt], in1=W.mean[:, :nt],
                                                          op=ALU.subtract),
                 reads=[("z", oc), "mean"], writes=[("z", oc)])
            p.op("dve", lambda e, oc=oc: e.tensor_tensor(out=W.z[:, oc, :nt], in0=W.z[:, oc, :nt], in1=W.rstd[:, :nt],
                                                         op=ALU.mult),
                 reads=[("z", oc), "rstd"], writes=[("z", oc)])
            oi = st["o"] % 2
            st["o"] += 1
            ot = W.ost[oi]
            gcol = C.vt[:, og + lnj * 16 + oc:og + lnj * 16 + oc + 1]
            bcol = C.vt[:, ol + lnj * 16 + oc:ol + lnj * 16 + oc + 1]
            p.op("act", lambda e, oc=oc, ot=ot, gcol=gcol, bcol=bcol: e.activation(
                out=ot[:, :nt], in_=W.z[:, oc, :nt], func=AF.Identity, scale=gcol, bias=bcol),
                reads=[("z", oc), "vecs"], writes=[("ost", oi)])
            p.dma("pool", dst[oc * 128:(oc + 1) * 128, t0:t0 + nt], ot[:, :nt],
                  reads=[("ost", oi)], writes=[("dst", tag, ti, oc)], group=("ostd", oi))

    modulate(0)
    for ti in range(len(tiles)):
        up(ti)
        if ti + 1 < len(tiles):
            modulate(ti + 1)
        down(ti)


def proj_phase(C):
    p, A = C.p, C.A
    W = Ctx()
    W.hT = [r3(A.bf16(16 * 512), b=512) for _ in range(2)]
    W.ws = [r3(A.bf16(16 * 512), b=512) for _ in range(4)]
    W.xst = [A.f32(512) for _ in range(2)]
    W.ost = [A.f32(512) for _ in range(3)]
    W.uT = r3(A.f32(8 * 512), b=512)
    W.vg = [A.f32(1024) for _ in range(4)]
    W.sqv = A.f32(1024)
    W.vn = [A.bf16(1024) for _ in range(2)]
    W.gmo = r3(A.bf16(8 * 512), b=512)
    W.st = [A.f32(16) for _ in range(6)]
    W.tmp = [A.f32(128) for _ in range(2)]
    lng = A.f32(1024)
    lnb = A.f32(1024)
    bsb = r3(A.f32(16 * 128), b=128)
    wsT = r3(A.bf16(16 * 128), b=128)
    p.dma("sp", lng, C.gm_ln_g.to_broadcast([128, 1024]) if False else bcast_rows(C.gm_ln_g, 1024), writes=["lng"])
    p.dma("sp", lnb, bcast_rows(C.gm_ln_b, 1024), writes=["lnb"])
    p.dma("sp", bsb, bcast_rows(C.gm_bs, 2048).rearrange("p (a b) -> p a b", b=128), writes=["bsb"])
    p.dma("pool", wsT, C.gm_wsT, writes=["wsT"])
    w_v = C.w_in_b.rearrange("(kc p) f -> p kc f", p=128)
    wkeys = C.cast_keys["w_in"]
    shift, sc1p, gate, mkx = None, None, None, None
    tiles = [(512 * i, 512, "x") for i in range(8)] + [(T, CT, "c")]
    st = {"w": 0, "x": 0, "o": 0, "ps": 0}

    def modulate(ti):
        t0, nt, nm = tiles[ti]
        shift, sc1p, gate, mk = mod_aps(C, nm, 1)
        hT = W.hT[ti % 2]
        for dc in range(16):
            xs = W.xst[st["x"] % 2]
            xk = ("xst", st["x"] % 2)
            st["x"] += 1
            p.dma("pool", xs[:, :nt], C.x1T[dc * 128:(dc + 1) * 128, t0:t0 + nt],
                  reads=[("dst", "fa", ti, dc)], writes=[xk])
            if dc % 2 == 0:
                p.op("act", lambda e, xs=xs, hT=hT, dc=dc, nt=nt, sc1p=sc1p, shift=shift: e.activation(
                    out=hT[:, dc, :nt], in_=xs[:, :nt], func=AF.Identity,
                    scale=sc1p[:, dc:dc + 1], bias=shift[:, dc:dc + 1]),
                    reads=[xk] + mk, writes=[("hT", ti % 2, dc)])
            else:
                p.op("dve", lambda e, xs=xs, hT=hT, dc=dc, nt=nt, sc1p=sc1p, shift=shift: e.tensor_scalar(
                    out=hT[:, dc, :nt], in0=xs[:, :nt], scalar1=sc1p[:, dc:dc + 1], scalar2=shift[:, dc:dc + 1],
                    op0=ALU.mult, op1=ALU.add),
                    reads=[xk] + mk, writes=[("hT", ti % 2, dc)])

    def load_w(f0, nf):
        si = st["w"] % 4
        st["w"] += 1
        slot = W.ws[si]
        p.dma("sp", slot[:, :, :nf], w_v[:, :, f0:f0 + nf], reads=wkeys, writes=[("pws", si)])
        return slot, ("pws", si)

    def nextps():
        i = st["ps"] % 8
        st["ps"] += 1
        return C.ps[i], ("ps", i)

    def tile(ti):
        t0, nt, nm = tiles[ti]
        hT = W.hT[ti % 2]
        hk = [("hT", ti % 2, dc) for dc in range(16)]
        nfm = 34 if nm == "x" else 26
        fc = 0
        while fc < nfm:
            nchunk = min(4, nfm - fc) if fc != 24 else 2
            slot, sk = load_w(fc * 128, nchunk * 128)
            for jj in range(nchunk):
                ps, pk = nextps()
                for kc in range(16):
                    p.op("pe", lambda e, ps=ps, slot=slot, kc=kc, jj=jj, nt=nt: e.matmul(
                        ps[:, :nt], slot[:, kc, jj * 128:(jj + 1) * 128], hT[:, kc, :nt],
                        start=(kc == 0), stop=(kc == 15)),
                        reads=[sk] + (hk if kc == 0 else []), writes=[pk], sig=(kc == 15))
                f = fc + jj
                if f < 26:
                    oi = st["o"] % 3
                    st["o"] += 1
                    ot = W.ost[oi]
                    p.op("act", lambda e, ot=ot, ps=ps, nt=nt: e.activation(out=ot[:, :nt], in_=ps[:, :nt],
                                                                          func=AF.Copy),
                         reads=[pk], writes=[("post", oi)])
                    p.dma("pool", C.pT[f * 128:(f + 1) * 128, t0:t0 + nt], ot[:, :nt],
                          reads=[("post", oi)], writes=[("pT", ti, f)], group=("postd", oi))
                else:
                    p.op("act", lambda e, ps=ps, f=f, nt=nt: e.activation(
                        out=W.uT[:, f - 26, :nt], in_=ps[:, :nt], func=AF.Gelu_apprx_tanh),
                        reads=[pk], writes=[("uT", f - 26)])
            fc += nchunk
        if nm != "x":
            return
        vs = [load_w(4352 + 512 * h, 512) for h in range(2)]
        for h in range(2):
            slot, sk = vs[h]
            for tb in range(4):
                ps, pk = nextps()
                for kc in range(16):
                    p.op("pe", lambda e, ps=ps, slot=slot, kc=kc, tb=tb: e.matmul(
                        ps[:, :], hT[:, kc, tb * 128:(tb + 1) * 128], slot[:, kc, :],
                        start=(kc == 0), stop=(kc == 15)),
                        reads=[sk] + (hk if kc == 0 else []), writes=[pk], sig=(kc == 15))
                p.op("act", lambda e, ps=ps, tb=tb, h=h: e.activation(
                    out=W.vg[tb][:, h * 512:(h + 1) * 512], in_=ps[:, :], func=AF.Gelu_apprx_tanh),
                    reads=[pk], writes=[("vg", tb, h)])
        for tb in range(4):
            vg = W.vg[tb]
            vg3 = vg.rearrange("p (g d) -> p g d", d=64)
            s1, s2, mean, var, rstd, nm_ = W.st
            vk = [("vg", tb, 0), ("vg", tb, 1)]
            p.op("dve", lambda e, vg3=vg3: e.tensor_reduce(out=s1, in_=vg3, axis=mybir.AxisListType.X, op=ALU.add),
                 reads=vk, writes=["gs1"])
            p.op("act", lambda e, vg=vg: e.activation(out=W.sqv, in_=vg, func=AF.Square), reads=vk, writes=["sqv"])
            p.op("dve", lambda e: e.tensor_reduce(out=s2, in_=W.sqv.rearrange("p (g d) -> p g d", d=64),
                                                  axis=mybir.AxisListType.X, op=ALU.add),
                 reads=["sqv"], writes=["gs2"])
            p.op("dve", lambda e: e.tensor_scalar_mul(out=mean, in0=s1, scalar1=1.0 / 64), reads=["gs1"],
                 writes=["gmean"])
            p.op("dve", lambda e: e.tensor_tensor(out=var, in0=mean, in1=mean, op=ALU.mult), reads=["gmean"],
                 writes=["gvar"])
            p.op("dve", lambda e: e.scalar_tensor_tensor(out=var, in0=s2, scalar=1.0 / 64, in1=var, op0=ALU.mult,
                                                         op1=ALU.subtract),
                 reads=["gs2", "gvar"], writes=["gvar"])
            p.op("act", lambda e: e.activation(out=rstd, in_=var, func=AF.Ln, bias=C.epsg, scale=1.0),
                 reads=["gvar", "epsc2"], writes=["grstd"])
            p.op("act", lambda e: e.activation(out=rstd, in_=rstd, func=AF.Exp, scale=-0.5), reads=["grstd"],
                 writes=["grstd"])
            mb = mean.rearrange("p (g o) -> p g o", o=1).to_broadcast([128, 16, 64])
            rb = rstd.rearrange("p (g o) -> p g o", o=1).to_broadcast([128, 16, 64])
            p.op("pool", lambda e, vg3=vg3, mb=mb: e.tensor_tensor(out=vg3, in0=vg3, in1=mb, op=ALU.subtract),
                 reads=vk + ["gmean"], writes=vk)
            p.op("dve", lambda e, vg3=vg3, rb=rb: e.tensor_tensor(out=vg3, in0=vg3, in1=rb, op=ALU.mult),
                 reads=vk + ["grstd"], writes=vk)
            p.op("pool", lambda e, vg=vg: e.tensor_tensor(out=vg, in0=vg, in1=lng, op=ALU.mult),
                 reads=vk + ["lng"], writes=vk)
            vn = W.vn[tb % 2]
            p.op("dve", lambda e, vg=vg, vn=vn: e.tensor_tensor(out=vn, in0=vg, in1=lnb, op=ALU.add),
                 reads=vk + ["lnb"], writes=[("vn", tb % 2)])
            for g in range(16):
                if g % 4 == 0:
                    ps, pk = nextps()
                po = ps[:, (g % 4) * 128:(g % 4 + 1) * 128]
                p.op("pe", lambda e, po=po, vn=vn, g=g: e.matmul(
                    po, vn[:, (g // 2) * 128:(g // 2 + 1) * 128], wsT[:, g, :], start=True, stop=True),
                    reads=[("vn", tb % 2), "wsT"], writes=[pk])
                if g % 4 == 3:
                    for hh in range(2):
                        rows = slice(64 * hh, 64 * hh + 64)
                        for pair in range(2):
                            gg = g - 3 + 2 * pair + hh
                            tmp = W.tmp[hh]
                            pslice = ps[rows, (gg % 4) * 128:(gg % 4 + 1) * 128]
                            p.op("dve", lambda e, tmp=tmp, pslice=pslice, rows=rows, gg=gg: e.tensor_tensor(
                                out=tmp[rows, :], in0=pslice, in1=bsb[rows, gg, :], op=ALU.add),
                                reads=[pk, "bsb"], writes=[("gtmp", hh)])
                            p.op("pool", lambda e, tmp=tmp, rows=rows, gg=gg, tb=tb: e.tensor_tensor(
                                out=W.gmo[rows, gg // 2, tb * 128:(tb + 1) * 128], in0=tmp[rows, :],
                                in1=W.uT[rows, gg // 2, tb * 128:(tb + 1) * 128], op=ALU.mult),
                                reads=[("gtmp", hh), ("uT", gg // 2)], writes=[("gmo", gg // 2)])
        p.dma("pool", C.catT[1024:2048, t0:t0 + nt].rearrange("(cc p) t -> p cc t", p=128), W.gmo[:, :, :nt],
              reads=[("gmo", c) for c in range(8)], writes=[("catg", ti)], group="gmod")

    modulate(0)
    for ti in range(len(tiles)):
        if ti + 1 < len(tiles):
            modulate(ti + 1)
        tile(ti)


def bcast_rows(ap1d, n):
    return bass.AP(ap1d.tensor, ap1d.offset, [[0, 128], [1, n]])


def shift_phase(C):
    p, A = C.p, C.A
    X = [A.f32(640) for _ in range(3)]
    XS = [A.f32(512) for _ in range(3)]
    om, _ = VC["mu"]
    o4, _ = VC["sel4"]
    o2, _ = VC["sel2"]
    omm = A.f32(26)
    m4 = A.f32(26 * 4).rearrange("p (f j) -> p f j", j=4)
    m2 = A.f32(26 * 2).rearrange("p (f j) -> p f j", j=2)
    mu = C.vt[:, om:om + 26]
    p.op("dve", lambda e: e.tensor_scalar(out=omm, in0=mu, scalar1=-1.0, scalar2=1.0, op0=ALU.mult, op1=ALU.add),
         reads=["vecs"], writes=["omm"])
    for jj in range(4):
        p.op("dve", lambda e, jj=jj: e.tensor_scalar_mul(out=m4[:, :, jj], in0=mu, scalar1=C.vt[:, o4 + jj:o4 + jj + 1]),
             reads=["vecs"], writes=[("m4", jj)])
    for jj in range(2):
        p.op("dve", lambda e, jj=jj: e.tensor_scalar_mul(out=m2[:, :, jj], in0=mu, scalar1=C.vt[:, o2 + jj:o2 + jj + 1]),
             reads=["vecs"], writes=[("m2", jj)])
    mk = ["omm"] + [("m4", j) for j in range(4)] + [("m2", j) for j in range(2)]
    n = 0
    for blk in range(9):
        for fc in range(26):
            xi = n % 3
            n += 1
            x, xs = X[xi], XS[xi]
            xk, sk = ("shx", xi), ("shs", xi)
            if blk < 8:
                t0 = blk * 512
                lo = max(t0 - 64, 0)
                hi = min(t0 + 576, T)
                p.dma("sp", x[:, lo - (t0 - 64):hi - (t0 - 64)], C.pT[fc * 128:(fc + 1) * 128, lo:hi],
                      reads=[("pT", ti, fc) for ti in range(max(blk - 1, 0), min(blk + 2, 8))], writes=[xk])
                nt = 512
                cur = x[:, 64:576]
                p.op("act", lambda e, xs=xs, cur=cur, fc=fc: e.activation(out=xs, in_=cur, func=AF.Copy,
                                                                           scale=omm[:, fc:fc + 1]),
                     reads=[xk] + mk, writes=[sk])
                xs3 = xs.rearrange("p (r c) -> p r c", c=64)
                cur3 = cur.rearrange("p (r c) -> p r c", c=64)
                p.op("dve", lambda e, xs3=xs3, cur3=cur3, fc=fc: e.scalar_tensor_tensor(
                    out=xs3[:, :, 1:64], in0=cur3[:, :, 0:63], scalar=m4[:, fc, 0:1], in1=xs3[:, :, 1:64],
                    op0=ALU.mult, op1=ALU.add), reads=[xk, sk], writes=[sk])
                p.op("dve", lambda e, xs3=xs3, cur3=cur3, fc=fc: e.scalar_tensor_tensor(
                    out=xs3[:, :, 0:63], in0=cur3[:, :, 1:64], scalar=m4[:, fc, 1:2], in1=xs3[:, :, 0:63],
                    op0=ALU.mult, op1=ALU.add), reads=[xk, sk], writes=[sk])
                l0 = 64 if blk == 0 else 0
                p.op("dve", lambda e, xs=xs, x=x, fc=fc, l0=l0: e.scalar_tensor_tensor(
                    out=xs[:, l0:512], in0=x[:, l0:512], scalar=m4[:, fc, 2:3], in1=xs[:, l0:512],
                    op0=ALU.mult, op1=ALU.add), reads=[xk, sk], writes=[sk])
                h0 = 448 if blk == 7 else 512
                p.op("dve", lambda e, xs=xs, x=x, fc=fc, h0=h0: e.scalar_tensor_tensor(
                    out=xs[:, 0:h0], in0=x[:, 128:128 + h0], scalar=m4[:, fc, 3:4], in1=xs[:, 0:h0],
                    op0=ALU.mult, op1=ALU.add), reads=[xk, sk], writes=[sk])
            else:
                t0, nt = T, CT
                p.dma("sp", x[:, 0:CT], C.pT[fc * 128:(fc + 1) * 128, T:TT], reads=[("pT", 8, fc)], writes=[xk])
                cur = x[:, 0:CT]
                p.op("act", lambda e, xs=xs, cur=cur, fc=fc: e.activation(out=xs[:, 0:CT], in_=cur, func=AF.Copy,
                                                                           scale=omm[:, fc:fc + 1]),
                     reads=[xk] + mk, writes=[sk])
                p.op("dve", lambda e, xs=xs, x=x, fc=fc: e.scalar_tensor_tensor(
                    out=xs[:, 1:CT], in0=x[:, 0:CT - 1], scalar=m2[:, fc, 0:1], in1=xs[:, 1:CT],
                    op0=ALU.mult, op1=ALU.add), reads=[xk, sk], writes=[sk])
                p.op("dve", lambda e, xs=xs, x=x, fc=fc: e.scalar_tensor_tensor(
                    out=xs[:, 0:CT - 1], in0=x[:, 1:CT], scalar=m2[:, fc, 1:2], in1=xs[:, 0:CT - 1],
                    op0=ALU.mult, op1=ALU.add), reads=[xk, sk], writes=[sk])
            p.dma("pool", C.xsT[fc * 128:(fc + 1) * 128, t0:t0 + nt], xs[:, :nt], reads=[sk],
                  writes=[("xsT", blk, fc)], group=("shsd", xi))


def bc3(ap, n):
    return ap.rearrange("p (o m) -> p o m", o=1).to_broadcast([128, n, ap.shape[1]])


def mixer_consts(C):
    p, A, AR, AC = C.p, C.A, C.AR, C.AC
    M = Ctx()
    C.M = M
    cm = AC.f32(7 * 128).rearrange("p (a b) -> p a b", b=128)
    p.dma("sp", cm, C.cmat, writes=["cmat"])
    M.ident = AR.f32(128)
    M.identf = cm[:, 0, :]
    p.op("act", lambda e: e.activation(out=M.ident, in_=cm[:, 0, :], func=AF.Copy), reads=["cmat"], writes=["ident"])
    M.ml, M.mu, M.mui, M.bones = cm[:, 1, :], cm[:, 2, :], cm[:, 3, :], cm[:, 4, :]
    M.rm01 = cm[:, 6, 0:1]
    M.segm = A.f32(512)
    p.op("dve", lambda e: e.tensor_copy(out=M.segm.rearrange("p (c t) -> p c t", t=64), in_=bc3(cm[:, 5, 0:64], 8)),
         reads=["cmat"], writes=["segm"])
    M.gup = A.bf16(1024)
    p.dma("pool", M.gup, C.g_up, writes=["gup"])
    M.par = {}
    for nm in ["k_k", "k_a", "r_k"]:
        o, w = VC[nm]
        t = A.f32(512)
        p.op("dve", lambda e, t=t, o=o: e.tensor_copy(
            out=t.rearrange("p (c t) -> p c t", t=64),
            in_=C.vt[:, o:o + 8].rearrange("p (c o) -> p c o", o=1).to_broadcast([128, 8, 64])),
            reads=["vecs"], writes=[("par", nm)])
        M.par[nm] = t
    t = A.f32(512)
    p.op("dve", lambda e, t=t: e.tensor_scalar(out=t, in0=M.par["k_a"], scalar1=-1.0, scalar2=1.0, op0=ALU.mult,
                                               op1=ALU.add), reads=[("par", "k_a")], writes=[("par", "omka")])
    M.par["omka"] = t
    M.eps30 = AC.f32(2)[:, 0:1]
    p.op("pool", lambda e: e.memset(M.eps30, 1e-30), writes=["eps30"])
    M.epsgn = AC.f32(2)[:, 0:1]
    p.op("pool", lambda e: e.memset(M.epsgn, GN_EPS), writes=["epsgn"])


def mixer_pass(C, d):
    p, A, AR, M = C.p, C.A, C.AR, C.M
    rev = (d == 1)
    c0 = float(np.exp(-0.5))
    W = Ctx()
    lw = A.bf16(2 * 1024).rearrange("p (w f) -> p w f", w=2)
    p.dma("pool", lw, C.lora[:, d], writes=["lw"])
    par = dict(M.par)
    for nm in ["w0", "a0"]:
        o, w = VC[nm]
        t = A.f32(512)
        p.op("dve", lambda e, t=t, o=o: e.tensor_copy(
            out=t.rearrange("p (c t) -> p c t", t=64),
            in_=C.vt[:, o + 8 * d:o + 8 * d + 8].rearrange("p (c o) -> p c o", o=1).to_broadcast([128, 8, 64])),
            reads=["vecs"], writes=[("par", nm)])
        par[nm] = t
    f3 = lambda n=512: A.f32(n)
    big = lambda dt=None: (AR if dt is F32R else A).f32(1024).rearrange("p (c m) -> p c m", m=128)
    v3 = lambda ap: ap.rearrange("p (c t) -> p c t", t=64)
    EX = [dict(Ax=big(F32R), Rx=big(F32R), b2=big(F32R), k2=big(F32R), v2=big(F32R), gam=A.f32(8))
          for _ in range(3)]
    PR = [dict(AakT=big(F32R), RtF=big(F32R)) for _ in range(2)]
    RbT, RkT = big(F32R), big(F32R)
    Bstk, Kstk, Vstk = big(F32R), big(F32R), big(F32R)
    S = [big(F32R), big(F32R)]
    St = [big(F32R), big(F32R)]
    RtT = big(F32R)
    Pt, Ut = big(F32R), big(F32R)
    H = [big(F32R), big(F32R)]
    Hm = [big(), big()]
    yT = f3()
    ldn = dict(r=f3(), k=f3(), v=f3(), x24=A.f32(64), x25=A.f32(64))
    if rev:
        ldr = dict(r=f3(), k=f3(), v=f3(), x24=A.f32(64))
        ld = dict(ldr)
        ld["x25"] = ldn["x25"]
        LK = "ldr"
    else:
        ld = ldn
        LK = "ld"
    lo24 = A.bf16(64)
    sg25 = A.bf16(64)
    tmp = {n: f3() for n in ["s", "ic", "kk", "nkk", "kd", "bb", "rk", "sR", "cs", "csm", "Er", "Eb", "t"]}
    gst = f3()
    for hi in range(2):
        p.op("act", lambda e, h=H[hi]: e.activation(
            out=h, in_=M.bones.rearrange("p (o m) -> p o m", o=1).to_broadcast([128, 8, 128]), func=AF.Copy,
            scale=0.0), reads=["cmat"], writes=[("H", hi)])
        p.op("pool", lambda e, h=Hm[hi]: e.memset(h, 0.0), writes=[("Hm", hi, 0), ("Hm", hi, 1)])
    chunks = [(T + 64 * i, True) for i in range(4)] + [(64 * i, False) for i in range(64)]
    if rev:
        chunks = [(T + 64 * i, True) for i in reversed(range(4))] + [(64 * i, False) for i in reversed(range(64))]
    NCH = len(chunks)
    st = {"ps": 0}
    bm3 = M.bones.rearrange("p (h t) -> p h t", t=64)
    bm4 = M.bones.rearrange("p (o h t) -> p o h t", o=1, t=64).to_broadcast([128, 8, 2, 64])

    def R3(ap):
        a = v3(ap)
        return a[:, :, ::-1] if rev else a

    def nextps():
        i = st["ps"] % 8
        st["ps"] += 1
        return C.ps[i], ("ps", i)

    def mm8(lhs_fn, rhs_fn, reads, n=128, extra=None):
        terms = [(lhs_fn, rhs_fn)] + (extra or [])
        out = []
        for g in range(2):
            ps, pk = nextps()
            for c4 in range(4):
                cc = 4 * g + c4
                for ti, (lf, rf) in enumerate(terms):
                    p.op("pe", lambda e, ps=ps, c4=c4, cc=cc, lf=lf, rf=rf, ti=ti: e.matmul(
                        ps[:, c4 * n:(c4 + 1) * n], lf(cc), rf(cc), start=(ti == 0), stop=(ti == len(terms) - 1)),
                        reads=reads, writes=[pk], sig=(ti == len(terms) - 1))
            out.append((ps, pk))
        return out

    def grp(t, g):
        return t[:, 4 * g:4 * g + 4, :]

    def psv(ps, n=128):
        return ps[:, 0:4 * n].rearrange("p (c m) -> p c m", m=n)

    def prep(ci):
        col, is_ctx = chunks[ci]
        ex = EX[ci % 3]
        ek = ("ex", ci % 3)
        blk = (col // 512) if not is_ctx else 8
        xkeys = lambda f0: [("xsT", blk, f0 + c) for c in range(8)]
        for nm, r0 in (("r", 0), ("k", 1024), ("v", 2048)):
            p.dma("sp", v3(ldn[nm]), C.xsT[r0:r0 + 1024, col:col + 64].rearrange("(c p) t -> p c t", p=128),
                  reads=xkeys(r0 // 128), writes=[("ld", nm)])
        p.dma("sp", ldn["x24"], C.xsT[3072:3200, col:col + 64], reads=[("xsT", blk, 24)], writes=[("ld", "x24")])
        if rev:
            for nm in ("r", "k", "v"):
                p.op("pool", lambda e, nm=nm: e.tensor_copy(out=v3(ldr[nm]), in_=v3(ldn[nm])[:, :, ::-1]),
                     reads=[("ld", nm)], writes=[("ldr", nm)])
            p.op("pool", lambda e: e.tensor_copy(out=ldr["x24"], in_=ldn["x24"][:, ::-1]),
                 reads=[("ld", "x24")], writes=[("ldr", "x24")])
        p.op("act", lambda e: e.activation(out=lo24[0:64, :], in_=ld["x24"][0:64, :], func=AF.Tanh),
             reads=[(LK, "x24")], writes=["lo24a"])
        p.op("dve", lambda e: e.tensor_copy(out=lo24[64:128, :], in_=ld["x24"][64:128, :]),
             reads=[(LK, "x24")], writes=["lo24b"])
        yield
        for nm, wi, pn in (("s", 0, "w0"), ("ic", 1, "a0")):
            pss = mm8(lambda cc, wi=wi: lw[:, wi, cc * 128:(cc + 1) * 128], lambda cc: lo24[:, :],
                      ["lw", "lo24a", "lo24b"], n=64)
            for g, (ps, pk) in enumerate(pss):
                dst = tmp[nm][:, 256 * g:256 * g + 256]
                p.op("dve", lambda e, ps=ps, dst=dst, g=g, pn=pn: e.tensor_tensor(
                    out=dst, in0=ps[:, 0:256], in1=par[pn][:, 256 * g:256 * g + 256], op=ALU.add),
                    reads=[pk, ("par", pn)], writes=[("tmp", nm, g)])
            p.op("act", lambda e, nm=nm: e.activation(out=tmp[nm], in_=tmp[nm], func=AF.Sigmoid),
                 reads=[("tmp", nm, 0), ("tmp", nm, 1)], writes=[("tmp", nm)])
        yield
        p.op("dve", lambda e: e.tensor_tensor(out=tmp["kk"], in0=ld["k"], in1=par["k_k"], op=ALU.mult),
             reads=[(LK, "k"), ("par", "k_k")], writes=[("tmp", "kk")])
        p.op("act", lambda e: e.activation(out=tmp["t"], in_=tmp["kk"], func=AF.Square),
             reads=[("tmp", "kk")], writes=[("tmp", "t")])
        pss = mm8(lambda cc: M.bones, lambda cc: tmp["t"][:, cc * 64:(cc + 1) * 64], ["cmat", ("tmp", "t")], n=64)
        for g, (ps, pk) in enumerate(pss):
            dst = tmp["nkk"][:, 256 * g:256 * g + 256]
            p.op("act", lambda e, ps=ps, dst=dst: e.activation(out=dst, in_=ps[:, 0:256], func=AF.Ln, bias=M.eps30,
                                                              scale=1.0),
                 reads=[pk, "eps30"], writes=[("tmp", "nkk", g)])
        p.op("act", lambda e: e.activation(out=tmp["nkk"], in_=tmp["nkk"], func=AF.Exp, scale=-0.5),
             reads=[("tmp", "nkk", 0), ("tmp", "nkk", 1)], writes=[("tmp", "nkk")])
        p.op("dve", lambda e: e.tensor_tensor(out=tmp["kk"], in0=tmp["kk"], in1=tmp["nkk"], op=ALU.mult),
             reads=[("tmp", "kk"), ("tmp", "nkk")], writes=[("tmp", "kk")])
        p.op("act", lambda e: e.mul(out=tmp["nkk"], in_=tmp["kk"], mul=-1.0),
             reads=[("tmp", "kk")], writes=[("tmp", "nkk")])
        yield
        p.op("dve", lambda e: e.tensor_tensor(out=tmp["kd"], in0=tmp["ic"], in1=par["k_a"], op=ALU.mult),
             reads=[("tmp", "ic"), ("par", "k_a")], writes=[("tmp", "kd")])
        p.op("pool", lambda e: e.tensor_tensor(out=tmp["kd"], in0=tmp["kd"], in1=par["omka"], op=ALU.add),
             reads=[("tmp", "kd"), ("par", "omka")], writes=[("tmp", "kd")])
        p.op("pool", lambda e: e.tensor_tensor(out=tmp["kd"], in0=tmp["kd"], in1=ld["k"], op=ALU.mult),
             reads=[("tmp", "kd"), (LK, "k")], writes=[("tmp", "kd")])
        p.op("pool", lambda e: e.tensor_tensor(out=tmp["bb"], in0=tmp["kk"], in1=tmp["ic"], op=ALU.mult),
             reads=[("tmp", "kk"), ("tmp", "ic")], writes=[("tmp", "bb")])
        yield
        if not is_ctx:
            p.op("pool", lambda e: e.tensor_tensor(out=tmp["rk"], in0=ld["r"], in1=par["r_k"], op=ALU.mult),
                 reads=[(LK, "r"), ("par", "r_k")], writes=[("tmp", "rk")])
            p.op("pool", lambda e: e.tensor_tensor(out=tmp["rk"], in0=tmp["rk"], in1=tmp["kd"], op=ALU.mult),
                 reads=[("tmp", "rk"), ("tmp", "kd")], writes=[("tmp", "rk")])
            pss = mm8(lambda cc: M.bones, lambda cc: tmp["rk"][:, cc * 64:(cc + 1) * 64], ["cmat", ("tmp", "rk")],
                      n=64)
            for g, (ps, pk) in enumerate(pss):
                go = v3(gst)[:, 4 * g:4 * g + 4, :]
                if rev:
                    go = go[:, :, ::-1]
                p.op("dve", lambda e, ps=ps, g=g, go=go: e.tensor_tensor(
                    out=go, in0=ps[:, 0:256].rearrange("p (c t) -> p c t", t=64),
                    in1=v3(ld["v"])[:, 4 * g:4 * g + 4, :], op=ALU.mult), reads=[pk, (LK, "v")],
                    writes=[("gst", g)])
            p.dma("pool", C.bonT[d][:, col:col + 64].rearrange("(c p) t -> p c t", p=128), v3(gst),
                  reads=[("gst", 0), ("gst", 1)], writes=[("bonT", d, ci)], group="gstd")
            if d == 0:
                p.dma("sp", ld["x25"], C.xsT[3200:3328, col:col + 64], reads=[("xsT", blk, 25)],
                      writes=[("ld", "x25")])
                p.op("act", lambda e: e.activation(out=sg25, in_=ld["x25"], func=AF.Sigmoid),
                     reads=[("ld", "x25")], writes=["sg25"])
                pss = mm8(lambda cc: M.gup[:, cc * 128:(cc + 1) * 128], lambda cc: sg25[:, :], ["gup", "sg25"], n=64)
                for g, (ps, pk) in enumerate(pss):
                    p.op("act", lambda e, ps=ps, g=g: e.activation(out=gst[:, 256 * g:256 * g + 256],
                                                                   in_=ps[:, 0:256], func=AF.Copy),
                         reads=[pk], writes=[("gst", g)])
                p.dma("pool", C.gTs[:, col:col + 64].rearrange("(c p) t -> p c t", p=128), v3(gst),
                      reads=[("gst", 0), ("gst", 1)], writes=[("gTs", ci)], group="gstd")
            yield
        sR, sRk = tmp["s"], ("tmp", "s")
        p.op("dve", lambda e: e.tensor_tensor_scan(out=tmp["cs"], data0=M.segm, data1=sR, initial=0.0,
                                                   op0=ALU.mult, op1=ALU.add),
             reads=[sRk, "segm"], writes=[("tmp", "cs")])
        p.op("act", lambda e: e.activation(out=tmp["Er"], in_=tmp["cs"], func=AF.Exp, scale=-c0),
             reads=[("tmp", "cs")], writes=[("tmp", "Er")])
        p.op("act", lambda e: e.activation(out=tmp["Eb"], in_=tmp["cs"], func=AF.Exp, scale=c0),
             reads=[("tmp", "cs")], writes=[("tmp", "Eb")])
        p.op("pool", lambda e: e.tensor_tensor(out=tmp["csm"], in0=tmp["cs"], in1=sR, op=ALU.subtract),
             reads=[("tmp", "cs"), sRk], writes=[("tmp", "csm")])
        p.op("act", lambda e: e.activation(out=tmp["csm"], in_=tmp["csm"], func=AF.Exp, scale=-c0),
             reads=[("tmp", "csm")], writes=[("tmp", "csm")])
        yield
        e4 = lambda t: t.rearrange("p c (h t) -> p c h t", t=64)
        b4 = lambda ap: v3(ap).rearrange("p c (o t) -> p c o t", o=1).to_broadcast([128, 8, 2, 64])
        p.op("pool", lambda e: e.tensor_tensor(out=v3(tmp["t"]), in0=v3(tmp["nkk"]), in1=v3(tmp["csm"]), op=ALU.mult),
             reads=[("tmp", "nkk"), ("tmp", "csm")], writes=[("tmp", "t")])
        p.op("pool", lambda e: e.tensor_tensor(out=e4(ex["Ax"]), in0=b4(tmp["t"]), in1=bm4, op=ALU.mult),
             reads=[("tmp", "t"), "cmat"], writes=[ek + ("Ax",)])
        b4r = b4
        p.op("dve", lambda e: e.tensor_tensor(out=e4(ex["b2"]), in0=b4r(tmp["bb"]), in1=b4(tmp["Eb"]), op=ALU.mult),
             reads=[("tmp", "bb"), ("tmp", "Eb")], writes=[ek + ("b",)])
        p.op("pool", lambda e: e.tensor_tensor(out=e4(ex["k2"]), in0=b4r(tmp["kd"]), in1=b4(tmp["Eb"]), op=ALU.mult),
             reads=[("tmp", "kd"), ("tmp", "Eb")], writes=[ek + ("k",)])
        p.op("dve", lambda e: e.tensor_copy(out=e4(ex["v2"]), in_=b4r(ld["v"])), reads=[(LK, "v")],
             writes=[ek + ("v",)])
        p.op("pool", lambda e: e.tensor_copy(out=ex["gam"], in_=v3(tmp["Er"])[:, :, 63]),
             reads=[("tmp", "Er")], writes=[ek + ("gam",)])
        if not is_ctx:
            p.op("pool", lambda e: e.tensor_tensor(out=v3(tmp["t"]), in0=v3(ld["r"]), in1=v3(tmp["Er"]), op=ALU.mult),
                 reads=[(LK, "r"), ("tmp", "Er")], writes=[("tmp", "t")])
            p.op("pool", lambda e: e.tensor_tensor(out=e4(ex["Rx"]), in0=b4(tmp["t"]), in1=bm4, op=ALU.mult),
                 reads=[("tmp", "t"), "cmat"], writes=[ek + ("Rx",)])
        yield

    def stage_ab(ci):
        col, is_ctx = chunks[ci]
        ex = EX[ci % 3]
        ek = ("ex", ci % 3)
        pr = PR[ci % 2]
        qk = ("pr", ci % 2)
        Ax = lambda cc: ex["Ax"][:, cc, :]
        Bb = lambda cc: ex["b2"][:, cc, :]
        Kb = lambda cc: ex["k2"][:, cc, :]

        def masked(pss, dst, mask, dk):
            for g, (ps, pk) in enumerate(pss):
                p.op("dve", lambda e, ps=ps, g=g: e.tensor_tensor(
                    out=grp(dst, g), in0=psv(ps), in1=mask.rearrange("p (o m) -> p o m", o=1).to_broadcast(
                        [128, 4, 128]), op=ALU.mult), reads=[pk, "cmat"], writes=[dk + (g,)])

        masked(mm8(Ax, Bb, [ek + ("Ax",), ek + ("b",)]), S[0], M.ml, ("S", 0))
        yield
        masked(mm8(Bb, Ax, [ek + ("Ax",), ek + ("b",)]), St[0], M.mu, ("St", 0))
        yield
        masked(mm8(Kb, Ax, [ek + ("Ax",), ek + ("k",)]), pr["AakT"], M.mu, qk + ("AakT",))
        rts = [RtT, pr["RtF"]]
        rtk = [("RtT",), qk + ("RtF",)]
        for g in range(2):
            p.op("pool", lambda e, g=g: e.tensor_tensor(
                out=grp(rts[0], g), in0=grp(St[0], g),
                in1=M.identf.rearrange("p (o m) -> p o m", o=1).to_broadcast([128, 4, 128]), op=ALU.add),
                reads=[("St", 0, g), "ident"], writes=[rtk[0] + (g,)])
        yield
        for j in range(1, 6):
            a, b = (j - 1) % 2, j % 2
            sk_prev = [("S", a, 0), ("S", a, 1), ("St", a, 0), ("St", a, 1)]
            pss = mm8(lambda cc, a=a: St[a][:, cc, :], lambda cc, a=a: S[a][:, cc, :], sk_prev)
            for g, (ps, pk) in enumerate(pss):
                p.op("act", lambda e, ps=ps, g=g, b=b: e.activation(out=grp(S[b], g), in_=psv(ps), func=AF.Copy),
                     reads=[pk], writes=[("S", b, g)])
            yield
            if j < 5:
                pss = mm8(lambda cc, a=a: S[a][:, cc, :], lambda cc, a=a: St[a][:, cc, :], sk_prev)
                for g, (ps, pk) in enumerate(pss):
                    p.op("act", lambda e, ps=ps, g=g, b=b: e.activation(out=grp(St[b], g), in_=psv(ps), func=AF.Copy),
                         reads=[pk], writes=[("St", b, g)])
                yield
            src, dst = rts[(j - 1) % 2], rts[j % 2]
            srck, dstk = rtk[(j - 1) % 2], rtk[j % 2]
            pss = mm8(lambda cc, b=b: S[b][:, cc, :], lambda cc, src=src: src[:, cc, :],
                      [("S", b, 0), ("S", b, 1), srck + (0,), srck + (1,)])
            for g, (ps, pk) in enumerate(pss):
                p.op("dve", lambda e, ps=ps, g=g, src=src, dst=dst: e.tensor_tensor(
                    out=grp(dst, g), in0=psv(ps), in1=grp(src, g), op=ALU.add),
                    reads=[pk, srck + (g,)], writes=[dstk + (g,)])
            yield

    def stage_c(ci):
        col, is_ctx = chunks[ci]
        ex = EX[ci % 3]
        ek = ("ex", ci % 3)
        pr = PR[ci % 2]
        qk = ("pr", ci % 2)
        Hc, Hn = H[ci % 2], H[(ci + 1) % 2]
        hk, hnk = ("H", ci % 2), ("H", (ci + 1) % 2)
        ident = lambda cc: M.ident
        two = lambda k: [k + (0,), k + (1,)]
        bonesb = M.bones.rearrange("p (o m) -> p o m", o=1).to_broadcast([128, 4, 128])

        def masked_ev(pss, dst, dk, mask=None):
            for g, (ps, pk) in enumerate(pss):
                p.op("dve", lambda e, ps=ps, g=g: e.tensor_tensor(
                    out=grp(dst, g), in0=psv(ps),
                    in1=(bonesb if mask is None else mask.rearrange("p (o m) -> p o m", o=1).to_broadcast(
                        [128, 4, 128])), op=ALU.mult), reads=[pk, "cmat"], writes=[dk + (g,)])

        masked_ev(mm8(lambda cc: ex["v2"][:, cc, :], ident, [ek + ("v",), "ident"]), Vstk, ("Vstk",))
        yield
        pss = mm8(lambda cc: ex["Ax"][:, cc, :], lambda cc: Hc[:, cc, :],
                  [ek + ("Ax",), hk] + two(qk + ("AakT",)) + two(("Vstk",)),
                  extra=[(lambda cc: pr["AakT"][:, cc, :], lambda cc: Vstk[:, cc, :])])
        for g, (ps, pk) in enumerate(pss):
            p.op("act", lambda e, ps=ps, g=g: e.activation(out=grp(Pt, g), in_=psv(ps), func=AF.Copy),
                 reads=[pk], writes=[("Pt", g)])
        yield
        masked_ev(mm8(lambda cc: ex["b2"][:, cc, :], ident, [ek + ("b",), "ident"]), Bstk, ("Bstk",))
        masked_ev(mm8(lambda cc: ex["k2"][:, cc, :], ident, [ek + ("k",), "ident"]), Kstk, ("Kstk",))
        if not is_ctx:
            Rx = lambda cc: ex["Rx"][:, cc, :]
            masked_ev(mm8(lambda cc: ex["b2"][:, cc, :], Rx, [ek + ("Rx",), ek + ("b",)]), RbT, ("RbT",), M.mui)
            masked_ev(mm8(lambda cc: ex["k2"][:, cc, :], Rx, [ek + ("Rx",), ek + ("k",)]), RkT, ("RkT",), M.mui)
        yield
        pss = mm8(lambda cc: pr["RtF"][:, cc, :], lambda cc: Pt[:, cc, :], two(qk + ("RtF",)) + two(("Pt",)))
        for g, (ps, pk) in enumerate(pss):
            p.op("dve", lambda e, ps=ps, g=g: e.tensor_copy(out=grp(Ut, g), in_=psv(ps)), reads=[pk],
                 writes=[("Ut", g)])
        yield
        Hmc, Hmn = Hm[ci % 2], Hm[(ci + 1) % 2]
        hmk, hmnk = ("Hm", ci % 2), ("Hm", (ci + 1) % 2)
        pss = mm8(lambda cc: Bstk[:, cc, :], lambda cc: Ut[:, cc, :],
                  two(("Bstk",)) + two(("Kstk",)) + two(("Ut",)) + two(("Vstk",)),
                  extra=[(lambda cc: Kstk[:, cc, :], lambda cc: Vstk[:, cc, :])])
        for g, (ps, pk) in enumerate(pss):
            p.op("dve", lambda e, ps=ps, g=g: e.tensor_tensor(out=grp(Hmn, g), in0=psv(ps), in1=grp(Hmc, g),
                                                             op=ALU.add),
                 reads=[pk, hmk + (g,)], writes=[hmnk + (g,)])
            p.op("pool", lambda e, g=g: e.tensor_tensor(
                out=grp(Hmn, g), in0=grp(Hmn, g),
                in1=ex["gam"][:, 4 * g:4 * g + 4].rearrange("p (c o) -> p c o", o=1).to_broadcast([128, 4, 128]),
                op=ALU.mult), reads=[hmnk + (g,), ek + ("gam",)], writes=[hmnk + (g,)])
            p.op("act", lambda e, g=g: e.activation(out=grp(Hn, g), in_=grp(Hmn, g), func=AF.Copy),
                 reads=[hmnk + (g,)], writes=[hnk])
        yield
        if not is_ctx:
            pss = mm8(lambda cc: Hc[:, cc, :], lambda cc: ex["Rx"][:, cc, :],
                      [hk, ek + ("Rx",)] + two(("Ut",)) + two(("RbT",)) + two(("RkT",)) + two(("Vstk",)),
                      extra=[(lambda cc: Ut[:, cc, :], lambda cc: RbT[:, cc, :]),
                             (lambda cc: Vstk[:, cc, :], lambda cc: RkT[:, cc, :])])
            y3 = v3(yT)
            for g, (ps, pk) in enumerate(pss):
                for hh in range(2):
                    rows = slice(64 * hh, 64 * hh + 64)
                    o = y3[rows, 4 * g:4 * g + 4, :]
                    if rev:
                        o = o[:, :, ::-1]
                    src = psv(ps)[rows, :, 64 * hh:64 * hh + 64]
                    if hh == 0:
                        p.op("dve", lambda e, o=o, src=src: e.tensor_copy(out=o, in_=src), reads=[pk],
                             writes=[("yT", g, hh)])
                    else:
                        p.op("act", lambda e, o=o, src=src: e.activation(out=o, in_=src, func=AF.Copy), reads=[pk],
                             writes=[("yT", g, hh)])
            p.dma("pool", C.yTs[d][:, col:col + 64].rearrange("(c p) t -> p c t", p=128), y3,
                  reads=[("yT", g, hh) for g in range(2) for hh in range(2)], writes=[("yTs", d, ci)], group="yTd")
        yield

    def drain(*gens):
        gens = [g for g in gens if g is not None]
        while gens:
            for g in list(gens):
                try:
                    next(g)
                except StopIteration:
                    gens.remove(g)

    drain(prep(0))
    drain(stage_ab(0), prep(1))
    for ci in range(NCH):
        drain(stage_c(ci),
              stage_ab(ci + 1) if ci + 1 < NCH else None,
              prep(ci + 2) if ci + 2 < NCH else None)


def rwkv_out_phase(C):
    p, A, M = C.p, C.A, C.M
    L = [{n: A.f32(512) for n in ["y0", "y1", "b0", "b1", "g"]} for _ in range(2)]
    sq = A.f32(512)
    mean, var, rstd = A.f32(512), A.f32(512), A.f32(512)
    ob = [A.bf16(512) for _ in range(2)]
    ogg, _ = VC["gn_g"]
    ogb, _ = VC["gn_b"]
    n = 0
    for ti in range(8):
        t0 = ti * 512
        cks = list(range(8 * ti + 4, 8 * ti + 12)) if False else None
        for cc in range(8):
            l = L[n % 2]
            lk = lambda nm, n=n: ("rl", n % 2, nm)
            rows = slice(cc * 128, (cc + 1) * 128)
            for nm, src, dep in (("y0", C.yTs[0], "yTs0"), ("y1", C.yTs[1], "yTs1"), ("b0", C.bonT[0], "bon0"),
                                 ("b1", C.bonT[1], "bon1"), ("g", C.gTs, "gTs")):
                p.dma("sp", l[nm], src[rows, t0:t0 + 512], reads=["mixdone"], writes=[lk(nm)])
            p.op("pool", lambda e, l=l: e.tensor_tensor(out=l["y0"], in0=l["y0"], in1=l["y1"], op=ALU.add),
                 reads=[lk("y0"), lk("y1")], writes=[lk("y0")])
            p.op("pool", lambda e, l=l: e.tensor_tensor(out=l["b0"], in0=l["b0"], in1=l["b1"], op=ALU.add),
                 reads=[lk("b0"), lk("b1")], writes=[lk("b0")])
            p.op("act", lambda e, l=l: e.activation(out=sq, in_=l["y0"], func=AF.Square), reads=[lk("y0")],
                 writes=["rsq"])
            p.op("pe", lambda e, l=l: e.matmul(C.ps[0][:, :], M.bones, l["y0"], start=True, stop=True),
                 reads=["cmat", lk("y0")], writes=[("ps", 0)])
            p.op("pe", lambda e: e.matmul(C.ps[1][:, :], M.bones, sq, start=True, stop=True),
                 reads=["cmat", "rsq"], writes=[("ps", 1)])
            inv = 1.0 / 64
            p.op("act", lambda e: e.activation(out=mean, in_=C.ps[0][:, :], func=AF.Copy, scale=inv),
                 reads=[("ps", 0)], writes=["rmean"])
            p.op("act", lambda e: e.activation(out=var, in_=C.ps[0][:, :], func=AF.Square, scale=inv),
                 reads=[("ps", 0)], writes=["rvar"])
            p.op("dve", lambda e: e.scalar_tensor_tensor(out=var, in0=C.ps[1][:, :], scalar=inv, in1=var,
                                                         op0=ALU.mult, op1=ALU.subtract),
                 reads=[("ps", 1), "rvar"], writes=["rvar"])
            p.op("act", lambda e: e.activation(out=rstd, in_=var, func=AF.Ln, bias=M.epsgn, scale=1.0),
                 reads=["rvar", "epsgn"], writes=["rrstd"])
            p.op("act", lambda e: e.activation(out=rstd, in_=rstd, func=AF.Exp, scale=-0.5), reads=["rrstd"],
                 writes=["rrstd"])
            p.op("pool", lambda e, l=l: e.tensor_tensor(out=l["y0"], in0=l["y0"], in1=mean, op=ALU.subtract),
                 reads=[lk("y0"), "rmean"], writes=[lk("y0")])
            p.op("dve", lambda e, l=l: e.tensor_tensor(out=l["y0"], in0=l["y0"], in1=rstd, op=ALU.mult),
                 reads=[lk("y0"), "rrstd"], writes=[lk("y0")])
            p.op("act", lambda e, l=l, cc=cc: e.activation(out=l["y0"], in_=l["y0"], func=AF.Identity,
                                                           scale=C.vt[:, ogg + cc:ogg + cc + 1],
                                                           bias=C.vt[:, ogb + cc:ogb + cc + 1]),
                 reads=[lk("y0"), "vecs"], writes=[lk("y0")])
            p.op("pool", lambda e, l=l: e.tensor_tensor(out=l["y0"], in0=l["y0"], in1=l["b0"], op=ALU.add),
                 reads=[lk("y0"), lk("b0")], writes=[lk("y0")])
            o = ob[n % 2]
            p.op("dve", lambda e, l=l, o=o: e.tensor_tensor(out=o, in0=l["y0"], in1=l["g"], op=ALU.mult),
                 reads=[lk("y0"), lk("g")], writes=[("rob", n % 2)])
            p.dma("pool", C.catT[rows, t0:t0 + 512], o, reads=[("rob", n % 2)], writes=[("catr", ti, cc)],
                  group=("robd", n % 2))
            n += 1


def wout_phase(C):
    p, A = C.p, C.A
    wo = r3(A.bf16(16 * 2048), b=2048)
    cat = [r3(A.bf16(16 * 512), b=512) for _ in range(2)]
    z = r3(A.f32(16 * 512), b=512)
    sq = [A.f32(512) for _ in range(2)]
    mean, var, rstd = A.f32(512), A.f32(512), A.f32(512)
    ost = [A.f32(512) for _ in range(2)]
    wv = C.w_out_b.rearrange("(kc p) f -> p kc f", p=128)
    for h in range(4):
        p.dma("sp", wo[:, 4 * h:4 * h + 4, :], wv[:, 4 * h:4 * h + 4, :], reads=C.cast_keys["w_out"],
              writes=[("wo", h)])
    wok = [("wo", h) for h in range(4)]
    shift, sc1p, gate, mk = mod_aps(C, "x", 1)
    og, _ = VC["ln_g"]
    ol, _ = VC["ln_b"]
    lnj = 1
    nt = 512
    oidx = 0
    for ti in range(8):
        t0 = ti * 512
        ct = cat[ti % 2]
        p.dma("sp", ct, C.catT[:, t0:t0 + 512].rearrange("(kc p) t -> p kc t", p=128),
              reads=[("catg", ti)] + [("catr", ti, cc) for cc in range(8)], writes=[("cat", ti % 2)])
        pend = None

        def stats(oc):
            p.op("pe", lambda e: e.matmul(C.ps[6][:, :], C.ones, z[:, oc, :], start=(oc == 0), stop=(oc == 15)),
                 reads=["ones", ("z", oc)], writes=[("ps", 6)], sig=(oc == 15))
            p.op("pe", lambda e: e.matmul(C.ps[7][:, :], C.ones, sq[oc % 2], start=(oc == 0), stop=(oc == 15)),
                 reads=["ones", ("sq", oc % 2)], writes=[("ps", 7)], sig=True)

        for oc in range(16):
            p.dma("pool", z[:, oc, :], C.x1T[oc * 128:(oc + 1) * 128, t0:t0 + 512],
                  reads=[("dst", "fa", ti, oc)], writes=[("z", oc)])
            ps = C.ps[4 + oc % 2]
            for kc in range(16):
                p.op("pe", lambda e, ps=ps, kc=kc, oc=oc, ct=ct: e.matmul(
                    ps[:, :], wo[:, kc, oc * 128:(oc + 1) * 128], ct[:, kc, :], start=(kc == 0), stop=(kc == 15)),
                    reads=(wok + [("cat", ti % 2)]) if kc == 0 else [], writes=[("ps", 4 + oc % 2)], sig=(kc == 15))
            if pend is not None:
                stats(pend)
            p.op("dve", lambda e, ps=ps, oc=oc: e.scalar_tensor_tensor(
                out=z[:, oc, :], in0=ps[:, :], scalar=gate[:, oc:oc + 1], in1=z[:, oc, :],
                op0=ALU.mult, op1=ALU.add), reads=[("ps", 4 + oc % 2)] + mk, writes=[("z", oc)])
            p.op("act", lambda e, oc=oc: e.activation(out=sq[oc % 2], in_=z[:, oc, :], func=AF.Square),
                 reads=[("z", oc)], writes=[("sq", oc % 2)])
            pend = oc
        stats(pend)
        inv = 1.0 / D
        p.op("act", lambda e: e.activation(out=mean, in_=C.ps[6][:, :], func=AF.Copy, scale=inv),
             reads=[("ps", 6)], writes=["mean"])
        p.op("act", lambda e: e.activation(out=var, in_=C.ps[6][:, :], func=AF.Square, scale=inv),
             reads=[("ps", 6)], writes=["var"])
        p.op("dve", lambda e: e.scalar_tensor_tensor(out=var, in0=C.ps[7][:, :], scalar=inv, in1=var,
                                                     op0=ALU.mult, op1=ALU.subtract),
             reads=[("ps", 7), "var"], writes=["var"])
        p.op("act", lambda e: e.activation(out=rstd, in_=var, func=AF.Ln, bias=C.epsln, scale=1.0),
             reads=["var", "epsc"], writes=["rstd"])
        p.op("act", lambda e: e.activation(out=rstd, in_=rstd, func=AF.Exp, scale=-0.5), reads=["rstd"],
             writes=["rstd"])
        for oc in range(16):
            p.op("pool", lambda e, oc=oc: e.tensor_tensor(out=z[:, oc, :], in0=z[:, oc, :], in1=mean, op=ALU.subtract),
                 reads=[("z", oc), "mean"], writes=[("z", oc)])
            p.op("dve", lambda e, oc=oc: e.tensor_tensor(out=z[:, oc, :], in0=z[:, oc, :], in1=rstd, op=ALU.mult),
                 reads=[("z", oc), "rstd"], writes=[("z", oc)])
            oi = oidx % 2
            oidx += 1
            ot = ost[oi]
            gcol = C.vt[:, og + lnj * 16 + oc:og + lnj * 16 + oc + 1]
            bcol = C.vt[:, ol + lnj * 16 + oc:ol + lnj * 16 + oc + 1]
            p.op("act", lambda e, oc=oc, ot=ot, gcol=gcol, bcol=bcol: e.activation(
                out=ot, in_=z[:, oc, :], func=AF.Identity, scale=gcol, bias=bcol),
                reads=[("z", oc), "vecs"], writes=[("ost", oi)])
            p.dma("pool", C.x2T[oc * 128:(oc + 1) * 128, t0:t0 + 512], ot,
                  reads=[("ost", oi)], writes=[("dst", "w3", ti, oc)], group=("ostd", oi))


_NC_CACHE = {}


def _pm(v):
    return np.ascontiguousarray(np.asarray(v, np.float32).reshape(-1, 128).T)


def _cmat():
    m = np.zeros((128, 7, 128), np.float32)
    r = np.arange(128)
    hb = (r[:, None] // 64) == (r[None, :] // 64)
    ti, tj = r[:, None] % 64, r[None, :] % 64
    m[:, 0, :] = np.eye(128)
    m[:, 1, :] = hb & (tj < ti)
    m[:, 2, :] = hb & (ti < tj)
    m[:, 3, :] = hb & (ti <= tj)
    m[:, 4, :] = hb
    m[:, 5, :] = 1.0
    m[:, 5, 0] = 0.0
    m[:64, 6, :] = 1.0
    return m


def _lora(inputs):
    m = np.zeros((128, 2, 2, 1024), np.float32)
    m[:64, :, 0, :] = np.transpose(inputs["w_up"][0], (1, 0, 2))
    m[64:, :, 1, :] = np.transpose(inputs["a_up"][0], (1, 0, 2))
    return m


def make_in_maps(inputs):
    x = np.asarray(inputs["x"], np.float32)
    ctx = np.asarray(inputs["ctx"], np.float32)
    maps = []
    shared = {
        "w_ada": np.ascontiguousarray(inputs["w_ada"][0], dtype=np.float32),
        "ffn_a_wi": np.ascontiguousarray(inputs["ffn_a_wi"][0], dtype=np.float32),
        "ffn_a_wo": np.ascontiguousarray(inputs["ffn_a_wo"][0], dtype=np.float32),
        "ffn_b_wi": np.ascontiguousarray(inputs["ffn_b_wi"][0], dtype=np.float32),
        "ffn_b_wo": np.ascontiguousarray(inputs["ffn_b_wo"][0], dtype=np.float32),
        "w_in": np.ascontiguousarray(inputs["w_in"][0], dtype=np.float32),
        "w_out": np.ascontiguousarray(inputs["w_out"][0], dtype=np.float32),
        "gm_ln_g": np.ascontiguousarray(inputs["gm_ln_g"][0], dtype=np.float32),
        "gm_ln_b": np.ascontiguousarray(inputs["gm_ln_b"][0], dtype=np.float32),
        "gm_bs": np.ascontiguousarray(inputs["gm_bs"][0].reshape(-1), dtype=np.float32),
        "gm_wsT": np.ascontiguousarray(np.transpose(inputs["gm_ws"][0], (2, 0, 1)), dtype=np.float32),
        "cmat": _cmat(),
        "lora": _lora(inputs),
        "g_up": np.ascontiguousarray(inputs["g_up"][0], dtype=np.float32),
    }
    for b in range(NCORES):
        vec = np.zeros((128, NV), np.float32)

        def put(name, arr):
            o, w = VC[name]
            assert arr.shape == (128, w), (name, arr.shape)
            vec[:, o:o + w] = arr

        put("c", _pm(inputs["c"][b]))
        put("cctx", _pm(inputs["c_ctx"]))
        put("b_ada", _pm(inputs["b_ada"][0]))
        put("ln_g", _pm(inputs["ln_g"][0].reshape(-1)))
        put("ln_b", _pm(inputs["ln_b"][0].reshape(-1)))
        put("mu", _pm(inputs["mu_shift"][0]))
        put("w0", _pm(inputs["w0"][0].reshape(-1)))
        put("a0", _pm(inputs["a0"][0].reshape(-1)))
        put("k_k", _pm(inputs["k_k"][0]))
        put("k_a", _pm(inputs["k_a"][0]))
        put("r_k", _pm(inputs["r_k"][0].reshape(-1)))
        put("gn_g", _pm(inputs["gn_g"][0]))
        put("gn_b", _pm(inputs["gn_b"][0]))
        put("sel4", (np.arange(128)[:, None] % 4 == np.arange(4)[None, :]).astype(np.float32))
        put("sel2", (np.arange(128)[:, None] % 2 == np.arange(2)[None, :]).astype(np.float32))
        xT = np.empty((D, TT), np.float32)
        xT[:, :T] = x[b].T
        xT[:, T:] = ctx[b].T
        m = dict(shared)
        m["xT"] = xT
        m["vecs"] = vec
        maps.append(m)
    return maps


def kernel(**inputs):
    if "nc" not in _NC_CACHE:
        _NC_CACHE["nc"] = build()
    nc = _NC_CACHE["nc"]
    maps = make_in_maps(inputs)
    res = run_bass_kernel_spmd(nc, maps, core_ids=list(range(NCORES)))
    out = np.empty((NCORES, T, D), np.float32)
    for b in range(NCORES):
        out[b] = res.results[b]["outT"].T
    return out
```

```python
import numpy as np
from contextlib import ExitStack
import concourse.bass as bass
import concourse.mybir as mybir
from concourse.bass_utils import run_bass_kernel_spmd

F32 = mybir.dt.float32
BF16 = mybir.dt.bfloat16
F32R = mybir.dt.float32r
AF = mybir.ActivationFunctionType
ALU = mybir.AluOpType

D = 2048
T = 4096
CT = 256
TT = T + CT
DFF = 5632
NMOD = 9
RW = 1024
RWIN = 3328
INW = 5376
ALPHA = 2.0 ** 0.25
LN_EPS = 1e-5
GN_EPS = 64e-5
NCORES = 8

VC = {}
_o = 0
for _n, _w in [("c", 16), ("cctx", 16), ("b_ada", 144), ("ln_g", 48), ("ln_b", 48), ("mu", 26),
               ("w0", 16), ("a0", 16), ("k_k", 8), ("k_a", 8), ("r_k", 8), ("gn_g", 8), ("gn_b", 8),
               ("sel4", 4), ("sel2", 2)]:
    VC[_n] = (_o, _w)
    _o += _w
NV = _o

ENGS = ["pe", "act", "dve", "pool", "sp"]


class Tok:
    __slots__ = ("sk", "n")

    def __init__(self, sk):
        self.sk = sk
        self.n = None


class Prog:
    EPOCH = 30000

    def __init__(self, nc, es):
        self.nc = nc
        self.es = es
        self.ops = {e: [] for e in ENGS}
        self.cnt = {}
        self.cur = {}
        self.lastw = {}
        self.readers = {}
        self.semh = {}
        self.latest = {}

    def _tok(self, sk, sig):
        t = self.cur.get(sk)
        if t is None:
            t = Tok(sk)
            self.cur[sk] = t
        if sig:
            self.cnt[sk] = self.cnt.get(sk, 0) + 1
            t.n = self.cnt[sk]
            self.cur[sk] = None
            self.latest[sk] = t
        return t

    def _deps(self, reads, writes, own_group=None):
        deps = []
        for k in reads:
            t = self.lastw.get(k)
            if t is not None:
                deps.append(t)
        for k in writes:
            t = self.lastw.get(k)
            if t is not None and not (own_group is not None and t.sk == own_group):
                deps.append(t)
            r = self.readers.get(k)
            if r:
                deps.extend(r.values())
        return deps

    def _commit(self, tok, reads, writes):
        for k in reads:
            self.readers.setdefault(k, {})[tok.sk] = tok
        for k in writes:
            self.lastw[k] = tok
            self.readers[k] = {}

    def op(self, eng, fn, reads=(), writes=(), sig=True):
        deps = self._deps(reads, writes)
        tok = self._tok(eng, sig)
        self.ops[eng].append((deps, fn, tok if sig else None, 1))
        self._commit(tok, reads, writes)
        return tok

    def dma(self, q, out, in_, reads=(), writes=(), group=None):
        if group is None:
            group = writes[0] if writes else reads[0]
        sk = ("dma", group)
        deps = self._deps(reads, writes, own_group=sk)
        tok = self._tok(sk, True)
        self.ops[q].append((deps, (lambda e, o=out, i=in_: e.dma_start(out=o, in_=i)), tok, 16))
        self._commit(tok, reads, writes)
        return tok

    def barrier(self):
        toks = [t for t in self.latest.values()]
        for e in ENGS:
            self.ops[e].append((list(toks), None, None, 0))

    def _semval(self, tok, per):
        ep = (tok.n - 1) // per
        v = (tok.n - 1) % per + 1
        key = (tok.sk, ep)
        h = self.semh.get(key)
        if h is None:
            h = self.es.enter_context(self.nc.semaphore("s%d" % len(self.semh)))
            self.semh[key] = h
        return key, h, v

    def emit(self, block):
        def per_of(sk):
            return self.EPOCH // 16 if isinstance(sk, tuple) else self.EPOCH

        def run(e, name):
            waited = {}
            for deps, fn, tok, inc in self.ops[name]:
                need = {}
                for t in deps:
                    if t.n is None:
                        raise RuntimeError("unsignaled dependency on %s" % (t.sk,))
                    if t.sk == name and name == "pe":
                        continue
                    key, h, v = self._semval(t, per_of(t.sk))
                    if waited.get(key, 0) >= v:
                        continue
                    if need.get(key, (None, 0))[1] < v:
                        need[key] = (h, v)
                for key, (h, v) in need.items():
                    e.wait_ge(h, v * (16 if isinstance(key[0], tuple) else 1))
                    waited[key] = v
                if fn is None:
                    continue
                ins = fn(e)
                if tok is not None:
                    key, h, v = self._semval(tok, per_of(tok.sk))
                    ins.then_inc(h, inc)

        for name in ENGS:
            for deps, fn, tok, inc in self.ops[name]:
                for t in deps:
                    if t.n is not None:
                        self._semval(t, per_of(t.sk))
                if tok is not None:
                    self._semval(tok, per_of(tok.sk))

        @block.tensor
        def _(e):
            run(e, "pe")

        @block.scalar
        def _(e):
            run(e, "act")

        @block.vector
        def _(e):
            run(e, "dve")

        @block.gpsimd
        def _(e):
            run(e, "pool")

        @block.sync
        def _(e):
            run(e, "sp")


class Carver:
    def __init__(self, pool, nwords):
        self.pool = pool
        self.n = nwords
        self.off = 0

    def mark(self):
        return self.off

    def reset(self, m):
        self.peak = max(getattr(self, "peak", 0), self.off)
        self.off = m

    def f32(self, n, dt=None):
        n = (n + 1) // 2 * 2
        a = self.pool[:, self.off:self.off + n]
        self.off += n
        assert self.off <= self.n, "SBUF pool overflow %d > %d" % (self.off, self.n)
        return a

    def bf16(self, n):
        w = (n + 3) // 4 * 2
        a = self.pool[:, self.off:self.off + w].bitcast(BF16)
        self.off += w
        assert self.off <= self.n, "SBUF pool overflow %d > %d" % (self.off, self.n)
        return a[:, 0:n]


def r3(ap, **kw):
    return ap.rearrange("p (a b) -> p a b", **kw)


class Ctx:
    pass


def build(stage=99):
    nc = bass.Bass("TRN2", target_bir_lowering=False)
    C = Ctx()
    C.nc = nc
    dt_in = lambda n, s: nc.dram_tensor(n, s, F32, kind="ExternalInput").ap()
    C.xT = dt_in("xT", [D, TT])
    C.vecs = dt_in("vecs", [128, NV])
    C.w_ada = dt_in("w_ada", [D, NMOD * D])
    C.wa_i = dt_in("ffn_a_wi", [D, 2 * DFF])
    C.wa_o = dt_in("ffn_a_wo", [DFF, D])
    C.wb_i = dt_in("ffn_b_wi", [D, 2 * DFF])
    C.wb_o = dt_in("ffn_b_wo", [DFF, D])
    C.w_in = dt_in("w_in", [D, INW])
    C.w_out = dt_in("w_out", [D, D])
    C.gm_ln_g = dt_in("gm_ln_g", [1024])
    C.gm_ln_b = dt_in("gm_ln_b", [1024])
    C.gm_bs = dt_in("gm_bs", [2048])
    C.gm_wsT = dt_in("gm_wsT", [128, 16, 128])
    C.cmat = dt_in("cmat", [128, 7, 128])
    C.lora = dt_in("lora", [128, 2, 2, 1024])
    C.g_up = dt_in("g_up", [128, 1024])
    sc = lambda n, s, d: nc.dram_tensor(n, s, d).ap()
    C.wa_i_b = sc("wa_i_b", [D, 2 * DFF], BF16)
    C.wa_o_b = sc("wa_o_b", [DFF, D], BF16)
    C.wb_i_b = sc("wb_i_b", [D, 2 * DFF], BF16)
    C.wb_o_b = sc("wb_o_b", [DFF, D], BF16)
    C.w_in_b = sc("w_in_b", [D, INW], BF16)
    C.w_out_b = sc("w_out_b", [D, D], BF16)
    dbg = lambda n, s, d, st: (nc.dram_tensor("dbg", s, d, kind="ExternalOutput").ap() if stage == st
                                else sc(n, s, d))
    C.x1T = dbg("x1T", [D, TT], F32, 1)
    C.pT = dbg("pT", [RWIN, TT], F32, 2)
    C.catT = dbg("catT", [D, T], BF16, 5)
    C.xsT = dbg("xsT", [RWIN, TT], F32, 3)
    C.yTs = [dbg("yT0", [RW, T], F32, 4), sc("yT1", [RW, T], F32)]
    C.bonT = [sc("bonT0", [RW, T], F32), sc("bonT1", [RW, T], F32)]
    C.gTs = sc("gTs", [RW, T], F32)
    C.x2T = dbg("x2T", [D, T], F32, 6)
    C.outT = nc.dram_tensor("outT", [D, T], F32, kind="ExternalOutput").ap() if stage >= 99 else sc("outT", [D, T], F32)

    with ExitStack() as es:
        NCW = 9 * 256
        cpool = es.enter_context(nc.sbuf_tensor("cpool", [128, NCW], F32))
        C.ps = [es.enter_context(nc.psum_tensor("ps%d" % i, [128, 512], F32)) for i in range(8)]
        p = Prog(nc, es)
        C.p = p
        C.AC = Carver(cpool, NCW)
        st = {"k": 0}

        def run_phase(fn, kib, rkib=0):
            st["k"] += 1
            cm = nc.sbuf_tensor("ph%d" % st["k"], [128, kib * 256], F32)
            C.A = Carver(cm.__enter__(), kib * 256)
            cr = None
            if rkib:
                cr = nc.sbuf_tensor("rp%d" % st["k"], [128, rkib * 512], BF16)
                C.AR = Carver(cr.__enter__(), rkib * 512)
            fn()
            print("phase", st["k"], "pool use KiB", max(C.A.off, getattr(C.A, "peak", 0)) / 256.0,
                  (max(C.AR.off, getattr(C.AR, "peak", 0)) / 512.0 if rkib else 0))
            if cr is not None:
                cr.__exit__(None, None, None)
            cm.__exit__(None, None, None)
            p.barrier()

        def ph0():
            phase_consts(C)
            phase_cast(C)
            phase_mod(C)
        run_phase(ph0, 130)
        tiles = [(512 * i, 512, "x") for i in range(8)] + [(T, CT, "c")]
        run_phase(lambda: ffn_phase(C, "fa", C.xT, C.x1T, tiles, C.wa_i_b, C.wa_o_b, 0, 0), 196)
        if stage >= 2:
            run_phase(lambda: proj_phase(C), 198)
        if stage >= 4:
            def mix():
                mixer_consts(C)
                m1, m2 = C.A.mark(), C.AR.mark()
                for d in range(2 if stage >= 5 else 1):
                    mixer_pass(C, d)
                    C.A.reset(m1)
                    C.AR.reset(m2)
                    p.barrier()
            run_phase(mix, 78, 70)
        if stage >= 5:
            run_phase(lambda: rwkv_out_phase(C), 60)
        if stage >= 6:
            run_phase(lambda: wout_phase(C), 160)
        if stage >= 99:
            tiles_b = [(512 * i, 512, "x") for i in range(8)]
            run_phase(lambda: ffn_phase(C, "fb", C.x2T, C.outT, tiles_b, C.wb_i_b, C.wb_o_b, 2, 2), 196)
        p.barrier()
        block = es.enter_context(nc.Block())
        p.emit(block)
    return nc


def phase_consts(C):
    p, A = C.p, C.AC
    C.vt = A.f32(NV)
    p.dma("sp", C.vt, C.vecs, writes=["vecs"])
    C.ones = A.f32(128)
    C.epsln = A.f32(2)[:, 0:1]
    p.op("pool", lambda e: e.memset(C.epsln, LN_EPS / (ALPHA * ALPHA)), writes=["epsc"])
    p.op("pool", lambda e: e.memset(C.ones, 1.0), writes=["ones"])
    C.epsg = A.f32(2)[:, 0:1]
    p.op("pool", lambda e: e.memset(C.epsg, LN_EPS), writes=["epsc2"])


def vcol(C, name, i=0, n=1):
    o, w = VC[name]
    return C.vt[:, o + i:o + i + n]


def phase_cast(C):
    p = C.p
    C.cast_keys = {}

    def cast(name, src, dst, rows, piece):
        keys = []
        for r0 in range(0, rows, piece):
            k = ("w", name, r0)
            p.dma("pool", dst[r0:r0 + piece, :], src[r0:r0 + piece, :], writes=[k], group=("cast", name))
            keys.append(k)
        C.cast_keys[name] = keys

    C.cast = cast
    C.cast_blk = {}

    def cast_cols(name, src, dst, blocks):
        ks = []
        for bi, cols in enumerate(blocks):
            k = ("w", name, "blk", bi)
            for (c0, c1) in cols:
                p.dma("pool", dst[:, c0:c1], src[:, c0:c1], writes=[k], group=("cast", name, bi % 6))
            ks.append(k)
        C.cast_blk[name] = ks
        C.cast_keys[name] = ks

    C.cast_cols = cast_cols


def phase_mod(C):
    p, A, AC = C.p, C.A, C.AC
    s_bf = r3(AC.bf16(32), b=2)
    ctmp = AC.f32(32)
    C.modx = None
    oc, _ = VC["c"]
    p.op("act", lambda e: e.activation(out=ctmp, in_=C.vt[:, oc:oc + 32], func=AF.Silu),
         reads=["vecs"], writes=["ctmp"])
    p.op("dve", lambda e: e.tensor_copy(out=s_bf[:, :, 0], in_=ctmp[:, 0:16]), reads=["ctmp"], writes=["s_bf0"])
    p.op("dve", lambda e: e.tensor_copy(out=s_bf[:, :, 1], in_=ctmp[:, 16:32]), reads=["ctmp"], writes=["s_bf1"])
    wad = [r3(A.bf16(16 * 2048), b=2048) for _ in range(2)]
    src = C.w_ada.rearrange("(kc p) f -> p kc f", p=128)
    psm = C.ps[0]
    for s in range(NMOD):
        w = wad[s % 2]
        for h in range(2):
            p.dma("pool", w[:, 8 * h:8 * h + 8, :], src[:, 8 * h:8 * h + 8, s * D:(s + 1) * D],
                  writes=[("wad", s % 2)])
        for j in range(16):
            m = s * 16 + j
            for kc in range(16):
                p.op("pe", lambda e, w=w, kc=kc, j=j, m=m: e.matmul(
                    psm[:, 2 * m:2 * m + 2], w[:, kc, j * 128:(j + 1) * 128], s_bf[:, kc, :],
                    start=(kc == 0), stop=(kc == 15)),
                    reads=[("wad", s % 2), "s_bf0", "s_bf1"], writes=["psmod"], sig=(kc == 15))
    C.cast_cols("wa_i", C.wa_i, C.wa_i_b,
                [[(fb * 512, fb * 512 + 512), (DFF + fb * 512, DFF + fb * 512 + 512)] for fb in range(11)])
    C.cast_cols("wa_o", C.wa_o, C.wa_o_b, [[(ob * 256, ob * 256 + 256)] for ob in range(8)])
    C.cast("w_in", C.w_in, C.w_in_b, D, 128)
    C.cast("w_out", C.w_out, C.w_out_b, D, 128)
    C.cast("wb_i", C.wb_i, C.wb_i_b, D, 128)
    C.cast("wb_o", C.wb_o, C.wb_o_b, DFF, 128)
    C.mod = {}
    ob, _ = VC["b_ada"]
    pv = psm[:, 0:288].rearrange("p (m two) -> p m two", two=2)
    for idx, nm in enumerate(["x", "c"]):
        mt = AC.f32(144)
        p.op("dve", lambda e, mt=mt, idx=idx: e.tensor_tensor(out=mt, in0=pv[:, :, idx], in1=C.vt[:, ob:ob + 144],
                                                              op=ALU.add),
             reads=["psmod", "vecs"], writes=[("mod", nm)])
        der = AC.f32(16 * 6)
        for j in range(3):
            p.op("dve", lambda e, der=der, mt=mt, j=j: e.tensor_scalar_add(
                out=der[:, 16 * j:16 * j + 16], in0=mt[:, (3 * j + 1) * 16:(3 * j + 2) * 16], scalar1=1.0),
                reads=[("mod", nm)], writes=[("der", nm, j)])
            gs = (0.5 if j != 1 else 1.0) / ALPHA
            p.op("dve", lambda e, der=der, mt=mt, j=j, gs=gs: e.tensor_scalar_mul(
                out=der[:, 48 + 16 * j:48 + 16 * j + 16], in0=mt[:, (3 * j + 2) * 16:(3 * j + 3) * 16], scalar1=gs),
                reads=[("mod", nm)], writes=[("derg", nm, j)])
        C.mod[nm] = (mt, der)


def mod_aps(C, nm, j):
    mt, der = C.mod[nm]
    shift = mt[:, (3 * j) * 16:(3 * j + 1) * 16]
    sc1p = der[:, 16 * j:16 * j + 16]
    gate = der[:, 48 + 16 * j:48 + 16 * j + 16]
    keys = [("mod", nm), ("der", nm, j), ("derg", nm, j)]
    return shift, sc1p, gate, keys


def ffn_phase(C, tag, src, dst, tiles, wi_b, wo_b, j, lnj):
    p, A, nc = C.p, C.A, C.nc
    W = Ctx()
    W.hT = [r3(A.bf16(16 * 512), b=512) for _ in range(2)]
    W.gT = r3(A.bf16(44 * 512), b=512)
    W.z = r3(A.f32(16 * 512), b=512)
    W.ws = [A.bf16(16384) for _ in range(2)]
    W.xst = [A.f32(512) for _ in range(2)]
    W.sg = [A.f32(512) for _ in range(2)]
    W.sq = [A.f32(512) for _ in range(2)]
    W.mean = A.f32(512)
    W.var = A.f32(512)
    W.rstd = A.f32(512)
    W.ost = [A.f32(512) for _ in range(2)]
    wi_v = wi_b.rearrange("(kc p) f -> p kc f", p=128)
    wo_v = wo_b.rearrange("(kc p) f -> p kc f", p=128)
    wname_i = {"fa": "wa_i", "fb": "wb_i"}[tag]
    wname_o = {"fa": "wa_o", "fb": "wb_o"}[tag]
    wkeys_i = C.cast_keys[wname_i]
    wkeys_o = C.cast_keys[wname_o]
    og, _ = VC["ln_g"]
    ol, _ = VC["ln_b"]
    eps_p = LN_EPS / (ALPHA * ALPHA)
    st = {"w": 0, "q": 0, "x": 0, "o": 0}

    def modulate(ti):
        t0, nt, nm = tiles[ti]
        shift, sc1p, gate, mk = mod_aps(C, nm, j)
        hT = W.hT[ti % 2]
        for dc in range(16):
            xs = W.xst[st["x"] % 2]
            xk = ("xst", st["x"] % 2)
            st["x"] += 1
            p.dma("pool", xs[:, :nt], src[dc * 128:(dc + 1) * 128, t0:t0 + nt],
                  reads=[("dst", "w3", ti, dc)], writes=[xk])
            if dc % 2 == 0:
                p.op("act", lambda e, xs=xs, hT=hT, dc=dc, nt=nt: e.activation(
                    out=hT[:, dc, :nt], in_=xs[:, :nt], func=AF.Identity,
                    scale=sc1p[:, dc:dc + 1], bias=shift[:, dc:dc + 1]),
                    reads=[xk] + mk, writes=[("hT", ti % 2, dc)])
            else:
                p.op("dve", lambda e, xs=xs, hT=hT, dc=dc, nt=nt: e.tensor_scalar(
                    out=hT[:, dc, :nt], in0=xs[:, :nt], scalar1=sc1p[:, dc:dc + 1], scalar2=shift[:, dc:dc + 1],
                    op0=ALU.mult, op1=ALU.add),
                    reads=[xk] + mk, writes=[("hT", ti % 2, dc)])

    def up(ti):
        t0, nt, nm = tiles[ti]
        hT = W.hT[ti % 2]
        hk = [("hT", ti % 2, dc) for dc in range(16)]
        for fb in range(11):
            si = st["w"] % 2
            st["w"] += 1
            slot = W.ws[si].rearrange("p (w k f) -> p w k f", w=2, k=16)
            for w_ in range(2):
                c0 = w_ * DFF + fb * 512
                p.dma("sp", slot[:, w_], wi_v[:, :, c0:c0 + 512],
                      reads=([C.cast_blk[wname_i][fb]] if wname_i in C.cast_blk else wkeys_i), writes=[("ws", si)])
            for jj in range(4):
                q = st["q"]
                st["q"] += 1
                pg, pu = C.ps[2 * (q % 2)], C.ps[2 * (q % 2) + 1]
                for w_, ps in ((0, pg), (1, pu)):
                    for kc in range(16):
                        p.op("pe", lambda e, ps=ps, slot=slot, w_=w_, kc=kc, jj=jj, hT=hT, nt=nt: e.matmul(
                            ps[:, :nt], slot[:, w_, kc, jj * 128:(jj + 1) * 128], hT[:, kc, :nt],
                            start=(kc == 0), stop=(kc == 15)),
                            reads=[("ws", si)] + (hk if kc == 0 else []), writes=[("ps", 2 * (q % 2) + w_)],
                            sig=(kc == 15))
                sg = W.sg[q % 2]
                p.op("act", lambda e, sg=sg, pg=pg, nt=nt: e.activation(out=sg[:, :nt], in_=pg[:, :nt], func=AF.Silu),
                     reads=[("ps", 2 * (q % 2))], writes=[("sg", q % 2)])
                fc = fb * 4 + jj
                p.op("dve", lambda e, sg=sg, pu=pu, fc=fc, nt=nt: e.tensor_tensor(
                    out=W.gT[:, fc, :nt], in0=sg[:, :nt], in1=pu[:, :nt], op=ALU.mult),
                    reads=[("sg", q % 2), ("ps", 2 * (q % 2) + 1)], writes=[("gT", fc)])

    def down(ti):
        t0, nt, nm = tiles[ti]
        shift, sc1p, gate, mk = mod_aps(C, nm, j)
        gk = [("gT", fc) for fc in range(44)]
        pend = None

        def stats(oc, nt=nt):
            p.op("pe", lambda e: e.matmul(C.ps[6][:, :nt], C.ones, W.z[:, oc, :nt], start=(oc == 0), stop=(oc == 15)),
                 reads=["ones", ("z", oc)], writes=[("ps", 6)], sig=(oc == 15))
            p.op("pe", lambda e: e.matmul(C.ps[7][:, :nt], C.ones, W.sq[oc % 2][:, :nt], start=(oc == 0),
                                          stop=(oc == 15)),
                 reads=["ones", ("sq", oc % 2)], writes=[("ps", 7)], sig=True)

        for ob in range(8):
            si = st["w"] % 2
            st["w"] += 1
            slot = W.ws[si][:, 0:44 * 256].rearrange("p (k f) -> p k f", k=44)
            for h in range(2):
                p.dma("sp", slot[:, 22 * h:22 * h + 22, :], wo_v[:, 22 * h:22 * h + 22, ob * 256:(ob + 1) * 256],
                      reads=([C.cast_blk[wname_o][ob]] if wname_o in C.cast_blk else wkeys_o), writes=[("ws", si)])
            for o2 in range(2):
                oc = ob * 2 + o2
                p.dma("pool", W.z[:, oc, :nt], src[oc * 128:(oc + 1) * 128, t0:t0 + nt],
                      reads=[("dst", "w3", ti, oc)], writes=[("z", oc)])
                ps = C.ps[4 + oc % 2]
                for kc in range(44):
                    p.op("pe", lambda e, ps=ps, slot=slot, kc=kc, o2=o2, nt=nt: e.matmul(
                        ps[:, :nt], slot[:, kc, o2 * 128:(o2 + 1) * 128], W.gT[:, kc, :nt],
                        start=(kc == 0), stop=(kc == 43)),
                        reads=[("ws", si)] + (gk if kc == 0 else []), writes=[("ps", 4 + oc % 2)], sig=(kc == 43))
                if pend is not None:
                    stats(pend)
                p.op("dve", lambda e, ps=ps, oc=oc, nt=nt: e.scalar_tensor_tensor(
                    out=W.z[:, oc, :nt], in0=ps[:, :nt], scalar=gate[:, oc:oc + 1], in1=W.z[:, oc, :nt],
                    op0=ALU.mult, op1=ALU.add),
                    reads=[("ps", 4 + oc % 2)] + mk, writes=[("z", oc)])
                p.op("act", lambda e, oc=oc, nt=nt: e.activation(out=W.sq[oc % 2][:, :nt], in_=W.z[:, oc, :nt],
                                                                  func=AF.Square),
                     reads=[("z", oc)], writes=[("sq", oc % 2)])
                pend = oc
        stats(pend)
        inv = 1.0 / D
        p.op("act", lambda e: e.activation(out=W.mean[:, :nt], in_=C.ps[6][:, :nt], func=AF.Copy, scale=inv),
             reads=[("ps", 6)], writes=["mean"])
        p.op("act", lambda e: e.activation(out=W.var[:, :nt], in_=C.ps[6][:, :nt], func=AF.Square, scale=inv),
             reads=[("ps", 6)], writes=["var"])
        p.op("dve", lambda e: e.scalar_tensor_tensor(out=W.var[:, :nt], in0=C.ps[7][:, :nt], scalar=inv,
                                                     in1=W.var[:, :nt], op0=ALU.mult, op1=ALU.subtract),
             reads=[("ps", 7), "var"], writes=["var"])
        p.op("act", lambda e: e.activation(out=W.rstd[:, :nt], in_=W.var[:, :nt], func=AF.Ln, bias=C.epsln, scale=1.0),
             reads=["var", "epsc"], writes=["rstd"])
        p.op("act", lambda e: e.activation(out=W.rstd[:, :nt], in_=W.rstd[:, :nt], func=AF.Exp, scale=-0.5),
             reads=["rstd"], writes=["rstd"])
        for oc in range(16):
            p.op("pool", lambda e, oc=oc: e.tensor_tensor(out=W.z[:, oc, :nt], in0=W.z[:, oc, :nt], in1=W.mean[:, :nt],
                                                          op=ALU.subtract),
                 reads=[("z", oc), "mean"], writes=[("z", oc)])
            p.op("dve", lambda e, oc=oc: e.tensor_tensor(out=W.z[:, oc, :nt], in0=W.z[:, oc, :nt], in1=W.rstd[:, :nt],
                                                         op=ALU.mult),
                 reads=[("z", oc), "rstd"], writes=[("z", oc)])
            oi = st["o"] % 2
            st["o"] += 1
            ot = W.ost[oi]
            gcol = C.vt[:, og + lnj * 16 + oc:og + lnj * 16 + oc + 1]
            bcol = C.vt[:, ol + lnj * 16 + oc:ol + lnj * 16 + oc + 1]
            p.op("act", lambda e, oc=oc, ot=ot, gcol=gcol, bcol=bcol: e.activation(
                out=ot[:, :nt], in_=W.z[:, oc, :nt], func=AF.Identity, scale=gcol, bias=bcol),
                reads=[("z", oc), "vecs"], writes=[("ost", oi)])
            p.dma("act", dst[oc * 128:(oc + 1) * 128, t0:t0 + nt], ot[:, :nt],
                  reads=[("ost", oi)], writes=[("dst", tag, ti, oc)], group=("ostd", oi))

    modulate(0)
    for ti in range(len(tiles)):
        up(ti)
        if ti + 1 < len(tiles):
            modulate(ti + 1)
        down(ti)


def proj_phase(C):
    p, A = C.p, C.A
    W = Ctx()
    W.hT = [r3(A.bf16(16 * 512), b=512) for _ in range(2)]
    W.ws = [r3(A.bf16(16 * 512), b=512) for _ in range(4)]
    W.xst = [A.f32(512) for _ in range(2)]
    W.ost = [A.f32(512) for _ in range(2)]
    W.uT = r3(A.f32(8 * 512), b=512)
    W.vg = [A.f32(1024) for _ in range(4)]
    W.sqv = A.f32(1024)
    W.vn = [A.bf16(1024) for _ in range(4)]
    W.gmo = r3(A.bf16(8 * 512), b=512)
    W.st = [A.f32(16) for _ in range(6)]
    lng = A.f32(1024)
    lnb = A.f32(1024)
    bsb = r3(A.f32(16 * 128), b=128)
    wsT = r3(A.bf16(16 * 128), b=128)
    p.dma("sp", lng, C.gm_ln_g.to_broadcast([128, 1024]) if False else bcast_rows(C.gm_ln_g, 1024), writes=["lng"])
    p.dma("sp", lnb, bcast_rows(C.gm_ln_b, 1024), writes=["lnb"])
    p.dma("sp", bsb, bcast_rows(C.gm_bs, 2048).rearrange("p (a b) -> p a b", b=128), writes=["bsb"])
    p.dma("pool", wsT, C.gm_wsT, writes=["wsT"])
    c128 = A.bf16(128)
    bs_hi = r3(A.bf16(16 * 128), b=128)
    bs_lo = r3(A.bf16(16 * 128), b=128)
    p.op("act", lambda e: e.activation(out=c128, in_=C.ones, func=AF.Copy, scale=1.0 / 128), reads=["ones"],
         writes=["c128"])
    p.op("dve", lambda e: e.tensor_copy(out=bs_hi, in_=bsb), reads=["bsb"], writes=["bs_hi"])
    p.op("dve", lambda e: e.tensor_tensor(out=bs_lo, in0=bsb, in1=bs_hi, op=ALU.subtract), reads=["bsb", "bs_hi"],
         writes=["bs_lo"])
    w_v = C.w_in_b.rearrange("(kc p) f -> p kc f", p=128)
    wkeys = C.cast_keys["w_in"]
    shift, sc1p, gate, mkx = None, None, None, None
    tiles = [(512 * i, 512, "x") for i in range(8)] + [(T, CT, "c")]
    st = {"w": 0, "x": 0, "o": 0, "ps": 0}

    def modulate(ti):
        t0, nt, nm = tiles[ti]
        shift, sc1p, gate, mk = mod_aps(C, nm, 1)
        hT = W.hT[ti % 2]
        for dc in range(16):
            xs = W.xst[st["x"] % 2]
            xk = ("xst", st["x"] % 2)
            st["x"] += 1
            p.dma("pool", xs[:, :nt], C.x1T[dc * 128:(dc + 1) * 128, t0:t0 + nt],
                  reads=[("dst", "fa", ti, dc)], writes=[xk])
            if dc % 2 == 0:
                p.op("act", lambda e, xs=xs, hT=hT, dc=dc, nt=nt, sc1p=sc1p, shift=shift: e.activation(
                    out=hT[:, dc, :nt], in_=xs[:, :nt], func=AF.Identity,
                    scale=sc1p[:, dc:dc + 1], bias=shift[:, dc:dc + 1]),
                    reads=[xk] + mk, writes=[("hT", ti % 2, dc)])
            else:
                p.op("dve", lambda e, xs=xs, hT=hT, dc=dc, nt=nt, sc1p=sc1p, shift=shift: e.tensor_scalar(
                    out=hT[:, dc, :nt], in0=xs[:, :nt], scalar1=sc1p[:, dc:dc + 1], scalar2=shift[:, dc:dc + 1],
                    op0=ALU.mult, op1=ALU.add),
                    reads=[xk] + mk, writes=[("hT", ti % 2, dc)])

    shq = []

    def pull_shift(n):
        while n > 0 and shq:
            try:
                next(shq[0])
                n -= 1
            except StopIteration:
                shq.pop(0)

    def load_w(f0, nf):
        pull_shift(3)
        si = st["w"] % 4
        st["w"] += 1
        slot = W.ws[si]
        p.dma("sp", slot[:, :, :nf], w_v[:, :, f0:f0 + nf], reads=wkeys, writes=[("pws", si)])
        return slot, ("pws", si)

    def nextps():
        i = st["ps"] % 8
        st["ps"] += 1
        return C.ps[i], ("ps", i)

    def tile(ti):
        t0, nt, nm = tiles[ti]
        hT = W.hT[ti % 2]
        hk = [("hT", ti % 2, dc) for dc in range(16)]
        if nm == "x":
            vs = [load_w(4352 + 512 * h, 512) for h in range(2)]
            for h in range(2):
                slot, sk = vs[h]
                for tb in range(4):
                    ps, pk = nextps()
                    for kc in range(16):
                        p.op("pe", lambda e, ps=ps, slot=slot, kc=kc, tb=tb: e.matmul(
                            ps[:, :], hT[:, kc, tb * 128:(tb + 1) * 128], slot[:, kc, :],
                            start=(kc == 0), stop=(kc == 15)),
                            reads=[sk] + (hk if kc == 0 else []), writes=[pk], sig=(kc == 15))
                    p.op("act", lambda e, ps=ps, tb=tb, h=h: e.activation(
                        out=W.vg[tb][:, h * 512:(h + 1) * 512], in_=ps[:, :], func=AF.Gelu_apprx_tanh),
                        reads=[pk], writes=[("vg", tb, h)])
        if nm == "x":
            for tb in range(4):
                vg = W.vg[tb]
                vg3 = vg.rearrange("p (g d) -> p g d", d=64)
                s1, s2, mean, var, rstd, nm_ = W.st
                vk = [("vg", tb, 0), ("vg", tb, 1)]
                p.op("dve", lambda e, vg3=vg3: e.tensor_reduce(out=s1, in_=vg3, axis=mybir.AxisListType.X, op=ALU.add),
                     reads=vk, writes=["gs1"])
                p.op("act", lambda e, vg=vg: e.activation(out=W.sqv, in_=vg, func=AF.Square), reads=vk, writes=["sqv"])
                p.op("dve", lambda e: e.tensor_reduce(out=s2, in_=W.sqv.rearrange("p (g d) -> p g d", d=64),
                                                      axis=mybir.AxisListType.X, op=ALU.add),
                     reads=["sqv"], writes=["gs2"])
                p.op("dve", lambda e: e.tensor_scalar_mul(out=mean, in0=s1, scalar1=1.0 / 64), reads=["gs1"],
                     writes=["gmean"])
                p.op("dve", lambda e: e.tensor_tensor(out=var, in0=mean, in1=mean, op=ALU.mult), reads=["gmean"],
                     writes=["gvar"])
                p.op("dve", lambda e: e.scalar_tensor_tensor(out=var, in0=s2, scalar=1.0 / 64, in1=var, op0=ALU.mult,
                                                             op1=ALU.subtract),
                     reads=["gs2", "gvar"], writes=["gvar"])
                p.op("act", lambda e: e.activation(out=rstd, in_=var, func=AF.Ln, bias=C.epsg, scale=1.0),
                     reads=["gvar", "epsc2"], writes=["grstd"])
                p.op("act", lambda e: e.activation(out=rstd, in_=rstd, func=AF.Exp, scale=-0.5), reads=["grstd"],
                     writes=["grstd"])
                mb = mean.rearrange("p (g o) -> p g o", o=1).to_broadcast([128, 16, 64])
                rb = rstd.rearrange("p (g o) -> p g o", o=1).to_broadcast([128, 16, 64])
                p.op("pool", lambda e, vg3=vg3, mb=mb: e.tensor_tensor(out=vg3, in0=vg3, in1=mb, op=ALU.subtract),
                     reads=vk + ["gmean"], writes=vk)
                p.op("dve", lambda e, vg3=vg3, rb=rb: e.tensor_tensor(out=vg3, in0=vg3, in1=rb, op=ALU.mult),
                     reads=vk + ["grstd"], writes=vk)
                p.op("pool", lambda e, vg=vg: e.tensor_tensor(out=vg, in0=vg, in1=lng, op=ALU.mult),
                     reads=vk + ["lng"], writes=vk)
                vn = W.vn[tb]
                p.op("dve", lambda e, vg=vg, vn=vn: e.tensor_tensor(out=vn, in0=vg, in1=lnb, op=ALU.add),
                     reads=vk + ["lnb"], writes=[("vn", tb)])
        nfm = 34 if nm == "x" else 26
        fc = 0
        while fc < nfm:
            nchunk = min(4, nfm - fc) if fc != 24 else 2
            slot, sk = load_w(fc * 128, nchunk * 128)
            for jj in range(nchunk):
                ps, pk = nextps()
                for kc in range(16):
                    p.op("pe", lambda e, ps=ps, slot=slot, kc=kc, jj=jj, nt=nt: e.matmul(
                        ps[:, :nt], slot[:, kc, jj * 128:(jj + 1) * 128], hT[:, kc, :nt],
                        start=(kc == 0), stop=(kc == 15)),
                        reads=[sk] + (hk if kc == 0 else []), writes=[pk], sig=(kc == 15))
                f = fc + jj
                if f < 26:
                    oi = st["o"] % 2
                    st["o"] += 1
                    ot = W.ost[oi]
                    p.op("act", lambda e, ot=ot, ps=ps, nt=nt: e.activation(out=ot[:, :nt], in_=ps[:, :nt],
                                                                          func=AF.Copy),
                         reads=[pk], writes=[("post", oi)])
                    p.dma("act", C.pT[f * 128:(f + 1) * 128, t0:t0 + nt], ot[:, :nt],
                          reads=[("post", oi)], writes=[("pT", ti, f)], group=("postd", oi))
                else:
                    p.op("act", lambda e, ps=ps, f=f, nt=nt: e.activation(
                        out=W.uT[:, f - 26, :nt], in_=ps[:, :nt], func=AF.Gelu_apprx_tanh),
                        reads=[pk], writes=[("uT", f - 26)])
            fc += nchunk
        if nm != "x":
            return
        for tb in range(4):
            vn = W.vn[tb]
            for g in range(16):
                if g % 4 == 0:
                    ps, pk = nextps()
                po = ps[:, (g % 4) * 128:(g % 4 + 1) * 128]
                p.op("pe", lambda e, po=po, vn=vn, g=g: e.matmul(
                    po, vn[:, (g // 2) * 128:(g // 2 + 1) * 128], wsT[:, g, :], start=True, stop=False),
                    reads=[("vn", tb), "wsT"], writes=[pk], sig=False)
                p.op("pe", lambda e, po=po, g=g: e.matmul(po, c128, bs_hi[:, g, :], start=False, stop=False),
                     reads=["c128", "bs_hi"], writes=[pk], sig=False)
                p.op("pe", lambda e, po=po, g=g: e.matmul(po, c128, bs_lo[:, g, :], start=False, stop=True),
                     reads=["c128", "bs_lo"], writes=[pk])
                if g % 4 == 3:
                    b4_ = g // 4
                    psv4 = ps[:, :].rearrange("p (c m) -> p c m", m=128)
                    for hh in range(2):
                        rows = slice(64 * hh, 64 * hh + 64)
                        p.op("dve", lambda e, psv4=psv4, rows=rows, hh=hh, b4_=b4_, tb=tb: e.tensor_tensor(
                            out=W.gmo[rows, 2 * b4_:2 * b4_ + 2, tb * 128:(tb + 1) * 128],
                            in0=psv4[rows, hh::2, :],
                            in1=W.uT[rows, 2 * b4_:2 * b4_ + 2, tb * 128:(tb + 1) * 128], op=ALU.mult),
                            reads=[pk, ("uT", 2 * b4_), ("uT", 2 * b4_ + 1)],
                            writes=[("gmo", 2 * b4_, hh), ("gmo", 2 * b4_ + 1, hh)])
        p.dma("pool", C.catT[1024:2048, t0:t0 + nt].rearrange("(cc p) t -> p cc t", p=128), W.gmo[:, :, :nt],
              reads=[("gmo", c, hh) for c in range(8) for hh in range(2)], writes=[("catg", ti)], group="gmod")

    do_shift = shift_setup(C)
    modulate(0)
    for ti in range(len(tiles)):
        if ti + 1 < len(tiles):
            modulate(ti + 1)
        if 2 <= ti <= 8:
            shq.append(do_shift(ti - 2))
        tile(ti)
    shq.append(do_shift(7))
    shq.append(do_shift(8))
    pull_shift(1000)


def bcast_rows(ap1d, n):
    return bass.AP(ap1d.tensor, ap1d.offset, [[0, 128], [1, n]])


def shift_setup(C):
    p, A = C.p, C.A
    X = [A.f32(640) for _ in range(2)]
    XS = [A.f32(512) for _ in range(2)]
    om, _ = VC["mu"]
    o4, _ = VC["sel4"]
    o2, _ = VC["sel2"]
    omm = A.f32(26)
    m4 = A.f32(26 * 4).rearrange("p (f j) -> p f j", j=4)
    m2 = A.f32(26 * 2).rearrange("p (f j) -> p f j", j=2)
    mu = C.vt[:, om:om + 26]
    p.op("dve", lambda e: e.tensor_scalar(out=omm, in0=mu, scalar1=-1.0, scalar2=1.0, op0=ALU.mult, op1=ALU.add),
         reads=["vecs"], writes=["omm"])
    for jj in range(4):
        p.op("dve", lambda e, jj=jj: e.tensor_scalar_mul(out=m4[:, :, jj], in0=mu, scalar1=C.vt[:, o4 + jj:o4 + jj + 1]),
             reads=["vecs"], writes=[("m4", jj)])
    for jj in range(2):
        p.op("dve", lambda e, jj=jj: e.tensor_scalar_mul(out=m2[:, :, jj], in0=mu, scalar1=C.vt[:, o2 + jj:o2 + jj + 1]),
             reads=["vecs"], writes=[("m2", jj)])
    mk = ["omm"] + [("m4", j) for j in range(4)] + [("m2", j) for j in range(2)]
    cnt = {"n": 0}

    def do_block(blk):
        for fc in range(26):
            xi = cnt["n"] % 2
            cnt["n"] += 1
            x, xs = X[xi], XS[xi]
            xk, sk = ("shx", xi), ("shs", xi)
            if blk < 8:
                t0 = blk * 512
                lo = max(t0 - 64, 0)
                hi = min(t0 + 576, T)
                p.dma("sp", x[:, lo - (t0 - 64):hi - (t0 - 64)], C.pT[fc * 128:(fc + 1) * 128, lo:hi],
                      reads=[("pT", ti, fc) for ti in range(max(blk - 1, 0), min(blk + 2, 8))], writes=[xk])
                nt = 512
                cur = x[:, 64:576]
                p.op("act", lambda e, xs=xs, cur=cur, fc=fc: e.activation(out=xs, in_=cur, func=AF.Copy,
                                                                           scale=omm[:, fc:fc + 1]),
                     reads=[xk] + mk, writes=[sk])
                xs3 = xs.rearrange("p (r c) -> p r c", c=64)
                cur3 = cur.rearrange("p (r c) -> p r c", c=64)
                p.op("dve", lambda e, xs3=xs3, cur3=cur3, fc=fc: e.scalar_tensor_tensor(
                    out=xs3[:, :, 1:64], in0=cur3[:, :, 0:63], scalar=m4[:, fc, 0:1], in1=xs3[:, :, 1:64],
                    op0=ALU.mult, op1=ALU.add), reads=[xk, sk], writes=[sk])
                p.op("dve", lambda e, xs3=xs3, cur3=cur3, fc=fc: e.scalar_tensor_tensor(
                    out=xs3[:, :, 0:63], in0=cur3[:, :, 1:64], scalar=m4[:, fc, 1:2], in1=xs3[:, :, 0:63],
                    op0=ALU.mult, op1=ALU.add), reads=[xk, sk], writes=[sk])
                l0 = 64 if blk == 0 else 0
                p.op("dve", lambda e, xs=xs, x=x, fc=fc, l0=l0: e.scalar_tensor_tensor(
                    out=xs[:, l0:512], in0=x[:, l0:512], scalar=m4[:, fc, 2:3], in1=xs[:, l0:512],
                    op0=ALU.mult, op1=ALU.add), reads=[xk, sk], writes=[sk])
                h0 = 448 if blk == 7 else 512
                p.op("dve", lambda e, xs=xs, x=x, fc=fc, h0=h0: e.scalar_tensor_tensor(
                    out=xs[:, 0:h0], in0=x[:, 128:128 + h0], scalar=m4[:, fc, 3:4], in1=xs[:, 0:h0],
                    op0=ALU.mult, op1=ALU.add), reads=[xk, sk], writes=[sk])
            else:
                t0, nt = T, CT
                p.dma("sp", x[:, 0:CT], C.pT[fc * 128:(fc + 1) * 128, T:TT], reads=[("pT", 8, fc)], writes=[xk])
                cur = x[:, 0:CT]
                p.op("act", lambda e, xs=xs, cur=cur, fc=fc: e.activation(out=xs[:, 0:CT], in_=cur, func=AF.Copy,
                                                                           scale=omm[:, fc:fc + 1]),
                     reads=[xk] + mk, writes=[sk])
                p.op("dve", lambda e, xs=xs, x=x, fc=fc: e.scalar_tensor_tensor(
                    out=xs[:, 1:CT], in0=x[:, 0:CT - 1], scalar=m2[:, fc, 0:1], in1=xs[:, 1:CT],
                    op0=ALU.mult, op1=ALU.add), reads=[xk, sk], writes=[sk])
                p.op("dve", lambda e, xs=xs, x=x, fc=fc: e.scalar_tensor_tensor(
                    out=xs[:, 0:CT - 1], in0=x[:, 1:CT], scalar=m2[:, fc, 1:2], in1=xs[:, 0:CT - 1],
                    op0=ALU.mult, op1=ALU.add), reads=[xk, sk], writes=[sk])
            p.dma("pool", C.xsT[fc * 128:(fc + 1) * 128, t0:t0 + nt], xs[:, :nt], reads=[sk],
                  writes=[("xsT", blk, fc)], group=("shsd", xi))
            yield


    return do_block

def bc3(ap, n):
    return ap.rearrange("p (o m) -> p o m", o=1).to_broadcast([128, n, ap.shape[1]])


def mixer_consts(C):
    p, A, AR, AC = C.p, C.A, C.AR, C.AC
    M = Ctx()
    C.M = M
    cm = AC.f32(7 * 128).rearrange("p (a b) -> p a b", b=128)
    p.dma("sp", cm, C.cmat, writes=["cmat"])
    M.ident = AR.f32(128)
    M.identf = cm[:, 0, :]
    p.op("act", lambda e: e.activation(out=M.ident, in_=cm[:, 0, :], func=AF.Copy), reads=["cmat"], writes=["ident"])
    M.ml, M.mu, M.mui, M.bones = cm[:, 1, :], cm[:, 2, :], cm[:, 3, :], cm[:, 4, :]
    M.rm01 = cm[:, 6, 0:1]
    M.segm = A.f32(512)
    p.op("dve", lambda e: e.tensor_copy(out=M.segm.rearrange("p (c t) -> p c t", t=64), in_=bc3(cm[:, 5, 0:64], 8)),
         reads=["cmat"], writes=["segm"])
    M.gup = A.bf16(1024)
    p.dma("pool", M.gup, C.g_up, writes=["gup"])
    M.par = {}
    for nm in ["k_k", "k_a", "r_k"]:
        o, w = VC[nm]
        t = A.f32(512)
        p.op("dve", lambda e, t=t, o=o: e.tensor_copy(
            out=t.rearrange("p (c t) -> p c t", t=64),
            in_=C.vt[:, o:o + 8].rearrange("p (c o) -> p c o", o=1).to_broadcast([128, 8, 64])),
            reads=["vecs"], writes=[("par", nm)])
        M.par[nm] = t
    t = A.f32(512)
    p.op("dve", lambda e, t=t: e.tensor_scalar(out=t, in0=M.par["k_a"], scalar1=-1.0, scalar2=1.0, op0=ALU.mult,
                                               op1=ALU.add), reads=[("par", "k_a")], writes=[("par", "omka")])
    M.par["omka"] = t
    M.eps30 = AC.f32(2)[:, 0:1]
    p.op("pool", lambda e: e.memset(M.eps30, 1e-30), writes=["eps30"])
    M.epsgn = AC.f32(2)[:, 0:1]
    p.op("pool", lambda e: e.memset(M.epsgn, GN_EPS), writes=["epsgn"])


def mixer_pass(C, d):
    p, A, AR, M = C.p, C.A, C.AR, C.M
    rev = (d == 1)
    c0 = float(np.exp(-0.5))
    W = Ctx()
    lw = A.bf16(2 * 1024).rearrange("p (w f) -> p w f", w=2)
    p.dma("pool", lw, C.lora[:, d], writes=["lw"])
    par = dict(M.par)
    for nm in ["w0", "a0"]:
        o, w = VC[nm]
        t = A.f32(512)
        p.op("dve", lambda e, t=t, o=o: e.tensor_copy(
            out=t.rearrange("p (c t) -> p c t", t=64),
            in_=C.vt[:, o + 8 * d:o + 8 * d + 8].rearrange("p (c o) -> p c o", o=1).to_broadcast([128, 8, 64])),
            reads=["vecs"], writes=[("par", nm)])
        par[nm] = t
    f3 = lambda n=512: A.f32(n)
    big = lambda dt=None: (AR if dt is F32R else A).f32(1024).rearrange("p (c m) -> p c m", m=128)
    v3 = lambda ap: ap.rearrange("p (c t) -> p c t", t=64)
    EX = [dict(Ax=big(F32R), Rx=big(F32R), b2=big(F32R), k2=big(F32R), v2=big(F32R), gam=A.f32(8))
          for _ in range(3)]
    PR = [dict(AakT=big(F32R), RtF=big(F32R)) for _ in range(2)]
    RbT, RkT = big(F32R), big(F32R)
    Bstk, Kstk, Vstk = big(F32R), big(F32R), big(F32R)
    S = [big(F32R), big(F32R)]
    St = [big(F32R), big(F32R)]
    RtT = big(F32R)
    Pt, Ut = big(F32R), big(F32R)
    H = [big(F32R), big(F32R)]
    Hm = [big(), big()]
    yT = f3()
    ldn = dict(r=f3(), k=f3(), v=f3(), x24=A.f32(64), x25=A.f32(64))
    if rev:
        ldr = dict(r=f3(), k=f3(), v=f3(), x24=A.f32(64))
        ld = dict(ldr)
        ld["x25"] = ldn["x25"]
        LK = "ldr"
    else:
        ld = ldn
        LK = "ld"
    lo24 = A.bf16(64)
    sg25 = A.bf16(64)
    tmp = {n: f3() for n in ["s", "ic", "kk", "nkk", "kd", "bb", "rk", "sR", "cs", "csm", "Er", "Eb", "t"]}
    gst = f3()
    for hi in range(2):
        p.op("act", lambda e, h=H[hi]: e.activation(
            out=h, in_=M.bones.rearrange("p (o m) -> p o m", o=1).to_broadcast([128, 8, 128]), func=AF.Copy,
            scale=0.0), reads=["cmat"], writes=[("H", hi)])
        p.op("pool", lambda e, h=Hm[hi]: e.memset(h, 0.0), writes=[("Hm", hi, 0), ("Hm", hi, 1)])
    chunks = [(T + 64 * i, True) for i in range(4)] + [(64 * i, False) for i in range(64)]
    if rev:
        chunks = [(T + 64 * i, True) for i in reversed(range(4))] + [(64 * i, False) for i in reversed(range(64))]
    NCH = len(chunks)
    st = {"ps": 0}
    bm3 = M.bones.rearrange("p (h t) -> p h t", t=64)
    bm4 = M.bones.rearrange("p (o h t) -> p o h t", o=1, t=64).to_broadcast([128, 8, 2, 64])

    def R3(ap):
        a = v3(ap)
        return a[:, :, ::-1] if rev else a

    def nextps():
        i = st["ps"] % 8
        st["ps"] += 1
        return C.ps[i], ("ps", i)

    def mm8(lhs_fn, rhs_fn, reads, n=128, extra=None):
        terms = [(lhs_fn, rhs_fn)] + (extra or [])
        out = []
        for g in range(2):
            ps, pk = nextps()
            for c4 in range(4):
                cc = 4 * g + c4
                for ti, (lf, rf) in enumerate(terms):
                    p.op("pe", lambda e, ps=ps, c4=c4, cc=cc, lf=lf, rf=rf, ti=ti: e.matmul(
                        ps[:, c4 * n:(c4 + 1) * n], lf(cc), rf(cc), start=(ti == 0), stop=(ti == len(terms) - 1)),
                        reads=reads, writes=[pk], sig=(ti == len(terms) - 1))
            out.append((ps, pk))
        return out

    def grp(t, g):
        return t[:, 4 * g:4 * g + 4, :]

    def psv(ps, n=128):
        return ps[:, 0:4 * n].rearrange("p (c m) -> p c m", m=n)

    def prep(ci):
        col, is_ctx = chunks[ci]
        ex = EX[ci % 3]
        ek = ("ex", ci % 3)
        blk = (col // 512) if not is_ctx else 8
        xkeys = lambda f0: [("xsT", blk, f0 + c) for c in range(8)]
        for nm, r0 in (("r", 0), ("k", 1024), ("v", 2048)):
            p.dma("sp", v3(ldn[nm]), C.xsT[r0:r0 + 1024, col:col + 64].rearrange("(c p) t -> p c t", p=128),
                  reads=xkeys(r0 // 128), writes=[("ld", nm)])
        p.dma("sp", ldn["x24"], C.xsT[3072:3200, col:col + 64], reads=[("xsT", blk, 24)], writes=[("ld", "x24")])
        if rev:
            for nm in ("r", "k", "v"):
                p.op("pool", lambda e, nm=nm: e.tensor_copy(out=v3(ldr[nm]), in_=v3(ldn[nm])[:, :, ::-1]),
                     reads=[("ld", nm)], writes=[("ldr", nm)])
            p.op("pool", lambda e: e.tensor_copy(out=ldr["x24"], in_=ldn["x24"][:, ::-1]),
                 reads=[("ld", "x24")], writes=[("ldr", "x24")])
        p.op("act", lambda e: e.activation(out=lo24[0:64, :], in_=ld["x24"][0:64, :], func=AF.Tanh),
             reads=[(LK, "x24")], writes=["lo24a"])
        p.op("dve", lambda e: e.tensor_copy(out=lo24[64:128, :], in_=ld["x24"][64:128, :]),
             reads=[(LK, "x24")], writes=["lo24b"])
        yield
        for nm, wi, pn in (("s", 0, "w0"), ("ic", 1, "a0")):
            pss = mm8(lambda cc, wi=wi: lw[:, wi, cc * 128:(cc + 1) * 128], lambda cc: lo24[:, :],
                      ["lw", "lo24a", "lo24b"], n=64)
            for g, (ps, pk) in enumerate(pss):
                dst = tmp[nm][:, 256 * g:256 * g + 256]
                p.op("dve", lambda e, ps=ps, dst=dst, g=g, pn=pn: e.tensor_tensor(
                    out=dst, in0=ps[:, 0:256], in1=par[pn][:, 256 * g:256 * g + 256], op=ALU.add),
                    reads=[pk, ("par", pn)], writes=[("tmp", nm, g)])
            p.op("act", lambda e, nm=nm: e.activation(out=tmp[nm], in_=tmp[nm], func=AF.Sigmoid),
                 reads=[("tmp", nm, 0), ("tmp", nm, 1)], writes=[("tmp", nm)])
        yield
        p.op("dve", lambda e: e.tensor_tensor(out=tmp["kk"], in0=ld["k"], in1=par["k_k"], op=ALU.mult),
             reads=[(LK, "k"), ("par", "k_k")], writes=[("tmp", "kk")])
        p.op("act", lambda e: e.activation(out=tmp["t"], in_=tmp["kk"], func=AF.Square),
             reads=[("tmp", "kk")], writes=[("tmp", "t")])
        pss = mm8(lambda cc: M.bones, lambda cc: tmp["t"][:, cc * 64:(cc + 1) * 64], ["cmat", ("tmp", "t")], n=64)
        for g, (ps, pk) in enumerate(pss):
            dst = tmp["nkk"][:, 256 * g:256 * g + 256]
            p.op("act", lambda e, ps=ps, dst=dst: e.activation(out=dst, in_=ps[:, 0:256], func=AF.Ln, bias=M.eps30,
                                                              scale=1.0),
                 reads=[pk, "eps30"], writes=[("tmp", "nkk", g)])
        p.op("act", lambda e: e.activation(out=tmp["nkk"], in_=tmp["nkk"], func=AF.Exp, scale=-0.5),
             reads=[("tmp", "nkk", 0), ("tmp", "nkk", 1)], writes=[("tmp", "nkk")])
        p.op("dve", lambda e: e.tensor_tensor(out=tmp["kk"], in0=tmp["kk"], in1=tmp["nkk"], op=ALU.mult),
             reads=[("tmp", "kk"), ("tmp", "nkk")], writes=[("tmp", "kk")])
        p.op("act", lambda e: e.mul(out=tmp["nkk"], in_=tmp["kk"], mul=-1.0),
             reads=[("tmp", "kk")], writes=[("tmp", "nkk")])
        yield
        p.op("dve", lambda e: e.tensor_tensor(out=tmp["kd"], in0=tmp["ic"], in1=par["k_a"], op=ALU.mult),
             reads=[("tmp", "ic"), ("par", "k_a")], writes=[("tmp", "kd")])
        p.op("pool", lambda e: e.tensor_tensor(out=tmp["kd"], in0=tmp["kd"], in1=par["omka"], op=ALU.add),
             reads=[("tmp", "kd"), ("par", "omka")], writes=[("tmp", "kd")])
        p.op("pool", lambda e: e.tensor_tensor(out=tmp["kd"], in0=tmp["kd"], in1=ld["k"], op=ALU.mult),
             reads=[("tmp", "kd"), (LK, "k")], writes=[("tmp", "kd")])
        p.op("pool", lambda e: e.tensor_tensor(out=tmp["bb"], in0=tmp["kk"], in1=tmp["ic"], op=ALU.mult),
             reads=[("tmp", "kk"), ("tmp", "ic")], writes=[("tmp", "bb")])
        yield
        if not is_ctx:
            p.op("pool", lambda e: e.tensor_tensor(out=tmp["rk"], in0=ld["r"], in1=par["r_k"], op=ALU.mult),
                 reads=[(LK, "r"), ("par", "r_k")], writes=[("tmp", "rk")])
            p.op("pool", lambda e: e.tensor_tensor(out=tmp["rk"], in0=tmp["rk"], in1=tmp["kd"], op=ALU.mult),
                 reads=[("tmp", "rk"), ("tmp", "kd")], writes=[("tmp", "rk")])
            pss = mm8(lambda cc: M.bones, lambda cc: tmp["rk"][:, cc * 64:(cc + 1) * 64], ["cmat", ("tmp", "rk")],
                      n=64)
            for g, (ps, pk) in enumerate(pss):
                go = v3(gst)[:, 4 * g:4 * g + 4, :]
                if rev:
                    go = go[:, :, ::-1]
                p.op("dve", lambda e, ps=ps, g=g, go=go: e.tensor_tensor(
                    out=go, in0=ps[:, 0:256].rearrange("p (c t) -> p c t", t=64),
                    in1=v3(ld["v"])[:, 4 * g:4 * g + 4, :], op=ALU.mult), reads=[pk, (LK, "v")],
                    writes=[("gst", g)])
            p.dma("pool", C.bonT[d][:, col:col + 64].rearrange("(c p) t -> p c t", p=128), v3(gst),
                  reads=[("gst", 0), ("gst", 1)], writes=[("bonT", d, ci)], group="gstd")
            if d == 0:
                p.dma("sp", ld["x25"], C.xsT[3200:3328, col:col + 64], reads=[("xsT", blk, 25)],
                      writes=[("ld", "x25")])
                p.op("act", lambda e: e.activation(out=sg25, in_=ld["x25"], func=AF.Sigmoid),
                     reads=[("ld", "x25")], writes=["sg25"])
                pss = mm8(lambda cc: M.gup[:, cc * 128:(cc + 1) * 128], lambda cc: sg25[:, :], ["gup", "sg25"], n=64)
                for g, (ps, pk) in enumerate(pss):
                    p.op("act", lambda e, ps=ps, g=g: e.activation(out=gst[:, 256 * g:256 * g + 256],
                                                                   in_=ps[:, 0:256], func=AF.Copy),
                         reads=[pk], writes=[("gst", g)])
                p.dma("act", C.gTs[:, col:col + 64].rearrange("(c p) t -> p c t", p=128), v3(gst),
                      reads=[("gst", 0), ("gst", 1)], writes=[("gTs", ci)], group="gstd")
            yield
        sR, sRk = tmp["s"], ("tmp", "s")
        p.op("dve", lambda e: e.tensor_tensor_scan(out=tmp["cs"], data0=M.segm, data1=sR, initial=0.0,
                                                   op0=ALU.mult, op1=ALU.add),
             reads=[sRk, "segm"], writes=[("tmp", "cs")])
        p.op("act", lambda e: e.activation(out=tmp["Er"], in_=tmp["cs"], func=AF.Exp, scale=-c0),
             reads=[("tmp", "cs")], writes=[("tmp", "Er")])
        p.op("act", lambda e: e.activation(out=tmp["Eb"], in_=tmp["cs"], func=AF.Exp, scale=c0),
             reads=[("tmp", "cs")], writes=[("tmp", "Eb")])
        p.op("pool", lambda e: e.tensor_tensor(out=tmp["csm"], in0=tmp["cs"], in1=sR, op=ALU.subtract),
             reads=[("tmp", "cs"), sRk], writes=[("tmp", "csm")])
        p.op("act", lambda e: e.activation(out=tmp["csm"], in_=tmp["csm"], func=AF.Exp, scale=-c0),
             reads=[("tmp", "csm")], writes=[("tmp", "csm")])
        yield
        e4 = lambda t: t.rearrange("p c (h t) -> p c h t", t=64)
        b4 = lambda ap: v3(ap).rearrange("p c (o t) -> p c o t", o=1).to_broadcast([128, 8, 2, 64])
        p.op("pool", lambda e: e.tensor_tensor(out=v3(tmp["t"]), in0=v3(tmp["nkk"]), in1=v3(tmp["csm"]), op=ALU.mult),
             reads=[("tmp", "nkk"), ("tmp", "csm")], writes=[("tmp", "t")])
        p.op("pool", lambda e: e.tensor_tensor(out=e4(ex["Ax"]), in0=b4(tmp["t"]), in1=bm4, op=ALU.mult),
             reads=[("tmp", "t"), "cmat"], writes=[ek + ("Ax",)])
        b4r = b4
        p.op("dve", lambda e: e.tensor_tensor(out=e4(ex["b2"]), in0=b4r(tmp["bb"]), in1=b4(tmp["Eb"]), op=ALU.mult),
             reads=[("tmp", "bb"), ("tmp", "Eb")], writes=[ek + ("b",)])
        p.op("pool", lambda e: e.tensor_tensor(out=e4(ex["k2"]), in0=b4r(tmp["kd"]), in1=b4(tmp["Eb"]), op=ALU.mult),
             reads=[("tmp", "kd"), ("tmp", "Eb")], writes=[ek + ("k",)])
        p.op("dve", lambda e: e.tensor_copy(out=e4(ex["v2"]), in_=b4r(ld["v"])), reads=[(LK, "v")],
             writes=[ek + ("v",)])
        p.op("pool", lambda e: e.tensor_copy(out=ex["gam"], in_=v3(tmp["Er"])[:, :, 63]),
             reads=[("tmp", "Er")], writes=[ek + ("gam",)])
        if not is_ctx:
            p.op("pool", lambda e: e.tensor_tensor(out=v3(tmp["t"]), in0=v3(ld["r"]), in1=v3(tmp["Er"]), op=ALU.mult),
                 reads=[(LK, "r"), ("tmp", "Er")], writes=[("tmp", "t")])
            p.op("pool", lambda e: e.tensor_tensor(out=e4(ex["Rx"]), in0=b4(tmp["t"]), in1=bm4, op=ALU.mult),
                 reads=[("tmp", "t"), "cmat"], writes=[ek + ("Rx",)])
        yield

    def stage_ab(ci):
        col, is_ctx = chunks[ci]
        ex = EX[ci % 3]
        ek = ("ex", ci % 3)
        pr = PR[ci % 2]
        qk = ("pr", ci % 2)
        Ax = lambda cc: ex["Ax"][:, cc, :]
        Bb = lambda cc: ex["b2"][:, cc, :]
        Kb = lambda cc: ex["k2"][:, cc, :]

        def masked(pss, dst, mask, dk):
            for g, (ps, pk) in enumerate(pss):
                p.op("dve", lambda e, ps=ps, g=g: e.tensor_tensor(
                    out=grp(dst, g), in0=psv(ps), in1=mask.rearrange("p (o m) -> p o m", o=1).to_broadcast(
                        [128, 4, 128]), op=ALU.mult), reads=[pk, "cmat"], writes=[dk + (g,)])

        masked(mm8(Ax, Bb, [ek + ("Ax",), ek + ("b",)]), S[0], M.ml, ("S", 0))
        yield
        masked(mm8(Bb, Ax, [ek + ("Ax",), ek + ("b",)]), St[0], M.mu, ("St", 0))
        yield
        masked(mm8(Kb, Ax, [ek + ("Ax",), ek + ("k",)]), pr["AakT"], M.mu, qk + ("AakT",))
        rts = [RtT, pr["RtF"]]
        rtk = [("RtT",), qk + ("RtF",)]
        for g in range(2):
            p.op("pool", lambda e, g=g: e.tensor_tensor(
                out=grp(rts[0], g), in0=grp(St[0], g),
                in1=M.identf.rearrange("p (o m) -> p o m", o=1).to_broadcast([128, 4, 128]), op=ALU.add),
                reads=[("St", 0, g), "ident"], writes=[rtk[0] + (g,)])
        yield
        for j in range(1, 6):
            a, b = (j - 1) % 2, j % 2
            sk_prev = [("S", a, 0), ("S", a, 1), ("St", a, 0), ("St", a, 1)]
            pss = mm8(lambda cc, a=a: St[a][:, cc, :], lambda cc, a=a: S[a][:, cc, :], sk_prev)
            for g, (ps, pk) in enumerate(pss):
                p.op("act", lambda e, ps=ps, g=g, b=b: e.activation(out=grp(S[b], g), in_=psv(ps), func=AF.Copy),
                     reads=[pk], writes=[("S", b, g)])
            yield
            if j < 5:
                pss = mm8(lambda cc, a=a: S[a][:, cc, :], lambda cc, a=a: St[a][:, cc, :], sk_prev)
                for g, (ps, pk) in enumerate(pss):
                    p.op("act", lambda e, ps=ps, g=g, b=b: e.activation(out=grp(St[b], g), in_=psv(ps), func=AF.Copy),
                         reads=[pk], writes=[("St", b, g)])
                yield
            src, dst = rts[(j - 1) % 2], rts[j % 2]
            srck, dstk = rtk[(j - 1) % 2], rtk[j % 2]
            pss = mm8(lambda cc, b=b: S[b][:, cc, :], lambda cc, src=src: src[:, cc, :],
                      [("S", b, 0), ("S", b, 1), srck + (0,), srck + (1,)])
            for g, (ps, pk) in enumerate(pss):
                p.op("dve", lambda e, ps=ps, g=g, src=src, dst=dst: e.tensor_tensor(
                    out=grp(dst, g), in0=psv(ps), in1=grp(src, g), op=ALU.add),
                    reads=[pk, srck + (g,)], writes=[dstk + (g,)])
            yield

    def stage_c(ci):
        col, is_ctx = chunks[ci]
        ex = EX[ci % 3]
        ek = ("ex", ci % 3)
        pr = PR[ci % 2]
        qk = ("pr", ci % 2)
        Hc, Hn = H[ci % 2], H[(ci + 1) % 2]
        hk, hnk = ("H", ci % 2), ("H", (ci + 1) % 2)
        ident = lambda cc: M.ident
        two = lambda k: [k + (0,), k + (1,)]
        bonesb = M.bones.rearrange("p (o m) -> p o m", o=1).to_broadcast([128, 4, 128])

        def masked_ev(pss, dst, dk, mask=None):
            for g, (ps, pk) in enumerate(pss):
                p.op("dve", lambda e, ps=ps, g=g: e.tensor_tensor(
                    out=grp(dst, g), in0=psv(ps),
                    in1=(bonesb if mask is None else mask.rearrange("p (o m) -> p o m", o=1).to_broadcast(
                        [128, 4, 128])), op=ALU.mult), reads=[pk, "cmat"], writes=[dk + (g,)])

        masked_ev(mm8(lambda cc: ex["v2"][:, cc, :], ident, [ek + ("v",), "ident"]), Vstk, ("Vstk",))
        yield
        pss = mm8(lambda cc: ex["Ax"][:, cc, :], lambda cc: Hc[:, cc, :],
                  [ek + ("Ax",), hk] + two(qk + ("AakT",)) + two(("Vstk",)),
                  extra=[(lambda cc: pr["AakT"][:, cc, :], lambda cc: Vstk[:, cc, :])])
        for g, (ps, pk) in enumerate(pss):
            p.op("act", lambda e, ps=ps, g=g: e.activation(out=grp(Pt, g), in_=psv(ps), func=AF.Copy),
                 reads=[pk], writes=[("Pt", g)])
        yield
        masked_ev(mm8(lambda cc: ex["b2"][:, cc, :], ident, [ek + ("b",), "ident"]), Bstk, ("Bstk",))
        masked_ev(mm8(lambda cc: ex["k2"][:, cc, :], ident, [ek + ("k",), "ident"]), Kstk, ("Kstk",))
        if not is_ctx:
            Rx = lambda cc: ex["Rx"][:, cc, :]
            masked_ev(mm8(lambda cc: ex["b2"][:, cc, :], Rx, [ek + ("Rx",), ek + ("b",)]), RbT, ("RbT",), M.mui)
            masked_ev(mm8(lambda cc: ex["k2"][:, cc, :], Rx, [ek + ("Rx",), ek + ("k",)]), RkT, ("RkT",), M.mui)
        yield
        pss = mm8(lambda cc: pr["RtF"][:, cc, :], lambda cc: Pt[:, cc, :], two(qk + ("RtF",)) + two(("Pt",)))
        for g, (ps, pk) in enumerate(pss):
            p.op("dve", lambda e, ps=ps, g=g: e.tensor_copy(out=grp(Ut, g), in_=psv(ps)), reads=[pk],
                 writes=[("Ut", g)])
        yield
        Hmc, Hmn = Hm[ci % 2], Hm[(ci + 1) % 2]
        hmk, hmnk = ("Hm", ci % 2), ("Hm", (ci + 1) % 2)
        pss = mm8(lambda cc: Bstk[:, cc, :], lambda cc: Ut[:, cc, :],
                  two(("Bstk",)) + two(("Kstk",)) + two(("Ut",)) + two(("Vstk",)),
                  extra=[(lambda cc: Kstk[:, cc, :], lambda cc: Vstk[:, cc, :])])
        for g, (ps, pk) in enumerate(pss):
            p.op("dve", lambda e, ps=ps, g=g: e.tensor_tensor(out=grp(Hmn, g), in0=psv(ps), in1=grp(Hmc, g),
                                                             op=ALU.add),
                 reads=[pk, hmk + (g,)], writes=[hmnk + (g,)])
            p.op("pool", lambda e, g=g: e.tensor_tensor(
                out=grp(Hmn, g), in0=grp(Hmn, g),
                in1=ex["gam"][:, 4 * g:4 * g + 4].rearrange("p (c o) -> p c o", o=1).to_broadcast([128, 4, 128]),
                op=ALU.mult), reads=[hmnk + (g,), ek + ("gam",)], writes=[hmnk + (g,)])
            p.op("act", lambda e, g=g: e.activation(out=grp(Hn, g), in_=grp(Hmn, g), func=AF.Copy),
                 reads=[hmnk + (g,)], writes=[hnk])
        yield
        if not is_ctx:
            pss = mm8(lambda cc: Hc[:, cc, :], lambda cc: ex["Rx"][:, cc, :],
                      [hk, ek + ("Rx",)] + two(("Ut",)) + two(("RbT",)) + two(("RkT",)) + two(("Vstk",)),
                      extra=[(lambda cc: Ut[:, cc, :], lambda cc: RbT[:, cc, :]),
                             (lambda cc: Vstk[:, cc, :], lambda cc: RkT[:, cc, :])])
            y3 = v3(yT)
            for g, (ps, pk) in enumerate(pss):
                for hh in range(2):
                    rows = slice(64 * hh, 64 * hh + 64)
                    o = y3[rows, 4 * g:4 * g + 4, :]
                    if rev:
                        o = o[:, :, ::-1]
                    src = psv(ps)[rows, :, 64 * hh:64 * hh + 64]
                    if hh == 0:
                        p.op("dve", lambda e, o=o, src=src: e.tensor_copy(out=o, in_=src), reads=[pk],
                             writes=[("yT", g, hh)])
                    else:
                        p.op("act", lambda e, o=o, src=src: e.activation(out=o, in_=src, func=AF.Copy), reads=[pk],
                             writes=[("yT", g, hh)])
            p.dma("pool", C.yTs[d][:, col:col + 64].rearrange("(c p) t -> p c t", p=128), y3,
                  reads=[("yT", g, hh) for g in range(2) for hh in range(2)], writes=[("yTs", d, ci)], group="yTd")
        yield

    def drain(*gens):
        gens = [g for g in gens if g is not None]
        while gens:
            for g in list(gens):
                try:
                    next(g)
                except StopIteration:
                    gens.remove(g)

    drain(prep(0))
    drain(stage_ab(0), prep(1))
    for ci in range(NCH):
        drain(stage_c(ci),
              stage_ab(ci + 1) if ci + 1 < NCH else None,
              prep(ci + 2) if ci + 2 < NCH else None)


def rwkv_out_phase(C):
    p, A, M = C.p, C.A, C.M
    L = [{n: A.f32(512) for n in ["y0", "y1", "b0", "b1", "g"]} for _ in range(2)]
    sq = A.f32(512)
    mean, var, rstd = A.f32(512), A.f32(512), A.f32(512)
    ob = [A.bf16(512) for _ in range(2)]
    ogg, _ = VC["gn_g"]
    ogb, _ = VC["gn_b"]
    n = 0
    for ti in range(8):
        t0 = ti * 512
        cks = list(range(8 * ti + 4, 8 * ti + 12)) if False else None
        for cc in range(8):
            l = L[n % 2]
            lk = lambda nm, n=n: ("rl", n % 2, nm)
            rows = slice(cc * 128, (cc + 1) * 128)
            for nm, src, dep in (("y0", C.yTs[0], "yTs0"), ("y1", C.yTs[1], "yTs1"), ("b0", C.bonT[0], "bon0"),
                                 ("b1", C.bonT[1], "bon1"), ("g", C.gTs, "gTs")):
                p.dma("sp", l[nm], src[rows, t0:t0 + 512], reads=["mixdone"], writes=[lk(nm)])
            p.op("pool", lambda e, l=l: e.tensor_tensor(out=l["y0"], in0=l["y0"], in1=l["y1"], op=ALU.add),
                 reads=[lk("y0"), lk("y1")], writes=[lk("y0")])
            p.op("pool", lambda e, l=l: e.tensor_tensor(out=l["b0"], in0=l["b0"], in1=l["b1"], op=ALU.add),
                 reads=[lk("b0"), lk("b1")], writes=[lk("b0")])
            p.op("act", lambda e, l=l: e.activation(out=sq, in_=l["y0"], func=AF.Square), reads=[lk("y0")],
                 writes=["rsq"])
            p.op("pe", lambda e, l=l: e.matmul(C.ps[0][:, :], M.bones, l["y0"], start=True, stop=True),
                 reads=["cmat", lk("y0")], writes=[("ps", 0)])
            p.op("pe", lambda e: e.matmul(C.ps[1][:, :], M.bones, sq, start=True, stop=True),
                 reads=["cmat", "rsq"], writes=[("ps", 1)])
            inv = 1.0 / 64
            p.op("act", lambda e: e.activation(out=mean, in_=C.ps[0][:, :], func=AF.Copy, scale=inv),
                 reads=[("ps", 0)], writes=["rmean"])
            p.op("act", lambda e: e.activation(out=var, in_=C.ps[0][:, :], func=AF.Square, scale=inv),
                 reads=[("ps", 0)], writes=["rvar"])
            p.op("dve", lambda e: e.scalar_tensor_tensor(out=var, in0=C.ps[1][:, :], scalar=inv, in1=var,
                                                         op0=ALU.mult, op1=ALU.subtract),
                 reads=[("ps", 1), "rvar"], writes=["rvar"])
            p.op("act", lambda e: e.activation(out=rstd, in_=var, func=AF.Ln, bias=M.epsgn, scale=1.0),
                 reads=["rvar", "epsgn"], writes=["rrstd"])
            p.op("act", lambda e: e.activation(out=rstd, in_=rstd, func=AF.Exp, scale=-0.5), reads=["rrstd"],
                 writes=["rrstd"])
            p.op("pool", lambda e, l=l: e.tensor_tensor(out=l["y0"], in0=l["y0"], in1=mean, op=ALU.subtract),
                 reads=[lk("y0"), "rmean"], writes=[lk("y0")])
            p.op("dve", lambda e, l=l: e.tensor_tensor(out=l["y0"], in0=l["y0"], in1=rstd, op=ALU.mult),
                 reads=[lk("y0"), "rrstd"], writes=[lk("y0")])
            p.op("act", lambda e, l=l, cc=cc: e.activation(out=l["y0"], in_=l["y0"], func=AF.Identity,
                                                           scale=C.vt[:, ogg + cc:ogg + cc + 1],
                                                           bias=C.vt[:, ogb + cc:ogb + cc + 1]),
                 reads=[lk("y0"), "vecs"], writes=[lk("y0")])
            p.op("pool", lambda e, l=l: e.tensor_tensor(out=l["y0"], in0=l["y0"], in1=l["b0"], op=ALU.add),
                 reads=[lk("y0"), lk("b0")], writes=[lk("y0")])
            o = ob[n % 2]
            p.op("dve", lambda e, l=l, o=o: e.tensor_tensor(out=o, in0=l["y0"], in1=l["g"], op=ALU.mult),
                 reads=[lk("y0"), lk("g")], writes=[("rob", n % 2)])
            p.dma("pool", C.catT[rows, t0:t0 + 512], o, reads=[("rob", n % 2)], writes=[("catr", ti, cc)],
                  group=("robd", n % 2))
            n += 1


def wout_phase(C):
    p, A = C.p, C.A
    wo = r3(A.bf16(16 * 2048), b=2048)
    cat = [r3(A.bf16(16 * 512), b=512) for _ in range(2)]
    z = r3(A.f32(16 * 512), b=512)
    sq = [A.f32(512) for _ in range(2)]
    mean, var, rstd = A.f32(512), A.f32(512), A.f32(512)
    ost = [A.f32(512) for _ in range(2)]
    wv = C.w_out_b.rearrange("(kc p) f -> p kc f", p=128)
    for h in range(4):
        p.dma("sp", wo[:, 4 * h:4 * h + 4, :], wv[:, 4 * h:4 * h + 4, :], reads=C.cast_keys["w_out"],
              writes=[("wo", h)])
    wok = [("wo", h) for h in range(4)]
    shift, sc1p, gate, mk = mod_aps(C, "x", 1)
    og, _ = VC["ln_g"]
    ol, _ = VC["ln_b"]
    lnj = 1
    nt = 512
    oidx = 0
    for ti in range(8):
        t0 = ti * 512
        ct = cat[ti % 2]
        p.dma("sp", ct, C.catT[:, t0:t0 + 512].rearrange("(kc p) t -> p kc t", p=128),
              reads=[("catg", ti)] + [("catr", ti, cc) for cc in range(8)], writes=[("cat", ti % 2)])
        pend = None

        def stats(oc):
            p.op("pe", lambda e: e.matmul(C.ps[6][:, :], C.ones, z[:, oc, :], start=(oc == 0), stop=(oc == 15)),
                 reads=["ones", ("z", oc)], writes=[("ps", 6)], sig=(oc == 15))
            p.op("pe", lambda e: e.matmul(C.ps[7][:, :], C.ones, sq[oc % 2], start=(oc == 0), stop=(oc == 15)),
                 reads=["ones", ("sq", oc % 2)], writes=[("ps", 7)], sig=True)

        for oc in range(16):
            p.dma("pool", z[:, oc, :], C.x1T[oc * 128:(oc + 1) * 128, t0:t0 + 512],
                  reads=[("dst", "fa", ti, oc)], writes=[("z", oc)])
            ps = C.ps[4 + oc % 2]
            for kc in range(16):
                p.op("pe", lambda e, ps=ps, kc=kc, oc=oc, ct=ct: e.matmul(
                    ps[:, :], wo[:, kc, oc * 128:(oc + 1) * 128], ct[:, kc, :], start=(kc == 0), stop=(kc == 15)),
                    reads=(wok + [("cat", ti % 2)]) if kc == 0 else [], writes=[("ps", 4 + oc % 2)], sig=(kc == 15))
            if pend is not None:
                stats(pend)
            p.op("dve", lambda e, ps=ps, oc=oc: e.scalar_tensor_tensor(
                out=z[:, oc, :], in0=ps[:, :], scalar=gate[:, oc:oc + 1], in1=z[:, oc, :],
                op0=ALU.mult, op1=ALU.add), reads=[("ps", 4 + oc % 2)] + mk, writes=[("z", oc)])
            p.op("act", lambda e, oc=oc: e.activation(out=sq[oc % 2], in_=z[:, oc, :], func=AF.Square),
                 reads=[("z", oc)], writes=[("sq", oc % 2)])
            pend = oc
        stats(pend)
        inv = 1.0 / D
        p.op("act", lambda e: e.activation(out=mean, in_=C.ps[6][:, :], func=AF.Copy, scale=inv),
             reads=[("ps", 6)], writes=["mean"])
        p.op("act", lambda e: e.activation(out=var, in_=C.ps[6][:, :], func=AF.Square, scale=inv),
             reads=[("ps", 6)], writes=["var"])
        p.op("dve", lambda e: e.scalar_tensor_tensor(out=var, in0=C.ps[7][:, :], scalar=inv, in1=var,
                                                     op0=ALU.mult, op1=ALU.subtract),
             reads=[("ps", 7), "var"], writes=["var"])
        p.op("act", lambda e: e.activation(out=rstd, in_=var, func=AF.Ln, bias=C.epsln, scale=1.0),
             reads=["var", "epsc"], writes=["rstd"])
        p.op("act", lambda e: e.activation(out=rstd, in_=rstd, func=AF.Exp, scale=-0.5), reads=["rstd"],
             writes=["rstd"])
        for oc in range(16):
            p.op("pool", lambda e, oc=oc: e.tensor_tensor(out=z[:, oc, :], in0=z[:, oc, :], in1=mean, op=ALU.subtract),
                 reads=[("z", oc), "mean"], writes=[("z", oc)])
            p.op("dve", lambda e, oc=oc: e.tensor_tensor(out=z[:, oc, :], in0=z[:, oc, :], in1=rstd, op=ALU.mult),
                 reads=[("z", oc), "rstd"], writes=[("z", oc)])
            oi = oidx % 2
            oidx += 1
            ot = ost[oi]
            gcol = C.vt[:, og + lnj * 16 + oc:og + lnj * 16 + oc + 1]
            bcol = C.vt[:, ol + lnj * 16 + oc:ol + lnj * 16 + oc + 1]
            p.op("act", lambda e, oc=oc, ot=ot, gcol=gcol, bcol=bcol: e.activation(
                out=ot, in_=z[:, oc, :], func=AF.Identity, scale=gcol, bias=bcol),
                reads=[("z", oc), "vecs"], writes=[("ost", oi)])
            p.dma("act", C.x2T[oc * 128:(oc + 1) * 128, t0:t0 + 512], ot,
                  reads=[("ost", oi)], writes=[("dst", "w3", ti, oc)], group=("ostd", oi))


_NC_CACHE = {}


def _pm(v):
    return np.ascontiguousarray(np.asarray(v, np.float32).reshape(-1, 128).T)


def _cmat():
    m = np.zeros((128, 7, 128), np.float32)
    r = np.arange(128)
    hb = (r[:, None] // 64) == (r[None, :] // 64)
    ti, tj = r[:, None] % 64, r[None, :] % 64
    m[:, 0, :] = np.eye(128)
    m[:, 1, :] = hb & (tj < ti)
    m[:, 2, :] = hb & (ti < tj)
    m[:, 3, :] = hb & (ti <= tj)
    m[:, 4, :] = hb
    m[:, 5, :] = 1.0
    m[:, 5, 0] = 0.0
    m[:64, 6, :] = 1.0
    return m


def _lora(inputs):
    m = np.zeros((128, 2, 2, 1024), np.float32)
    m[:64, :, 0, :] = np.transpose(inputs["w_up"][0], (1, 0, 2))
    m[64:, :, 1, :] = np.transpose(inputs["a_up"][0], (1, 0, 2))
    return m


def make_in_maps(inputs):
    x = np.asarray(inputs["x"], np.float32)
    ctx = np.asarray(inputs["ctx"], np.float32)
    maps = []
    shared = {
        "w_ada": np.ascontiguousarray(inputs["w_ada"][0], dtype=np.float32),
        "ffn_a_wi": np.ascontiguousarray(inputs["ffn_a_wi"][0], dtype=np.float32),
        "ffn_a_wo": np.ascontiguousarray(inputs["ffn_a_wo"][0], dtype=np.float32),
        "ffn_b_wi": np.ascontiguousarray(inputs["ffn_b_wi"][0], dtype=np.float32),
        "ffn_b_wo": np.ascontiguousarray(inputs["ffn_b_wo"][0], dtype=np.float32),
        "w_in": np.ascontiguousarray(inputs["w_in"][0], dtype=np.float32),
        "w_out": np.ascontiguousarray(inputs["w_out"][0], dtype=np.float32),
        "gm_ln_g": np.ascontiguousarray(inputs["gm_ln_g"][0], dtype=np.float32),
        "gm_ln_b": np.ascontiguousarray(inputs["gm_ln_b"][0], dtype=np.float32),
        "gm_bs": np.ascontiguousarray(inputs["gm_bs"][0].reshape(-1), dtype=np.float32),
        "gm_wsT": np.ascontiguousarray(np.transpose(inputs["gm_ws"][0], (2, 0, 1)), dtype=np.float32),
        "cmat": _cmat(),
        "lora": _lora(inputs),
        "g_up": np.ascontiguousarray(inputs["g_up"][0], dtype=np.float32),
    }
    for b in range(NCORES):
        vec = np.zeros((128, NV), np.float32)

        def put(name, arr):
            o, w = VC[name]
            assert arr.shape == (128, w), (name, arr.shape)
            vec[:, o:o + w] = arr

        put("c", _pm(inputs["c"][b]))
        put("cctx", _pm(inputs["c_ctx"]))
        put("b_ada", _pm(inputs["b_ada"][0]))
        put("ln_g", _pm(inputs["ln_g"][0].reshape(-1)))
        put("ln_b", _pm(inputs["ln_b"][0].reshape(-1)))
        put("mu", _pm(inputs["mu_shift"][0]))
        put("w0", _pm(inputs["w0"][0].reshape(-1)))
        put("a0", _pm(inputs["a0"][0].reshape(-1)))
        put("k_k", _pm(inputs["k_k"][0]))
        put("k_a", _pm(inputs["k_a"][0]))
        put("r_k", _pm(inputs["r_k"][0].reshape(-1)))
        put("gn_g", _pm(inputs["gn_g"][0]))
        put("gn_b", _pm(inputs["gn_b"][0]))
        put("sel4", (np.arange(128)[:, None] % 4 == np.arange(4)[None, :]).astype(np.float32))
        put("sel2", (np.arange(128)[:, None] % 2 == np.arange(2)[None, :]).astype(np.float32))
        xT = np.empty((D, TT), np.float32)
        xT[:, :T] = x[b].T
        xT[:, T:] = ctx[b].T
        m = dict(shared)
        m["xT"] = xT
        m["vecs"] = vec
        maps.append(m)
    return maps


def kernel(**inputs):
    if "nc" not in _NC_CACHE:
        _NC_CACHE["nc"] = build()
    nc = _NC_CACHE["nc"]
    maps = make_in_maps(inputs)
    res = run_bass_kernel_spmd(nc, maps, core_ids=list(range(NCORES)))
    out = np.empty((NCORES, T, D), np.float32)
    for b in range(NCORES):
        out[b] = res.results[b]["outT"].T
    return out
```

```python
import numpy as np
from contextlib import ExitStack
import concourse.bass as bass
import concourse.mybir as mybir
from concourse.bass_utils import run_bass_kernel_spmd

F32 = mybir.dt.float32
BF16 = mybir.dt.bfloat16
F32R = mybir.dt.float32r
AF = mybir.ActivationFunctionType
ALU = mybir.AluOpType

D = 2048
T = 4096
CT = 256
TT = T + CT
DFF = 5632
NMOD = 9
RW = 1024
RWIN = 3328
INW = 5376
ALPHA = 2.0 ** 0.25
LN_EPS = 1e-5
GN_EPS = 64e-5
NCORES = 8

VC = {}
_o = 0
for _n, _w in [("c", 16), ("cctx", 16), ("b_ada", 144), ("ln_g", 48), ("ln_b", 48), ("mu", 26),
               ("w0", 16), ("a0", 16), ("k_k", 8), ("k_a", 8), ("r_k", 8), ("gn_g", 8), ("gn_b", 8),
               ("sel4", 4), ("sel2", 2)]:
    VC[_n] = (_o, _w)
    _o += _w
NV = _o

ENGS = ["pe", "act", "dve", "pool", "sp"]


class Tok:
    __slots__ = ("sk", "n")

    def __init__(self, sk):
        self.sk = sk
        self.n = None


class Prog:
    EPOCH = 30000

    def __init__(self, nc, es):
        self.nc = nc
        self.es = es
        self.ops = {e: [] for e in ENGS}
        self.cnt = {}
        self.cur = {}
        self.lastw = {}
        self.readers = {}
        self.semh = {}
        self.latest = {}

    def _tok(self, sk, sig):
        t = self.cur.get(sk)
        if t is None:
            t = Tok(sk)
            self.cur[sk] = t
        if sig:
            self.cnt[sk] = self.cnt.get(sk, 0) + 1
            t.n = self.cnt[sk]
            self.cur[sk] = None
            self.latest[sk] = t
        return t

    def _deps(self, reads, writes, own_group=None):
        deps = []
        for k in reads:
            t = self.lastw.get(k)
            if t is not None:
                deps.append(t)
        for k in writes:
            t = self.lastw.get(k)
            if t is not None and not (own_group is not None and t.sk == own_group):
                deps.append(t)
            r = self.readers.get(k)
            if r:
                deps.extend(r.values())
        return deps

    def _commit(self, tok, reads, writes):
        for k in reads:
            self.readers.setdefault(k, {})[tok.sk] = tok
        for k in writes:
            self.lastw[k] = tok
            self.readers[k] = {}

    def op(self, eng, fn, reads=(), writes=(), sig=True):
        deps = self._deps(reads, writes)
        tok = self._tok(eng, sig)
        self.ops[eng].append((deps, fn, tok if sig else None, 1))
        self._commit(tok, reads, writes)
        return tok

    def dma(self, q, out, in_, reads=(), writes=(), group=None):
        if group is None:
            group = writes[0] if writes else reads[0]
        sk = ("dma", group)
        deps = self._deps(reads, writes, own_group=sk)
        tok = self._tok(sk, True)
        self.ops[q].append((deps, (lambda e, o=out, i=in_: e.dma_start(out=o, in_=i)), tok, 16))
        self._commit(tok, reads, writes)
        return tok

    def barrier(self):
        toks = [t for t in self.latest.values()]
        for e in ENGS:
            self.ops[e].append((list(toks), None, None, 0))

    def _semval(self, tok, per):
        ep = (tok.n - 1) // per
        v = (tok.n - 1) % per + 1
        key = (tok.sk, ep)
        h = self.semh.get(key)
        if h is None:
            h = self.es.enter_context(self.nc.semaphore("s%d" % len(self.semh)))
            self.semh[key] = h
        return key, h, v

    def emit(self, block):
        def per_of(sk):
            return self.EPOCH // 16 if isinstance(sk, tuple) else self.EPOCH

        def run(e, name):
            waited = {}
            for deps, fn, tok, inc in self.ops[name]:
                need = {}
                for t in deps:
                    if t.n is None:
                        raise RuntimeError("unsignaled dependency on %s" % (t.sk,))
                    if t.sk == name and name == "pe":
                        continue
                    key, h, v = self._semval(t, per_of(t.sk))
                    if waited.get(key, 0) >= v:
                        continue
                    if need.get(key, (None, 0))[1] < v:
                        need[key] = (h, v)
                for key, (h, v) in need.items():
                    e.wait_ge(h, v * (16 if isinstance(key[0], tuple) else 1))
                    waited[key] = v
                if fn is None:
                    continue
                ins = fn(e)
                if tok is not None:
                    key, h, v = self._semval(tok, per_of(tok.sk))
                    ins.then_inc(h, inc)

        for name in ENGS:
            for deps, fn, tok, inc in self.ops[name]:
                for t in deps:
                    if t.n is not None:
                        self._semval(t, per_of(t.sk))
                if tok is not None:
                    self._semval(tok, per_of(tok.sk))

        @block.tensor
        def _(e):
            run(e, "pe")

        @block.scalar
        def _(e):
            run(e, "act")

        @block.vector
        def _(e):
            run(e, "dve")

        @block.gpsimd
        def _(e):
            run(e, "pool")

        @block.sync
        def _(e):
            run(e, "sp")


class Carver:
    def __init__(self, pool, nwords):
        self.pool = pool
        self.n = nwords
        self.off = 0

    def mark(self):
        return self.off

    def reset(self, m):
        self.peak = max(getattr(self, "peak", 0), self.off)
        self.off = m

    def f32(self, n, dt=None):
        n = (n + 1) // 2 * 2
        a = self.pool[:, self.off:self.off + n]
        self.off += n
        assert self.off <= self.n, "SBUF pool overflow %d > %d" % (self.off, self.n)
        return a

    def bf16(self, n):
        w = (n + 3) // 4 * 2
        a = self.pool[:, self.off:self.off + w].bitcast(BF16)
        self.off += w
        assert self.off <= self.n, "SBUF pool overflow %d > %d" % (self.off, self.n)
        return a[:, 0:n]


def r3(ap, **kw):
    return ap.rearrange("p (a b) -> p a b", **kw)


class Ctx:
    pass


def build(stage=99):
    nc = bass.Bass("TRN2", target_bir_lowering=False)
    C = Ctx()
    C.nc = nc
    dt_in = lambda n, s: nc.dram_tensor(n, s, F32, kind="ExternalInput").ap()
    C.xT = dt_in("xT", [D, TT])
    C.vecs = dt_in("vecs", [128, NV])
    C.w_ada = dt_in("w_ada", [D, NMOD * D])
    C.wa_i = dt_in("ffn_a_wi", [D, 2 * DFF])
    C.wa_o = dt_in("ffn_a_wo", [DFF, D])
    C.wb_i = dt_in("ffn_b_wi", [D, 2 * DFF])
    C.wb_o = dt_in("ffn_b_wo", [DFF, D])
    C.w_in = dt_in("w_in", [D, INW])
    C.w_out = dt_in("w_out", [D, D])
    C.gm_ln_g = dt_in("gm_ln_g", [1024])
    C.gm_ln_b = dt_in("gm_ln_b", [1024])
    C.gm_bs = dt_in("gm_bs", [2048])
    C.gm_wsT = dt_in("gm_wsT", [128, 16, 128])
    C.cmat = dt_in("cmat", [128, 7, 128])
    C.lora = dt_in("lora", [128, 2, 2, 1024])
    C.g_up = dt_in("g_up", [128, 1024])
    sc = lambda n, s, d: nc.dram_tensor(n, s, d).ap()
    C.wa_i_b = sc("wa_i_b", [D, 2 * DFF], BF16)
    C.wa_o_b = sc("wa_o_b", [DFF, D], BF16)
    C.wb_i_b = sc("wb_i_b", [D, 2 * DFF], BF16)
    C.wb_o_b = sc("wb_o_b", [DFF, D], BF16)
    C.w_in_b = sc("w_in_b", [D, INW], BF16)
    C.w_out_b = sc("w_out_b", [D, D], BF16)
    dbg = lambda n, s, d, st: (nc.dram_tensor("dbg", s, d, kind="ExternalOutput").ap() if stage == st
                                else sc(n, s, d))
    C.x1T = dbg("x1T", [D, TT], F32, 1)
    C.pT = dbg("pT", [RWIN, TT], F32, 2)
    C.catT = dbg("catT", [D, T], BF16, 5)
    C.xsT = dbg("xsT", [RWIN, TT], F32, 3)
    C.yTs = [dbg("yT0", [RW, T], F32, 4), sc("yT1", [RW, T], F32)]
    C.bonT = [sc("bonT0", [RW, T], F32), sc("bonT1", [RW, T], F32)]
    C.gTs = sc("gTs", [RW, T], F32)
    C.x2T = dbg("x2T", [D, T], F32, 6)
    C.outT = nc.dram_tensor("outT", [D, T], F32, kind="ExternalOutput").ap() if stage >= 99 else sc("outT", [D, T], F32)

    with ExitStack() as es:
        NCW = 9 * 256
        cpool = es.enter_context(nc.sbuf_tensor("cpool", [128, NCW], F32))
        C.ps = [es.enter_context(nc.psum_tensor("ps%d" % i, [128, 512], F32)) for i in range(8)]
        p = Prog(nc, es)
        C.p = p
        C.AC = Carver(cpool, NCW)
        st = {"k": 0}

        def run_phase(fn, kib, rkib=0):
            st["k"] += 1
            cm = nc.sbuf_tensor("ph%d" % st["k"], [128, kib * 256], F32)
            C.A = Carver(cm.__enter__(), kib * 256)
            cr = None
            if rkib:
                cr = nc.sbuf_tensor("rp%d" % st["k"], [128, rkib * 512], BF16)
                C.AR = Carver(cr.__enter__(), rkib * 512)
            fn()
            print("phase", st["k"], "pool use KiB", max(C.A.off, getattr(C.A, "peak", 0)) / 256.0,
                  (max(C.AR.off, getattr(C.AR, "peak", 0)) / 512.0 if rkib else 0))
            if cr is not None:
                cr.__exit__(None, None, None)
            cm.__exit__(None, None, None)
            p.barrier()

        def ph0():
            phase_consts(C)
            phase_cast(C)
            phase_mod(C)
        run_phase(ph0, 130)
        tiles = [(512 * i, 512, "x") for i in range(8)] + [(T, CT, "c")]
        run_phase(lambda: ffn_phase(C, "fa", C.xT, C.x1T, tiles, C.wa_i_b, C.wa_o_b, 0, 0), 196)
        if stage >= 2:
            run_phase(lambda: proj_phase(C), 198)
        if stage >= 4:
            def mix():
                mixer_consts(C)
                m1, m2 = C.A.mark(), C.AR.mark()
                for d in range(2 if stage >= 5 else 1):
                    mixer_pass(C, d)
                    C.A.reset(m1)
                    C.AR.reset(m2)
                    p.barrier()
            run_phase(mix, 78, 70)
        if stage >= 5:
            run_phase(lambda: rwkv_out_phase(C), 60)
        if stage >= 6:
            run_phase(lambda: wout_phase(C), 160)
        if stage >= 99:
            tiles_b = [(512 * i, 512, "x") for i in range(8)]
            run_phase(lambda: ffn_phase(C, "fb", C.x2T, C.outT, tiles_b, C.wb_i_b, C.wb_o_b, 2, 2), 196)
        p.barrier()
        block = es.enter_context(nc.Block())
        p.emit(block)
    return nc


def phase_consts(C):
    p, A = C.p, C.AC
    C.vt = A.f32(NV)
    p.dma("sp", C.vt, C.vecs, writes=["vecs"])
    C.ones = A.f32(128)
    C.epsln = A.f32(2)[:, 0:1]
    p.op("pool", lambda e: e.memset(C.epsln, LN_EPS / (ALPHA * ALPHA)), writes=["epsc"])
    p.op("pool", lambda e: e.memset(C.ones, 1.0), writes=["ones"])
    C.epsg = A.f32(2)[:, 0:1]
    p.op("pool", lambda e: e.memset(C.epsg, LN_EPS), writes=["epsc2"])


def vcol(C, name, i=0, n=1):
    o, w = VC[name]
    return C.vt[:, o + i:o + i + n]


def phase_cast(C):
    p = C.p
    C.cast_keys = {}

    def cast(name, src, dst, rows, piece):
        keys = []
        for r0 in range(0, rows, piece):
            k = ("w", name, r0)
            p.dma("pool", dst[r0:r0 + piece, :], src[r0:r0 + piece, :], writes=[k], group=("cast", name))
            keys.append(k)
        C.cast_keys[name] = keys

    C.cast = cast
    C.cast_blk = {}

    def cast_cols(name, src, dst, blocks):
        ks = []
        for bi, cols in enumerate(blocks):
            k = ("w", name, "blk", bi)
            for (c0, c1) in cols:
                p.dma("pool", dst[:, c0:c1], src[:, c0:c1], writes=[k], group=("cast", name, bi % 6))
            ks.append(k)
        C.cast_blk[name] = ks
        C.cast_keys[name] = ks

    C.cast_cols = cast_cols


def phase_mod(C):
    p, A, AC = C.p, C.A, C.AC
    s_bf = r3(AC.bf16(32), b=2)
    ctmp = AC.f32(32)
    C.modx = None
    oc, _ = VC["c"]
    p.op("act", lambda e: e.activation(out=ctmp, in_=C.vt[:, oc:oc + 32], func=AF.Silu),
         reads=["vecs"], writes=["ctmp"])
    p.op("dve", lambda e: e.tensor_copy(out=s_bf[:, :, 0], in_=ctmp[:, 0:16]), reads=["ctmp"], writes=["s_bf0"])
    p.op("dve", lambda e: e.tensor_copy(out=s_bf[:, :, 1], in_=ctmp[:, 16:32]), reads=["ctmp"], writes=["s_bf1"])
    wad = [r3(A.bf16(16 * 2048), b=2048) for _ in range(2)]
    src = C.w_ada.rearrange("(kc p) f -> p kc f", p=128)
    psm = C.ps[0]
    for s in range(NMOD):
        w = wad[s % 2]
        for h in range(2):
            p.dma("pool", w[:, 8 * h:8 * h + 8, :], src[:, 8 * h:8 * h + 8, s * D:(s + 1) * D],
                  writes=[("wad", s % 2)])
        for j in range(16):
            m = s * 16 + j
            for kc in range(16):
                p.op("pe", lambda e, w=w, kc=kc, j=j, m=m: e.matmul(
                    psm[:, 2 * m:2 * m + 2], w[:, kc, j * 128:(j + 1) * 128], s_bf[:, kc, :],
                    start=(kc == 0), stop=(kc == 15)),
                    reads=[("wad", s % 2), "s_bf0", "s_bf1"], writes=["psmod"], sig=(kc == 15))
    C.cast_cols("wa_i", C.wa_i, C.wa_i_b,
                [[(fb * 512, fb * 512 + 512), (DFF + fb * 512, DFF + fb * 512 + 512)] for fb in range(11)])
    C.cast_cols("wa_o", C.wa_o, C.wa_o_b, [[(ob * 256, ob * 256 + 256)] for ob in range(8)])
    C.cast("w_in", C.w_in, C.w_in_b, D, 128)
    C.cast("w_out", C.w_out, C.w_out_b, D, 128)
    C.cast("wb_i", C.wb_i, C.wb_i_b, D, 128)
    C.cast("wb_o", C.wb_o, C.wb_o_b, DFF, 128)
    C.mod = {}
    ob, _ = VC["b_ada"]
    pv = psm[:, 0:288].rearrange("p (m two) -> p m two", two=2)
    for idx, nm in enumerate(["x", "c"]):
        mt = AC.f32(144)
        p.op("dve", lambda e, mt=mt, idx=idx: e.tensor_tensor(out=mt, in0=pv[:, :, idx], in1=C.vt[:, ob:ob + 144],
                                                              op=ALU.add),
             reads=["psmod", "vecs"], writes=[("mod", nm)])
        der = AC.f32(16 * 6)
        for j in range(3):
            p.op("dve", lambda e, der=der, mt=mt, j=j: e.tensor_scalar_add(
                out=der[:, 16 * j:16 * j + 16], in0=mt[:, (3 * j + 1) * 16:(3 * j + 2) * 16], scalar1=1.0),
                reads=[("mod", nm)], writes=[("der", nm, j)])
            gs = (0.5 if j != 1 else 1.0) / ALPHA
            p.op("dve", lambda e, der=der, mt=mt, j=j, gs=gs: e.tensor_scalar_mul(
                out=der[:, 48 + 16 * j:48 + 16 * j + 16], in0=mt[:, (3 * j + 2) * 16:(3 * j + 3) * 16], scalar1=gs),
                reads=[("mod", nm)], writes=[("derg", nm, j)])
        C.mod[nm] = (mt, der)


def mod_aps(C, nm, j):
    mt, der = C.mod[nm]
    shift = mt[:, (3 * j) * 16:(3 * j + 1) * 16]
    sc1p = der[:, 16 * j:16 * j + 16]
    gate = der[:, 48 + 16 * j:48 + 16 * j + 16]
    keys = [("mod", nm), ("der", nm, j), ("derg", nm, j)]
    return shift, sc1p, gate, keys


def ffn_phase(C, tag, src, dst, tiles, wi_b, wo_b, j, lnj):
    p, A, nc = C.p, C.A, C.nc
    W = Ctx()
    W.hT = [r3(A.bf16(16 * 512), b=512) for _ in range(2)]
    W.gT = r3(A.bf16(44 * 512), b=512)
    W.z = r3(A.f32(16 * 512), b=512)
    W.ws = [A.bf16(16384) for _ in range(2)]
    W.xst = [A.f32(512) for _ in range(2)]
    W.sg = [A.f32(512) for _ in range(2)]
    W.sq = [A.f32(512) for _ in range(2)]
    W.mean = A.f32(512)
    W.var = A.f32(512)
    W.rstd = A.f32(512)
    W.ost = [A.f32(512) for _ in range(2)]
    wi_v = wi_b.rearrange("(kc p) f -> p kc f", p=128)
    wo_v = wo_b.rearrange("(kc p) f -> p kc f", p=128)
    wname_i = {"fa": "wa_i", "fb": "wb_i"}[tag]
    wname_o = {"fa": "wa_o", "fb": "wb_o"}[tag]
    wkeys_i = C.cast_keys[wname_i]
    wkeys_o = C.cast_keys[wname_o]
    og, _ = VC["ln_g"]
    ol, _ = VC["ln_b"]
    eps_p = LN_EPS / (ALPHA * ALPHA)
    st = {"w": 0, "q": 0, "x": 0, "o": 0}

    def modulate(ti):
        t0, nt, nm = tiles[ti]
        shift, sc1p, gate, mk = mod_aps(C, nm, j)
        hT = W.hT[ti % 2]
        for dc in range(16):
            xs = W.xst[st["x"] % 2]
            xk = ("xst", st["x"] % 2)
            st["x"] += 1
            p.dma("pool", xs[:, :nt], src[dc * 128:(dc + 1) * 128, t0:t0 + nt],
                  reads=[("dst", "w3", ti, dc)], writes=[xk])
            if dc % 2 == 0:
                p.op("act", lambda e, xs=xs, hT=hT, dc=dc, nt=nt: e.activation(
                    out=hT[:, dc, :nt], in_=xs[:, :nt], func=AF.Identity,
                    scale=sc1p[:, dc:dc + 1], bias=shift[:, dc:dc + 1]),
                    reads=[xk] + mk, writes=[("hT", ti % 2, dc)])
            else:
                p.op("dve", lambda e, xs=xs, hT=hT, dc=dc, nt=nt: e.tensor_scalar(
                    out=hT[:, dc, :nt], in0=xs[:, :nt], scalar1=sc1p[:, dc:dc + 1], scalar2=shift[:, dc:dc + 1],
                    op0=ALU.mult, op1=ALU.add),
                    reads=[xk] + mk, writes=[("hT", ti % 2, dc)])

    def up(ti):
        t0, nt, nm = tiles[ti]
        hT = W.hT[ti % 2]
        hk = [("hT", ti % 2, dc) for dc in range(16)]
        for fb in range(11):
            si = st["w"] % 2
            st["w"] += 1
            slot = W.ws[si].rearrange("p (w k f) -> p w k f", w=2, k=16)
            for w_ in range(2):
                c0 = w_ * DFF + fb * 512
                p.dma("sp", slot[:, w_], wi_v[:, :, c0:c0 + 512],
                      reads=([C.cast_blk[wname_i][fb]] if wname_i in C.cast_blk else wkeys_i), writes=[("ws", si)])
            for jj in range(4):
                q = st["q"]
                st["q"] += 1
                pg, pu = C.ps[2 * (q % 2)], C.ps[2 * (q % 2) + 1]
                for w_, ps in ((0, pg), (1, pu)):
                    for kc in range(16):
                        p.op("pe", lambda e, ps=ps, slot=slot, w_=w_, kc=kc, jj=jj, hT=hT, nt=nt: e.matmul(
                            ps[:, :nt], slot[:, w_, kc, jj * 128:(jj + 1) * 128], hT[:, kc, :nt],
                            start=(kc == 0), stop=(kc == 15)),
                            reads=[("ws", si)] + (hk if kc == 0 else []), writes=[("ps", 2 * (q % 2) + w_)],
                            sig=(kc == 15))
                sg = W.sg[q % 2]
                p.op("act", lambda e, sg=sg, pg=pg, nt=nt: e.activation(out=sg[:, :nt], in_=pg[:, :nt], func=AF.Silu),
                     reads=[("ps", 2 * (q % 2))], writes=[("sg", q % 2)])
                fc = fb * 4 + jj
                p.op("dve", lambda e, sg=sg, pu=pu, fc=fc, nt=nt: e.tensor_tensor(
                    out=W.gT[:, fc, :nt], in0=sg[:, :nt], in1=pu[:, :nt], op=ALU.mult),
                    reads=[("sg", q % 2), ("ps", 2 * (q % 2) + 1)], writes=[("gT", fc)])

    def down(ti):
        t0, nt, nm = tiles[ti]
        shift, sc1p, gate, mk = mod_aps(C, nm, j)
        gk = [("gT", fc) for fc in range(44)]
        pend = None

        def stats(oc, nt=nt):
            p.op("pe", lambda e: e.matmul(C.ps[6][:, :nt], C.ones, W.z[:, oc, :nt], start=(oc == 0), stop=(oc == 15)),
                 reads=["ones", ("z", oc)], writes=[("ps", 6)], sig=(oc == 15))
            p.op("pe", lambda e: e.matmul(C.ps[7][:, :nt], C.ones, W.sq[oc % 2][:, :nt], start=(oc == 0),
                                          stop=(oc == 15)),
                 reads=["ones", ("sq", oc % 2)], writes=[("ps", 7)], sig=True)

        for ob in range(8):
            si = st["w"] % 2
            st["w"] += 1
            slot = W.ws[si][:, 0:44 * 256].rearrange("p (k f) -> p k f", k=44)
            for h in range(2):
                p.dma("sp", slot[:, 22 * h:22 * h + 22, :], wo_v[:, 22 * h:22 * h + 22, ob * 256:(ob + 1) * 256],
                      reads=([C.cast_blk[wname_o][ob]] if wname_o in C.cast_blk else wkeys_o), writes=[("ws", si)])
            for o2 in range(2):
                oc = ob * 2 + o2
                p.dma("pool", W.z[:, oc, :nt], src[oc * 128:(oc + 1) * 128, t0:t0 + nt],
                      reads=[("dst", "w3", ti, oc)], writes=[("z", oc)])
                ps = C.ps[4 + oc % 2]
                for kc in range(44):
                    p.op("pe", lambda e, ps=ps, slot=slot, kc=kc, o2=o2, nt=nt: e.matmul(
                        ps[:, :nt], slot[:, kc, o2 * 128:(o2 + 1) * 128], W.gT[:, kc, :nt],
                        start=(kc == 0), stop=(kc == 43)),
                        reads=[("ws", si)] + (gk if kc == 0 else []), writes=[("ps", 4 + oc % 2)], sig=(kc == 43))
                if pend is not None:
                    stats(pend)
                p.op("dve", lambda e, ps=ps, oc=oc, nt=nt: e.scalar_tensor_tensor(
                    out=W.z[:, oc, :nt], in0=ps[:, :nt], scalar=gate[:, oc:oc + 1], in1=W.z[:, oc, :nt],
                    op0=ALU.mult, op1=ALU.add),
                    reads=[("ps", 4 + oc % 2)] + mk, writes=[("z", oc)])
                p.op("act", lambda e, oc=oc, nt=nt: e.activation(out=W.sq[oc % 2][:, :nt], in_=W.z[:, oc, :nt],
                                                                  func=AF.Square),
                     reads=[("z", oc)], writes=[("sq", oc % 2)])
                pend = oc
        stats(pend)
        inv = 1.0 / D
        p.op("act", lambda e: e.activation(out=W.mean[:, :nt], in_=C.ps[6][:, :nt], func=AF.Copy, scale=inv),
             reads=[("ps", 6)], writes=["mean"])
        p.op("act", lambda e: e.activation(out=W.var[:, :nt], in_=C.ps[6][:, :nt], func=AF.Square, scale=inv),
             reads=[("ps", 6)], writes=["var"])
        p.op("dve", lambda e: e.scalar_tensor_tensor(out=W.var[:, :nt], in0=C.ps[7][:, :nt], scalar=inv,
                                                     in1=W.var[:, :nt], op0=ALU.mult, op1=ALU.subtract),
             reads=[("ps", 7), "var"], writes=["var"])
        p.op("act", lambda e: e.activation(out=W.rstd[:, :nt], in_=W.var[:, :nt], func=AF.Ln, bias=C.epsln, scale=1.0),
             reads=["var", "epsc"], writes=["rstd"])
        p.op("act", lambda e: e.activation(out=W.rstd[:, :nt], in_=W.rstd[:, :nt], func=AF.Exp, scale=-0.5),
             reads=["rstd"], writes=["rstd"])
        for oc in range(16):
            p.op("pool", lambda e, oc=oc: e.tensor_tensor(out=W.z[:, oc, :nt], in0=W.z[:, oc, :nt], in1=W.mean[:, :nt],
                                                          op=ALU.subtract),
                 reads=[("z", oc), "mean"], writes=[("z", oc)])
            p.op("dve", lambda e, oc=oc: e.tensor_tensor(out=W.z[:, oc, :nt], in0=W.z[:, oc, :nt], in1=W.rstd[:, :nt],
                                                         op=ALU.mult),
                 reads=[("z", oc), "rstd"], writes=[("z", oc)])
            oi = st["o"] % 2
            st["o"] += 1
            ot = W.ost[oi]
            gcol = C.vt[:, og + lnj * 16 + oc:og + lnj * 16 + oc + 1]
            bcol = C.vt[:, ol + lnj * 16 + oc:ol + lnj * 16 + oc + 1]
            p.op("act", lambda e, oc=oc, ot=ot, gcol=gcol, bcol=bcol: e.activation(
                out=ot[:, :nt], in_=W.z[:, oc, :nt], func=AF.Identity, scale=gcol, bias=bcol),
                reads=[("z", oc), "vecs"], writes=[("ost", oi)])
            p.dma("act", dst[oc * 128:(oc + 1) * 128, t0:t0 + nt], ot[:, :nt],
                  reads=[("ost", oi)], writes=[("dst", tag, ti, oc)], group=("ostd", oi))

    modulate(0)
    for ti in range(len(tiles)):
        up(ti)
        if ti + 1 < len(tiles):
            modulate(ti + 1)
        down(ti)


def proj_phase(C):
    p, A = C.p, C.A
    W = Ctx()
    W.hT = [r3(A.bf16(16 * 512), b=512) for _ in range(2)]
    W.ws = [r3(A.bf16(16 * 512), b=512) for _ in range(4)]
    W.xst = [A.f32(512) for _ in range(2)]
    W.ost = [A.f32(512) for _ in range(2)]
    W.uT = r3(A.f32(8 * 512), b=512)
    W.vg = [A.f32(1024) for _ in range(4)]
    W.sqv = A.f32(1024)
    W.vn = [A.bf16(1024) for _ in range(4)]
    W.gmo = r3(A.bf16(8 * 512), b=512)
    W.st = [A.f32(16) for _ in range(6)]
    lng = A.f32(1024)
    lnb = A.f32(1024)
    bsb = r3(A.f32(16 * 128), b=128)
    wsT = r3(A.bf16(16 * 128), b=128)
    p.dma("sp", lng, C.gm_ln_g.to_broadcast([128, 1024]) if False else bcast_rows(C.gm_ln_g, 1024), writes=["lng"])
    p.dma("sp", lnb, bcast_rows(C.gm_ln_b, 1024), writes=["lnb"])
    p.dma("sp", bsb, bcast_rows(C.gm_bs, 2048).rearrange("p (a b) -> p a b", b=128), writes=["bsb"])
    p.dma("pool", wsT, C.gm_wsT, writes=["wsT"])
    c128 = A.bf16(128)
    bs_hi = r3(A.bf16(16 * 128), b=128)
    bs_lo = r3(A.bf16(16 * 128), b=128)
    p.op("act", lambda e: e.activation(out=c128, in_=C.ones, func=AF.Copy, scale=1.0 / 128), reads=["ones"],
         writes=["c128"])
    p.op("dve", lambda e: e.tensor_copy(out=bs_hi, in_=bsb), reads=["bsb"], writes=["bs_hi"])
    p.op("dve", lambda e: e.tensor_tensor(out=bs_lo, in0=bsb, in1=bs_hi, op=ALU.subtract), reads=["bsb", "bs_hi"],
         writes=["bs_lo"])
    w_v = C.w_in_b.rearrange("(kc p) f -> p kc f", p=128)
    wkeys = C.cast_keys["w_in"]
    shift, sc1p, gate, mkx = None, None, None, None
    tiles = [(512 * i, 512, "x") for i in range(8)] + [(T, CT, "c")]
    st = {"w": 0, "x": 0, "o": 0, "ps": 0}

    def modulate(ti):
        t0, nt, nm = tiles[ti]
        shift, sc1p, gate, mk = mod_aps(C, nm, 1)
        hT = W.hT[ti % 2]
        for dc in range(16):
            xs = W.xst[st["x"] % 2]
            xk = ("xst", st["x"] % 2)
            st["x"] += 1
            p.dma("pool", xs[:, :nt], C.x1T[dc * 128:(dc + 1) * 128, t0:t0 + nt],
                  reads=[("dst", "fa", ti, dc)], writes=[xk])
            if dc % 2 == 0:
                p.op("act", lambda e, xs=xs, hT=hT, dc=dc, nt=nt, sc1p=sc1p, shift=shift: e.activation(
                    out=hT[:, dc, :nt], in_=xs[:, :nt], func=AF.Identity,
                    scale=sc1p[:, dc:dc + 1], bias=shift[:, dc:dc + 1]),
                    reads=[xk] + mk, writes=[("hT", ti % 2, dc)])
            else:
                p.op("dve", lambda e, xs=xs, hT=hT, dc=dc, nt=nt, sc1p=sc1p, shift=shift: e.tensor_scalar(
                    out=hT[:, dc, :nt], in0=xs[:, :nt], scalar1=sc1p[:, dc:dc + 1], scalar2=shift[:, dc:dc + 1],
                    op0=ALU.mult, op1=ALU.add),
                    reads=[xk] + mk, writes=[("hT", ti % 2, dc)])

    shq = []

    def pull_shift(n):
        while n > 0 and shq:
            try:
                next(shq[0])
                n -= 1
            except StopIteration:
                shq.pop(0)

    def load_w(f0, nf):
        pull_shift(3)
        si = st["w"] % 4
        st["w"] += 1
        slot = W.ws[si]
        p.dma("sp", slot[:, :, :nf], w_v[:, :, f0:f0 + nf], reads=wkeys, writes=[("pws", si)])
        return slot, ("pws", si)

    def nextps():
        i = st["ps"] % 8
        st["ps"] += 1
        return C.ps[i], ("ps", i)

    def tile(ti):
        t0, nt, nm = tiles[ti]
        hT = W.hT[ti % 2]
        hk = [("hT", ti % 2, dc) for dc in range(16)]
        if nm == "x":
            vs = [load_w(4352 + 512 * h, 512) for h in range(2)]
            for h in range(2):
                slot, sk = vs[h]
                for tb in range(4):
                    ps, pk = nextps()
                    for kc in range(16):
                        p.op("pe", lambda e, ps=ps, slot=slot, kc=kc, tb=tb: e.matmul(
                            ps[:, :], hT[:, kc, tb * 128:(tb + 1) * 128], slot[:, kc, :],
                            start=(kc == 0), stop=(kc == 15)),
                            reads=[sk] + (hk if kc == 0 else []), writes=[pk], sig=(kc == 15))
                    p.op("act", lambda e, ps=ps, tb=tb, h=h: e.activation(
                        out=W.vg[tb][:, h * 512:(h + 1) * 512], in_=ps[:, :], func=AF.Gelu_apprx_tanh),
                        reads=[pk], writes=[("vg", tb, h)])
        if nm == "x":
            for tb in range(4):
                vg = W.vg[tb]
                vg3 = vg.rearrange("p (g d) -> p g d", d=64)
                s1, s2, mean, var, rstd, nm_ = W.st
                vk = [("vg", tb, 0), ("vg", tb, 1)]
                p.op("dve", lambda e, vg3=vg3: e.tensor_reduce(out=s1, in_=vg3, axis=mybir.AxisListType.X, op=ALU.add),
                     reads=vk, writes=["gs1"])
                p.op("act", lambda e, vg=vg: e.activation(out=W.sqv, in_=vg, func=AF.Square), reads=vk, writes=["sqv"])
                p.op("dve", lambda e: e.tensor_reduce(out=s2, in_=W.sqv.rearrange("p (g d) -> p g d", d=64),
                                                      axis=mybir.AxisListType.X, op=ALU.add),
                     reads=["sqv"], writes=["gs2"])
                p.op("dve", lambda e: e.tensor_scalar_mul(out=mean, in0=s1, scalar1=1.0 / 64), reads=["gs1"],
                     writes=["gmean"])
                p.op("dve", lambda e: e.tensor_tensor(out=var, in0=mean, in1=mean, op=ALU.mult), reads=["gmean"],
                     writes=["gvar"])
                p.op("dve", lambda e: e.scalar_tensor_tensor(out=var, in0=s2, scalar=1.0 / 64, in1=var, op0=ALU.mult,
                                                             op1=ALU.subtract),
                     reads=["gs2", "gvar"], writes=["gvar"])
                p.op("act", lambda e: e.activation(out=rstd, in_=var, func=AF.Ln, bias=C.epsg, scale=1.0),
                     reads=["gvar", "epsc2"], writes=["grstd"])
                p.op("act", lambda e: e.activation(out=rstd, in_=rstd, func=AF.Exp, scale=-0.5), reads=["grstd"],
                     writes=["grstd"])
                mb = mean.rearrange("p (g o) -> p g o", o=1).to_broadcast([128, 16, 64])
                rb = rstd.rearrange("p (g o) -> p g o", o=1).to_broadcast([128, 16, 64])
                p.op("pool", lambda e, vg3=vg3, mb=mb: e.tensor_tensor(out=vg3, in0=vg3, in1=mb, op=ALU.subtract),
                     reads=vk + ["gmean"], writes=vk)
                p.op("dve", lambda e, vg3=vg3, rb=rb: e.tensor_tensor(out=vg3, in0=vg3, in1=rb, op=ALU.mult),
                     reads=vk + ["grstd"], writes=vk)
                p.op("pool", lambda e, vg=vg: e.tensor_tensor(out=vg, in0=vg, in1=lng, op=ALU.mult),
                     reads=vk + ["lng"], writes=vk)
                vn = W.vn[tb]
                p.op("dve", lambda e, vg=vg, vn=vn: e.tensor_tensor(out=vn, in0=vg, in1=lnb, op=ALU.add),
                     reads=vk + ["lnb"], writes=[("vn", tb)])
        nfm = 34 if nm == "x" else 26
        fc = 0
        while fc < nfm:
            nchunk = min(4, nfm - fc) if fc != 24 else 2
            slot, sk = load_w(fc * 128, nchunk * 128)
            for jj in range(nchunk):
                ps, pk = nextps()
                for kc in range(16):
                    p.op("pe", lambda e, ps=ps, slot=slot, kc=kc, jj=jj, nt=nt: e.matmul(
                        ps[:, :nt], slot[:, kc, jj * 128:(jj + 1) * 128], hT[:, kc, :nt],
                        start=(kc == 0), stop=(kc == 15)),
                        reads=[sk] + (hk if kc == 0 else []), writes=[pk], sig=(kc == 15))
                f = fc + jj
                if f < 26:
                    oi = st["o"] % 2
                    st["o"] += 1
                    ot = W.ost[oi]
                    p.op("act", lambda e, ot=ot, ps=ps, nt=nt: e.activation(out=ot[:, :nt], in_=ps[:, :nt],
                                                                          func=AF.Copy),
                         reads=[pk], writes=[("post", oi)])
                    p.dma("act", C.pT[f * 128:(f + 1) * 128, t0:t0 + nt], ot[:, :nt],
                          reads=[("post", oi)], writes=[("pT", ti, f)], group=("postd", oi))
                else:
                    p.op("act", lambda e, ps=ps, f=f, nt=nt: e.activation(
                        out=W.uT[:, f - 26, :nt], in_=ps[:, :nt], func=AF.Gelu_apprx_tanh),
                        reads=[pk], writes=[("uT", f - 26)])
            fc += nchunk
        if nm != "x":
            return
        for tb in range(4):
            vn = W.vn[tb]
            for g in range(16):
                if g % 4 == 0:
                    ps, pk = nextps()
                po = ps[:, (g % 4) * 128:(g % 4 + 1) * 128]
                p.op("pe", lambda e, po=po, vn=vn, g=g: e.matmul(
                    po, vn[:, (g // 2) * 128:(g // 2 + 1) * 128], wsT[:, g, :], start=True, stop=False),
                    reads=[("vn", tb), "wsT"], writes=[pk], sig=False)
                p.op("pe", lambda e, po=po, g=g: e.matmul(po, c128, bs_hi[:, g, :], start=False, stop=False),
                     reads=["c128", "bs_hi"], writes=[pk], sig=False)
                p.op("pe", lambda e, po=po, g=g: e.matmul(po, c128, bs_lo[:, g, :], start=False, stop=True),
                     reads=["c128", "bs_lo"], writes=[pk])
                if g % 4 == 3:
                    b4_ = g // 4
                    psv4 = ps[:, :].rearrange("p (c m) -> p c m", m=128)
                    for hh in range(2):
                        rows = slice(64 * hh, 64 * hh + 64)
                        p.op("dve", lambda e, psv4=psv4, rows=rows, hh=hh, b4_=b4_, tb=tb: e.tensor_tensor(
                            out=W.gmo[rows, 2 * b4_:2 * b4_ + 2, tb * 128:(tb + 1) * 128],
                            in0=psv4[rows, hh::2, :],
                            in1=W.uT[rows, 2 * b4_:2 * b4_ + 2, tb * 128:(tb + 1) * 128], op=ALU.mult),
                            reads=[pk, ("uT", 2 * b4_), ("uT", 2 * b4_ + 1)],
                            writes=[("gmo", 2 * b4_, hh), ("gmo", 2 * b4_ + 1, hh)])
        p.dma("pool", C.catT[1024:2048, t0:t0 + nt].rearrange("(cc p) t -> p cc t", p=128), W.gmo[:, :, :nt],
              reads=[("gmo", c, hh) for c in range(8) for hh in range(2)], writes=[("catg", ti)], group="gmod")

    do_shift = shift_setup(C)
    modulate(0)
    for ti in range(len(tiles)):
        if ti + 1 < len(tiles):
            modulate(ti + 1)
        if 2 <= ti <= 8:
            shq.append(do_shift(ti - 2))
        tile(ti)
    shq.append(do_shift(7))
    shq.append(do_shift(8))
    pull_shift(1000)


def bcast_rows(ap1d, n):
    return bass.AP(ap1d.tensor, ap1d.offset, [[0, 128], [1, n]])


def shift_setup(C):
    p, A = C.p, C.A
    X = [A.f32(640) for _ in range(2)]
    XS = [A.f32(512) for _ in range(2)]
    om, _ = VC["mu"]
    o4, _ = VC["sel4"]
    o2, _ = VC["sel2"]
    omm = A.f32(26)
    m4 = A.f32(26 * 4).rearrange("p (f j) -> p f j", j=4)
    m2 = A.f32(26 * 2).rearrange("p (f j) -> p f j", j=2)
    mu = C.vt[:, om:om + 26]
    p.op("dve", lambda e: e.tensor_scalar(out=omm, in0=mu, scalar1=-1.0, scalar2=1.0, op0=ALU.mult, op1=ALU.add),
         reads=["vecs"], writes=["omm"])
    for jj in range(4):
        p.op("dve", lambda e, jj=jj: e.tensor_scalar_mul(out=m4[:, :, jj], in0=mu, scalar1=C.vt[:, o4 + jj:o4 + jj + 1]),
             reads=["vecs"], writes=[("m4", jj)])
    for jj in range(2):
        p.op("dve", lambda e, jj=jj: e.tensor_scalar_mul(out=m2[:, :, jj], in0=mu, scalar1=C.vt[:, o2 + jj:o2 + jj + 1]),
             reads=["vecs"], writes=[("m2", jj)])
    mk = ["omm"] + [("m4", j) for j in range(4)] + [("m2", j) for j in range(2)]
    cnt = {"n": 0}

    def do_block(blk):
        for fc in range(26):
            xi = cnt["n"] % 2
            cnt["n"] += 1
            x, xs = X[xi], XS[xi]
            xk, sk = ("shx", xi), ("shs", xi)
            if blk < 8:
                t0 = blk * 512
                lo = max(t0 - 64, 0)
                hi = min(t0 + 576, T)
                p.dma("sp", x[:, lo - (t0 - 64):hi - (t0 - 64)], C.pT[fc * 128:(fc + 1) * 128, lo:hi],
                      reads=[("pT", ti, fc) for ti in range(max(blk - 1, 0), min(blk + 2, 8))], writes=[xk])
                nt = 512
                cur = x[:, 64:576]
                p.op("act", lambda e, xs=xs, cur=cur, fc=fc: e.activation(out=xs, in_=cur, func=AF.Copy,
                                                                           scale=omm[:, fc:fc + 1]),
                     reads=[xk] + mk, writes=[sk])
                xs3 = xs.rearrange("p (r c) -> p r c", c=64)
                cur3 = cur.rearrange("p (r c) -> p r c", c=64)
                p.op("dve", lambda e, xs3=xs3, cur3=cur3, fc=fc: e.scalar_tensor_tensor(
                    out=xs3[:, :, 1:64], in0=cur3[:, :, 0:63], scalar=m4[:, fc, 0:1], in1=xs3[:, :, 1:64],
                    op0=ALU.mult, op1=ALU.add), reads=[xk, sk], writes=[sk])
                p.op("dve", lambda e, xs3=xs3, cur3=cur3, fc=fc: e.scalar_tensor_tensor(
                    out=xs3[:, :, 0:63], in0=cur3[:, :, 1:64], scalar=m4[:, fc, 1:2], in1=xs3[:, :, 0:63],
                    op0=ALU.mult, op1=ALU.add), reads=[xk, sk], writes=[sk])
                l0 = 64 if blk == 0 else 0
                p.op("dve", lambda e, xs=xs, x=x, fc=fc, l0=l0: e.scalar_tensor_tensor(
                    out=xs[:, l0:512], in0=x[:, l0:512], scalar=m4[:, fc, 2:3], in1=xs[:, l0:512],
                    op0=ALU.mult, op1=ALU.add), reads=[xk, sk], writes=[sk])
                h0 = 448 if blk == 7 else 512
                p.op("dve", lambda e, xs=xs, x=x, fc=fc, h0=h0: e.scalar_tensor_tensor(
                    out=xs[:, 0:h0], in0=x[:, 128:128 + h0], scalar=m4[:, fc, 3:4], in1=xs[:, 0:h0],
                    op0=ALU.mult, op1=ALU.add), reads=[xk, sk], writes=[sk])
            else:
                t0, nt = T, CT
                p.dma("sp", x[:, 0:CT], C.pT[fc * 128:(fc + 1) * 128, T:TT], reads=[("pT", 8, fc)], writes=[xk])
                cur = x[:, 0:CT]
                p.op("act", lambda e, xs=xs, cur=cur, fc=fc: e.activation(out=xs[:, 0:CT], in_=cur, func=AF.Copy,
                                                                           scale=omm[:, fc:fc + 1]),
                     reads=[xk] + mk, writes=[sk])
                p.op("dve", lambda e, xs=xs, x=x, fc=fc: e.scalar_tensor_tensor(
                    out=xs[:, 1:CT], in0=x[:, 0:CT - 1], scalar=m2[:, fc, 0:1], in1=xs[:, 1:CT],
                    op0=ALU.mult, op1=ALU.add), reads=[xk, sk], writes=[sk])
                p.op("dve", lambda e, xs=xs, x=x, fc=fc: e.scalar_tensor_tensor(
                    out=xs[:, 0:CT - 1], in0=x[:, 1:CT], scalar=m2[:, fc, 1:2], in1=xs[:, 0:CT - 1],
                    op0=ALU.mult, op1=ALU.add), reads=[xk, sk], writes=[sk])
            p.dma("pool", C.xsT[fc * 128:(fc + 1) * 128, t0:t0 + nt], xs[:, :nt], reads=[sk],
                  writes=[("xsT", blk, fc)], group=("shsd", xi))
            yield


    return do_block

def bc3(ap, n):
    return ap.rearrange("p (o m) -> p o m", o=1).to_broadcast([128, n, ap.shape[1]])


def mixer_consts(C):
    p, A, AR, AC = C.p, C.A, C.AR, C.AC
    M = Ctx()
    C.M = M
    cm = AC.f32(7 * 128).rearrange("p (a b) -> p a b", b=128)
    p.dma("sp", cm, C.cmat, writes=["cmat"])
    M.ident = AR.f32(128)
    M.identf = cm[:, 0, :]
    p.op("act", lambda e: e.activation(out=M.ident, in_=cm[:, 0, :], func=AF.Copy), reads=["cmat"], writes=["ident"])
    M.ml, M.mu, M.mui, M.bones = cm[:, 1, :], cm[:, 2, :], cm[:, 3, :], cm[:, 4, :]
    M.rm01 = cm[:, 6, 0:1]
    M.segm = A.f32(512)
    p.op("dve", lambda e: e.tensor_copy(out=M.segm.rearrange("p (c t) -> p c t", t=64), in_=bc3(cm[:, 5, 0:64], 8)),
         reads=["cmat"], writes=["segm"])
    M.gup = A.bf16(1024)
    p.dma("pool", M.gup, C.g_up, writes=["gup"])
    M.par = {}
    for nm in ["k_k", "k_a", "r_k"]:
        o, w = VC[nm]
        t = A.f32(512)
        p.op("dve", lambda e, t=t, o=o: e.tensor_copy(
            out=t.rearrange("p (c t) -> p c t", t=64),
            in_=C.vt[:, o:o + 8].rearrange("p (c o) -> p c o", o=1).to_broadcast([128, 8, 64])),
            reads=["vecs"], writes=[("par", nm)])
        M.par[nm] = t
    t = A.f32(512)
    p.op("dve", lambda e, t=t: e.tensor_scalar(out=t, in0=M.par["k_a"], scalar1=-1.0, scalar2=1.0, op0=ALU.mult,
                                               op1=ALU.add), reads=[("par", "k_a")], writes=[("par", "omka")])
    M.par["omka"] = t
    M.eps30 = AC.f32(2)[:, 0:1]
    p.op("pool", lambda e: e.memset(M.eps30, 1e-30), writes=["eps30"])
    M.epsgn = AC.f32(2)[:, 0:1]
    p.op("pool", lambda e: e.memset(M.epsgn, GN_EPS), writes=["epsgn"])


def mixer_pass(C, d):
    p, A, AR, M = C.p, C.A, C.AR, C.M
    rev = (d == 1)
    c0 = float(np.exp(-0.5))
    W = Ctx()
    lw = A.bf16(2 * 1024).rearrange("p (w f) -> p w f", w=2)
    p.dma("pool", lw, C.lora[:, d], writes=["lw"])
    par = dict(M.par)
    for nm in ["w0", "a0"]:
        o, w = VC[nm]
        t = A.f32(512)
        p.op("dve", lambda e, t=t, o=o: e.tensor_copy(
            out=t.rearrange("p (c t) -> p c t", t=64),
            in_=C.vt[:, o + 8 * d:o + 8 * d + 8].rearrange("p (c o) -> p c o", o=1).to_broadcast([128, 8, 64])),
            reads=["vecs"], writes=[("par", nm)])
        par[nm] = t
    f3 = lambda n=512: A.f32(n)
    big = lambda dt=None: (AR if dt is F32R else A).f32(1024).rearrange("p (c m) -> p c m", m=128)
    v3 = lambda ap: ap.rearrange("p (c t) -> p c t", t=64)
    EX = [dict(Ax=big(F32R), Rx=big(F32R), b2=big(F32R), k2=big(F32R), v2=big(F32R), gam=A.f32(8))
          for _ in range(3)]
    PR = [dict(AakT=big(F32R), RtF=big(F32R)) for _ in range(2)]
    RbT, RkT = big(F32R), big(F32R)
    Bstk, Kstk, Vstk = big(F32R), big(F32R), big(F32R)
    S = [big(F32R), big(F32R)]
    St = [big(F32R), big(F32R)]
    RtT = big(F32R)
    Pt, Ut = big(F32R), big(F32R)
    H = [big(F32R), big(F32R)]
    Hm = [big(), big()]
    yT = f3()
    ldn = dict(r=f3(), k=f3(), v=f3(), x24=A.f32(64), x25=A.f32(64))
    if rev:
        ldr = dict(r=f3(), k=f3(), v=f3(), x24=A.f32(64))
        ld = dict(ldr)
        ld["x25"] = ldn["x25"]
        LK = "ldr"
    else:
        ld = ldn
        LK = "ld"
    lo24 = A.bf16(64)
    sg25 = A.bf16(64)
    tmp = {n: f3() for n in ["s", "ic", "kk", "nkk", "kd", "bb", "rk", "sR", "cs", "csm", "Er", "Eb", "t"]}
    gst = f3()
    for hi in range(2):
        p.op("act", lambda e, h=H[hi]: e.activation(
            out=h, in_=M.bones.rearrange("p (o m) -> p o m", o=1).to_broadcast([128, 8, 128]), func=AF.Copy,
            scale=0.0), reads=["cmat"], writes=[("H", hi, 0), ("H", hi, 1)])
        p.op("pool", lambda e, h=Hm[hi]: e.memset(h, 0.0), writes=[("Hm", hi, 0), ("Hm", hi, 1)])
    chunks = [(T + 64 * i, True) for i in range(4)] + [(64 * i, False) for i in range(64)]
    if rev:
        chunks = [(T + 64 * i, True) for i in reversed(range(4))] + [(64 * i, False) for i in reversed(range(64))]
    NCH = len(chunks)
    st = {"ps": 0}
    bm3 = M.bones.rearrange("p (h t) -> p h t", t=64)
    bm4 = M.bones.rearrange("p (o h t) -> p o h t", o=1, t=64).to_broadcast([128, 8, 2, 64])

    def R3(ap):
        a = v3(ap)
        return a[:, :, ::-1] if rev else a

    def nextps():
        i = st["ps"] % 8
        st["ps"] += 1
        return C.ps[i], ("ps", i)

    def mm8(lhs_fn, rhs_fn, reads, n=128, extra=None):
        terms = [(lhs_fn, rhs_fn)] + (extra or [])
        out = []
        for g in range(2):
            ps, pk = nextps()
            for c4 in range(4):
                cc = 4 * g + c4
                for ti, (lf, rf) in enumerate(terms):
                    p.op("pe", lambda e, ps=ps, c4=c4, cc=cc, lf=lf, rf=rf, ti=ti: e.matmul(
                        ps[:, c4 * n:(c4 + 1) * n], lf(cc), rf(cc), start=(ti == 0), stop=(ti == len(terms) - 1)),
                        reads=(reads(g) if callable(reads) else reads), writes=[pk], sig=(ti == len(terms) - 1))
            out.append((ps, pk))
        return out

    def grp(t, g):
        return t[:, 4 * g:4 * g + 4, :]

    def psv(ps, n=128):
        return ps[:, 0:4 * n].rearrange("p (c m) -> p c m", m=n)

    def prep(ci):
        col, is_ctx = chunks[ci]
        ex = EX[ci % 3]
        ek = ("ex", ci % 3)
        blk = (col // 512) if not is_ctx else 8
        xkeys = lambda f0: [("xsT", blk, f0 + c) for c in range(8)]
        for nm, r0 in (("r", 0), ("k", 1024), ("v", 2048)):
            p.dma("sp", v3(ldn[nm]), C.xsT[r0:r0 + 1024, col:col + 64].rearrange("(c p) t -> p c t", p=128),
                  reads=xkeys(r0 // 128), writes=[("ld", nm)])
        p.dma("sp", ldn["x24"], C.xsT[3072:3200, col:col + 64], reads=[("xsT", blk, 24)], writes=[("ld", "x24")])
        if rev:
            for nm in ("r", "k", "v"):
                p.op("pool", lambda e, nm=nm: e.tensor_copy(out=v3(ldr[nm]), in_=v3(ldn[nm])[:, :, ::-1]),
                     reads=[("ld", nm)], writes=[("ldr", nm)])
            p.op("pool", lambda e: e.tensor_copy(out=ldr["x24"], in_=ldn["x24"][:, ::-1]),
                 reads=[("ld", "x24")], writes=[("ldr", "x24")])
        p.op("act", lambda e: e.activation(out=lo24[0:64, :], in_=ld["x24"][0:64, :], func=AF.Tanh),
             reads=[(LK, "x24")], writes=["lo24a"])
        p.op("dve", lambda e: e.tensor_copy(out=lo24[64:128, :], in_=ld["x24"][64:128, :]),
             reads=[(LK, "x24")], writes=["lo24b"])
        yield
        for nm, wi, pn in (("s", 0, "w0"), ("ic", 1, "a0")):
            pss = mm8(lambda cc, wi=wi: lw[:, wi, cc * 128:(cc + 1) * 128], lambda cc: lo24[:, :],
                      ["lw", "lo24a", "lo24b"], n=64)
            for g, (ps, pk) in enumerate(pss):
                dst = tmp[nm][:, 256 * g:256 * g + 256]
                p.op("dve", lambda e, ps=ps, dst=dst, g=g, pn=pn: e.tensor_tensor(
                    out=dst, in0=ps[:, 0:256], in1=par[pn][:, 256 * g:256 * g + 256], op=ALU.add),
                    reads=[pk, ("par", pn)], writes=[("tmp", nm, g)])
            p.op("act", lambda e, nm=nm: e.activation(out=tmp[nm], in_=tmp[nm], func=AF.Sigmoid),
                 reads=[("tmp", nm, 0), ("tmp", nm, 1)], writes=[("tmp", nm)])
        yield
        p.op("dve", lambda e: e.tensor_tensor(out=tmp["kk"], in0=ld["k"], in1=par["k_k"], op=ALU.mult),
             reads=[(LK, "k"), ("par", "k_k")], writes=[("tmp", "kk")])
        p.op("act", lambda e: e.activation(out=tmp["t"], in_=tmp["kk"], func=AF.Square),
             reads=[("tmp", "kk")], writes=[("tmp", "t")])
        pss = mm8(lambda cc: M.bones, lambda cc: tmp["t"][:, cc * 64:(cc + 1) * 64], ["cmat", ("tmp", "t")], n=64)
        for g, (ps, pk) in enumerate(pss):
            dst = tmp["nkk"][:, 256 * g:256 * g + 256]
            p.op("act", lambda e, ps=ps, dst=dst: e.activation(out=dst, in_=ps[:, 0:256], func=AF.Ln, bias=M.eps30,
                                                              scale=1.0),
                 reads=[pk, "eps30"], writes=[("tmp", "nkk", g)])
        p.op("act", lambda e: e.activation(out=tmp["nkk"], in_=tmp["nkk"], func=AF.Exp, scale=-0.5),
             reads=[("tmp", "nkk", 0), ("tmp", "nkk", 1)], writes=[("tmp", "nkk")])
        p.op("dve", lambda e: e.tensor_tensor(out=tmp["kk"], in0=tmp["kk"], in1=tmp["nkk"], op=ALU.mult),
             reads=[("tmp", "kk"), ("tmp", "nkk")], writes=[("tmp", "kk")])
        p.op("act", lambda e: e.mul(out=tmp["nkk"], in_=tmp["kk"], mul=-1.0),
             reads=[("tmp", "kk")], writes=[("tmp", "nkk")])
        yield
        p.op("dve", lambda e: e.tensor_tensor(out=tmp["kd"], in0=tmp["ic"], in1=par["k_a"], op=ALU.mult),
             reads=[("tmp", "ic"), ("par", "k_a")], writes=[("tmp", "kd")])
        p.op("pool", lambda e: e.tensor_tensor(out=tmp["kd"], in0=tmp["kd"], in1=par["omka"], op=ALU.add),
             reads=[("tmp", "kd"), ("par", "omka")], writes=[("tmp", "kd")])
        p.op("pool", lambda e: e.tensor_tensor(out=tmp["kd"], in0=tmp["kd"], in1=ld["k"], op=ALU.mult),
             reads=[("tmp", "kd"), (LK, "k")], writes=[("tmp", "kd")])
        p.op("pool", lambda e: e.tensor_tensor(out=tmp["bb"], in0=tmp["kk"], in1=tmp["ic"], op=ALU.mult),
             reads=[("tmp", "kk"), ("tmp", "ic")], writes=[("tmp", "bb")])
        yield
        if not is_ctx:
            p.op("pool", lambda e: e.tensor_tensor(out=tmp["rk"], in0=ld["r"], in1=par["r_k"], op=ALU.mult),
                 reads=[(LK, "r"), ("par", "r_k")], writes=[("tmp", "rk")])
            p.op("pool", lambda e: e.tensor_tensor(out=tmp["rk"], in0=tmp["rk"], in1=tmp["kd"], op=ALU.mult),
                 reads=[("tmp", "rk"), ("tmp", "kd")], writes=[("tmp", "rk")])
            pss = mm8(lambda cc: M.bones, lambda cc: tmp["rk"][:, cc * 64:(cc + 1) * 64], ["cmat", ("tmp", "rk")],
                      n=64)
            for g, (ps, pk) in enumerate(pss):
                go = v3(gst)[:, 4 * g:4 * g + 4, :]
                if rev:
                    go = go[:, :, ::-1]
                p.op("dve", lambda e, ps=ps, g=g, go=go: e.tensor_tensor(
                    out=go, in0=ps[:, 0:256].rearrange("p (c t) -> p c t", t=64),
                    in1=v3(ld["v"])[:, 4 * g:4 * g + 4, :], op=ALU.mult), reads=[pk, (LK, "v")],
                    writes=[("gst", g)])
            p.dma("pool", C.bonT[d][:, col:col + 64].rearrange("(c p) t -> p c t", p=128), v3(gst),
                  reads=[("gst", 0), ("gst", 1)], writes=[("bonT", d, ci)], group="gstd")
            if d == 0:
                p.dma("sp", ld["x25"], C.xsT[3200:3328, col:col + 64], reads=[("xsT", blk, 25)],
                      writes=[("ld", "x25")])
                p.op("act", lambda e: e.activation(out=sg25, in_=ld["x25"], func=AF.Sigmoid),
                     reads=[("ld", "x25")], writes=["sg25"])
                pss = mm8(lambda cc: M.gup[:, cc * 128:(cc + 1) * 128], lambda cc: sg25[:, :], ["gup", "sg25"], n=64)
                for g, (ps, pk) in enumerate(pss):
                    p.op("act", lambda e, ps=ps, g=g: e.activation(out=gst[:, 256 * g:256 * g + 256],
                                                                   in_=ps[:, 0:256], func=AF.Copy),
                         reads=[pk], writes=[("gst", g)])
                p.dma("act", C.gTs[:, col:col + 64].rearrange("(c p) t -> p c t", p=128), v3(gst),
                      reads=[("gst", 0), ("gst", 1)], writes=[("gTs", ci)], group="gstd_g")
            yield
        sR, sRk = tmp["s"], ("tmp", "s")
        p.op("dve", lambda e: e.tensor_tensor_scan(out=tmp["cs"], data0=M.segm, data1=sR, initial=0.0,
                                                   op0=ALU.mult, op1=ALU.add),
             reads=[sRk, "segm"], writes=[("tmp", "cs")])
        p.op("act", lambda e: e.activation(out=tmp["Er"], in_=tmp["cs"], func=AF.Exp, scale=-c0),
             reads=[("tmp", "cs")], writes=[("tmp", "Er")])
        p.op("act", lambda e: e.activation(out=tmp["Eb"], in_=tmp["cs"], func=AF.Exp, scale=c0),
             reads=[("tmp", "cs")], writes=[("tmp", "Eb")])
        p.op("pool", lambda e: e.tensor_tensor(out=tmp["csm"], in0=tmp["cs"], in1=sR, op=ALU.subtract),
             reads=[("tmp", "cs"), sRk], writes=[("tmp", "csm")])
        p.op("act", lambda e: e.activation(out=tmp["csm"], in_=tmp["csm"], func=AF.Exp, scale=-c0),
             reads=[("tmp", "csm")], writes=[("tmp", "csm")])
        yield
        e4 = lambda t: t.rearrange("p c (h t) -> p c h t", t=64)
        b4 = lambda ap: v3(ap).rearrange("p c (o t) -> p c o t", o=1).to_broadcast([128, 8, 2, 64])
        p.op("pool", lambda e: e.tensor_tensor(out=v3(tmp["t"]), in0=v3(tmp["nkk"]), in1=v3(tmp["csm"]), op=ALU.mult),
             reads=[("tmp", "nkk"), ("tmp", "csm")], writes=[("tmp", "t")])
        p.op("pool", lambda e: e.tensor_tensor(out=e4(ex["Ax"]), in0=b4(tmp["t"]), in1=bm4, op=ALU.mult),
             reads=[("tmp", "t"), "cmat"], writes=[ek + ("Ax",)])
        b4r = b4
        p.op("dve", lambda e: e.tensor_tensor(out=e4(ex["b2"]), in0=b4r(tmp["bb"]), in1=b4(tmp["Eb"]), op=ALU.mult),
             reads=[("tmp", "bb"), ("tmp", "Eb")], writes=[ek + ("b",)])
        p.op("pool", lambda e: e.tensor_tensor(out=e4(ex["k2"]), in0=b4r(tmp["kd"]), in1=b4(tmp["Eb"]), op=ALU.mult),
             reads=[("tmp", "kd"), ("tmp", "Eb")], writes=[ek + ("k",)])
        p.op("dve", lambda e: e.tensor_copy(out=e4(ex["v2"]), in_=b4r(ld["v"])), reads=[(LK, "v")],
             writes=[ek + ("v",)])
        p.op("pool", lambda e: e.tensor_copy(out=ex["gam"], in_=v3(tmp["Er"])[:, :, 63]),
             reads=[("tmp", "Er")], writes=[ek + ("gam",)])
        if not is_ctx:
            p.op("pool", lambda e: e.tensor_tensor(out=v3(tmp["t"]), in0=v3(ld["r"]), in1=v3(tmp["Er"]), op=ALU.mult),
                 reads=[(LK, "r"), ("tmp", "Er")], writes=[("tmp", "t")])
            p.op("pool", lambda e: e.tensor_tensor(out=e4(ex["Rx"]), in0=b4(tmp["t"]), in1=bm4, op=ALU.mult),
                 reads=[("tmp", "t"), "cmat"], writes=[ek + ("Rx",)])
        yield

    def stage_ab(ci):
        col, is_ctx = chunks[ci]
        ex = EX[ci % 3]
        ek = ("ex", ci % 3)
        pr = PR[ci % 2]
        qk = ("pr", ci % 2)
        Ax = lambda cc: ex["Ax"][:, cc, :]
        Bb = lambda cc: ex["b2"][:, cc, :]
        Kb = lambda cc: ex["k2"][:, cc, :]

        def masked(pss, dst, mask, dk):
            for g, (ps, pk) in enumerate(pss):
                p.op("dve", lambda e, ps=ps, g=g: e.tensor_tensor(
                    out=grp(dst, g), in0=psv(ps), in1=mask.rearrange("p (o m) -> p o m", o=1).to_broadcast(
                        [128, 4, 128]), op=ALU.mult), reads=[pk, "cmat"], writes=[dk + (g,)])

        masked(mm8(Ax, Bb, [ek + ("Ax",), ek + ("b",)]), S[0], M.ml, ("S", 0))
        yield
        masked(mm8(Bb, Ax, [ek + ("Ax",), ek + ("b",)]), St[0], M.mu, ("St", 0))
        yield
        masked(mm8(Kb, Ax, [ek + ("Ax",), ek + ("k",)]), pr["AakT"], M.mu, qk + ("AakT",))
        rts = [RtT, pr["RtF"]]
        rtk = [("RtT",), qk + ("RtF",)]
        for g in range(2):
            p.op("pool", lambda e, g=g: e.tensor_tensor(
                out=grp(rts[0], g), in0=grp(St[0], g),
                in1=M.identf.rearrange("p (o m) -> p o m", o=1).to_broadcast([128, 4, 128]), op=ALU.add),
                reads=[("St", 0, g), "ident"], writes=[rtk[0] + (g,)])
        yield
        for j in range(1, 6):
            a, b = (j - 1) % 2, j % 2
            sk_prev = lambda g, a=a: [("S", a, g), ("St", a, g)]
            pss = mm8(lambda cc, a=a: St[a][:, cc, :], lambda cc, a=a: S[a][:, cc, :], sk_prev)
            for g, (ps, pk) in enumerate(pss):
                p.op("act", lambda e, ps=ps, g=g, b=b: e.activation(out=grp(S[b], g), in_=psv(ps), func=AF.Copy),
                     reads=[pk], writes=[("S", b, g)])
            yield
            if j < 5:
                pss = mm8(lambda cc, a=a: S[a][:, cc, :], lambda cc, a=a: St[a][:, cc, :], sk_prev)
                for g, (ps, pk) in enumerate(pss):
                    p.op("act", lambda e, ps=ps, g=g, b=b: e.activation(out=grp(St[b], g), in_=psv(ps), func=AF.Copy),
                         reads=[pk], writes=[("St", b, g)])
                yield
            src, dst = rts[(j - 1) % 2], rts[j % 2]
            srck, dstk = rtk[(j - 1) % 2], rtk[j % 2]
            pss = mm8(lambda cc, b=b: S[b][:, cc, :], lambda cc, src=src: src[:, cc, :],
                      lambda g, b=b, srck=srck: [("S", b, g), srck + (g,)])
            for g, (ps, pk) in enumerate(pss):
                p.op("dve", lambda e, ps=ps, g=g, src=src, dst=dst: e.tensor_tensor(
                    out=grp(dst, g), in0=psv(ps), in1=grp(src, g), op=ALU.add),
                    reads=[pk, srck + (g,)], writes=[dstk + (g,)])
            yield

    def stage_c(ci):
        col, is_ctx = chunks[ci]
        ex = EX[ci % 3]
        ek = ("ex", ci % 3)
        pr = PR[ci % 2]
        qk = ("pr", ci % 2)
        Hc, Hn = H[ci % 2], H[(ci + 1) % 2]
        hk, hnk = ("H", ci % 2), ("H", (ci + 1) % 2)
        ident = lambda cc: M.ident
        two = lambda k: [k + (0,), k + (1,)]
        bonesb = M.bones.rearrange("p (o m) -> p o m", o=1).to_broadcast([128, 4, 128])

        def masked_ev(pss, dst, dk, mask=None):
            for g, (ps, pk) in enumerate(pss):
                p.op("dve", lambda e, ps=ps, g=g: e.tensor_tensor(
                    out=grp(dst, g), in0=psv(ps),
                    in1=(bonesb if mask is None else mask.rearrange("p (o m) -> p o m", o=1).to_broadcast(
                        [128, 4, 128])), op=ALU.mult), reads=[pk, "cmat"], writes=[dk + (g,)])

        masked_ev(mm8(lambda cc: ex["v2"][:, cc, :], ident, [ek + ("v",), "ident"]), Vstk, ("Vstk",))
        yield
        pss = mm8(lambda cc: ex["Ax"][:, cc, :], lambda cc: Hc[:, cc, :],
                  lambda g: [ek + ("Ax",), hk + (g,), qk + ("AakT", g), ("Vstk", g)],
                  extra=[(lambda cc: pr["AakT"][:, cc, :], lambda cc: Vstk[:, cc, :])])
        for g, (ps, pk) in enumerate(pss):
            p.op("act", lambda e, ps=ps, g=g: e.activation(out=grp(Pt, g), in_=psv(ps), func=AF.Copy),
                 reads=[pk], writes=[("Pt", g)])
        yield
        masked_ev(mm8(lambda cc: ex["b2"][:, cc, :], ident, [ek + ("b",), "ident"]), Bstk, ("Bstk",))
        masked_ev(mm8(lambda cc: ex["k2"][:, cc, :], ident, [ek + ("k",), "ident"]), Kstk, ("Kstk",))
        if not is_ctx:
            Rx = lambda cc: ex["Rx"][:, cc, :]
            masked_ev(mm8(lambda cc: ex["b2"][:, cc, :], Rx, [ek + ("Rx",), ek + ("b",)]), RbT, ("RbT",), M.mui)
            masked_ev(mm8(lambda cc: ex["k2"][:, cc, :], Rx, [ek + ("Rx",), ek + ("k",)]), RkT, ("RkT",), M.mui)
        yield
        pss = mm8(lambda cc: pr["RtF"][:, cc, :], lambda cc: Pt[:, cc, :],
                  lambda g: [qk + ("RtF", g), ("Pt", g)])
        for g, (ps, pk) in enumerate(pss):
            p.op("dve", lambda e, ps=ps, g=g: e.tensor_copy(out=grp(Ut, g), in_=psv(ps)), reads=[pk],
                 writes=[("Ut", g)])
        yield
        Hmc, Hmn = Hm[ci % 2], Hm[(ci + 1) % 2]
        hmk, hmnk = ("Hm", ci % 2), ("Hm", (ci + 1) % 2)
        pss = mm8(lambda cc: Bstk[:, cc, :], lambda cc: Ut[:, cc, :],
                  lambda g: [("Bstk", g), ("Kstk", g), ("Ut", g), ("Vstk", g)],
                  extra=[(lambda cc: Kstk[:, cc, :], lambda cc: Vstk[:, cc, :])])
        for g, (ps, pk) in enumerate(pss):
            p.op("dve", lambda e, ps=ps, g=g: e.tensor_tensor(out=grp(Hmn, g), in0=psv(ps), in1=grp(Hmc, g),
                                                             op=ALU.add),
                 reads=[pk, hmk + (g,)], writes=[hmnk + (g,)])
            p.op("pool", lambda e, g=g: e.tensor_tensor(
                out=grp(Hmn, g), in0=grp(Hmn, g),
                in1=ex["gam"][:, 4 * g:4 * g + 4].rearrange("p (c o) -> p c o", o=1).to_broadcast([128, 4, 128]),
                op=ALU.mult), reads=[hmnk + (g,), ek + ("gam",)], writes=[hmnk + (g,)])
            p.op("act", lambda e, g=g: e.activation(out=grp(Hn, g), in_=grp(Hmn, g), func=AF.Copy),
                 reads=[hmnk + (g,)], writes=[hnk + (g,)])
        yield
        if not is_ctx:
            pss = mm8(lambda cc: Hc[:, cc, :], lambda cc: ex["Rx"][:, cc, :],
                      lambda g: [hk + (g,), ek + ("Rx",), ("Ut", g), ("RbT", g), ("RkT", g), ("Vstk", g)],
                      extra=[(lambda cc: Ut[:, cc, :], lambda cc: RbT[:, cc, :]),
                             (lambda cc: Vstk[:, cc, :], lambda cc: RkT[:, cc, :])])
            y3 = v3(yT)
            for g, (ps, pk) in enumerate(pss):
                for hh in range(2):
                    rows = slice(64 * hh, 64 * hh + 64)
                    o = y3[rows, 4 * g:4 * g + 4, :]
                    if rev:
                        o = o[:, :, ::-1]
                    src = psv(ps)[rows, :, 64 * hh:64 * hh + 64]
                    if hh == 0:
                        p.op("dve", lambda e, o=o, src=src: e.tensor_copy(out=o, in_=src), reads=[pk],
                             writes=[("yT", g, hh)])
                    else:
                        p.op("act", lambda e, o=o, src=src: e.activation(out=o, in_=src, func=AF.Copy), reads=[pk],
                             writes=[("yT", g, hh)])
            p.dma("pool", C.yTs[d][:, col:col + 64].rearrange("(c p) t -> p c t", p=128), y3,
                  reads=[("yT", g, hh) for g in range(2) for hh in range(2)], writes=[("yTs", d, ci)], group="yTd")
        yield

    def drain(*gens):
        gens = [g for g in gens if g is not None]
        while gens:
            for g in list(gens):
                try:
                    next(g)
                except StopIteration:
                    gens.remove(g)

    drain(prep(0))
    drain(stage_ab(0), prep(1))
    for ci in range(NCH):
        drain(stage_c(ci),
              stage_ab(ci + 1) if ci + 1 < NCH else None,
              prep(ci + 2) if ci + 2 < NCH else None)


def rwkv_out_phase(C):
    p, A, M = C.p, C.A, C.M
    L = [{n: A.f32(512) for n in ["y0", "y1", "b0", "b1", "g"]} for _ in range(2)]
    sq = A.f32(512)
    mean, var, rstd = A.f32(512), A.f32(512), A.f32(512)
    ob = [A.bf16(512) for _ in range(2)]
    ogg, _ = VC["gn_g"]
    ogb, _ = VC["gn_b"]
    n = 0
    for ti in range(8):
        t0 = ti * 512
        cks = list(range(8 * ti + 4, 8 * ti + 12)) if False else None
        for cc in range(8):
            l = L[n % 2]
            lk = lambda nm, n=n: ("rl", n % 2, nm)
            rows = slice(cc * 128, (cc + 1) * 128)
            for nm, src, dep in (("y0", C.yTs[0], "yTs0"), ("y1", C.yTs[1], "yTs1"), ("b0", C.bonT[0], "bon0"),
                                 ("b1", C.bonT[1], "bon1"), ("g", C.gTs, "gTs")):
                p.dma("sp", l[nm], src[rows, t0:t0 + 512], reads=["mixdone"], writes=[lk(nm)])
            p.op("pool", lambda e, l=l: e.tensor_tensor(out=l["y0"], in0=l["y0"], in1=l["y1"], op=ALU.add),
                 reads=[lk("y0"), lk("y1")], writes=[lk("y0")])
            p.op("pool", lambda e, l=l: e.tensor_tensor(out=l["b0"], in0=l["b0"], in1=l["b1"], op=ALU.add),
                 reads=[lk("b0"), lk("b1")], writes=[lk("b0")])
            p.op("act", lambda e, l=l: e.activation(out=sq, in_=l["y0"], func=AF.Square), reads=[lk("y0")],
                 writes=["rsq"])
            p.op("pe", lambda e, l=l: e.matmul(C.ps[0][:, :], M.bones, l["y0"], start=True, stop=True),
                 reads=["cmat", lk("y0")], writes=[("ps", 0)])
            p.op("pe", lambda e: e.matmul(C.ps[1][:, :], M.bones, sq, start=True, stop=True),
                 reads=["cmat", "rsq"], writes=[("ps", 1)])
            inv = 1.0 / 64
            p.op("act", lambda e: e.activation(out=mean, in_=C.ps[0][:, :], func=AF.Copy, scale=inv),
                 reads=[("ps", 0)], writes=["rmean"])
            p.op("act", lambda e: e.activation(out=var, in_=C.ps[0][:, :], func=AF.Square, scale=inv),
                 reads=[("ps", 0)], writes=["rvar"])
            p.op("dve", lambda e: e.scalar_tensor_tensor(out=var, in0=C.ps[1][:, :], scalar=inv, in1=var,
                                                         op0=ALU.mult, op1=ALU.subtract),
                 reads=[("ps", 1), "rvar"], writes=["rvar"])
            p.op("act", lambda e: e.activation(out=rstd, in_=var, func=AF.Ln, bias=M.epsgn, scale=1.0),
                 reads=["rvar", "epsgn"], writes=["rrstd"])
            p.op("act", lambda e: e.activation(out=rstd, in_=rstd, func=AF.Exp, scale=-0.5), reads=["rrstd"],
                 writes=["rrstd"])
            p.op("pool", lambda e, l=l: e.tensor_tensor(out=l["y0"], in0=l["y0"], in1=mean, op=ALU.subtract),
                 reads=[lk("y0"), "rmean"], writes=[lk("y0")])
            p.op("dve", lambda e, l=l: e.tensor_tensor(out=l["y0"], in0=l["y0"], in1=rstd, op=ALU.mult),
                 reads=[lk("y0"), "rrstd"], writes=[lk("y0")])
            p.op("act", lambda e, l=l, cc=cc: e.activation(out=l["y0"], in_=l["y0"], func=AF.Identity,
                                                           scale=C.vt[:, ogg + cc:ogg + cc + 1],
                                                           bias=C.vt[:, ogb + cc:ogb + cc + 1]),
                 reads=[lk("y0"), "vecs"], writes=[lk("y0")])
            p.op("pool", lambda e, l=l: e.tensor_tensor(out=l["y0"], in0=l["y0"], in1=l["b0"], op=ALU.add),
                 reads=[lk("y0"), lk("b0")], writes=[lk("y0")])
            o = ob[n % 2]
            p.op("dve", lambda e, l=l, o=o: e.tensor_tensor(out=o, in0=l["y0"], in1=l["g"], op=ALU.mult),
                 reads=[lk("y0"), lk("g")], writes=[("rob", n % 2)])
            p.dma("pool", C.catT[rows, t0:t0 + 512], o, reads=[("rob", n % 2)], writes=[("catr", ti, cc)],
                  group=("robd", n % 2))
            n += 1


def wout_phase(C):
    p, A = C.p, C.A
    wo = r3(A.bf16(16 * 2048), b=2048)
    cat = [r3(A.bf16(16 * 512), b=512) for _ in range(2)]
    z = r3(A.f32(16 * 512), b=512)
    sq = [A.f32(512) for _ in range(2)]
    mean, var, rstd = A.f32(512), A.f32(512), A.f32(512)
    ost = [A.f32(512) for _ in range(2)]
    wv = C.w_out_b.rearrange("(kc p) f -> p kc f", p=128)
    for h in range(4):
        p.dma("sp", wo[:, 4 * h:4 * h + 4, :], wv[:, 4 * h:4 * h + 4, :], reads=C.cast_keys["w_out"],
              writes=[("wo", h)])
    wok = [("wo", h) for h in range(4)]
    shift, sc1p, gate, mk = mod_aps(C, "x", 1)
    og, _ = VC["ln_g"]
    ol, _ = VC["ln_b"]
    lnj = 1
    nt = 512
    oidx = 0
    for ti in range(8):
        t0 = ti * 512
        ct = cat[ti % 2]
        p.dma("sp", ct, C.catT[:, t0:t0 + 512].rearrange("(kc p) t -> p kc t", p=128),
              reads=[("catg", ti)] + [("catr", ti, cc) for cc in range(8)], writes=[("cat", ti % 2)])
        pend = None

        def stats(oc):
            p.op("pe", lambda e: e.matmul(C.ps[6][:, :], C.ones, z[:, oc, :], start=(oc == 0), stop=(oc == 15)),
                 reads=["ones", ("z", oc)], writes=[("ps", 6)], sig=(oc == 15))
            p.op("pe", lambda e: e.matmul(C.ps[7][:, :], C.ones, sq[oc % 2], start=(oc == 0), stop=(oc == 15)),
                 reads=["ones", ("sq", oc % 2)], writes=[("ps", 7)], sig=True)

        for oc in range(16):
            p.dma("pool", z[:, oc, :], C.x1T[oc * 128:(oc + 1) * 128, t0:t0 + 512],
                  reads=[("dst", "fa", ti, oc)], writes=[("z", oc)])
            ps = C.ps[4 + oc % 2]
            for kc in range(16):
                p.op("pe", lambda e, ps=ps, kc=kc, oc=oc, ct=ct: e.matmul(
                    ps[:, :], wo[:, kc, oc * 128:(oc + 1) * 128], ct[:, kc, :], start=(kc == 0), stop=(kc == 15)),
                    reads=(wok + [("cat", ti % 2)]) if kc == 0 else [], writes=[("ps", 4 + oc % 2)], sig=(kc == 15))
            if pend is not None:
                stats(pend)
            p.op("dve", lambda e, ps=ps, oc=oc: e.scalar_tensor_tensor(
                out=z[:, oc, :], in0=ps[:, :], scalar=gate[:, oc:oc + 1], in1=z[:, oc, :],
                op0=ALU.mult, op1=ALU.add), reads=[("ps", 4 + oc % 2)] + mk, writes=[("z", oc)])
            p.op("act", lambda e, oc=oc: e.activation(out=sq[oc % 2], in_=z[:, oc, :], func=AF.Square),
                 reads=[("z", oc)], writes=[("sq", oc % 2)])
            pend = oc
        stats(pend)
        inv = 1.0 / D
        p.op("act", lambda e: e.activation(out=mean, in_=C.ps[6][:, :], func=AF.Copy, scale=inv),
             reads=[("ps", 6)], writes=["mean"])
        p.op("act", lambda e: e.activation(out=var, in_=C.ps[6][:, :], func=AF.Square, scale=inv),
             reads=[("ps", 6)], writes=["var"])
        p.op("dve", lambda e: e.scalar_tensor_tensor(out=var, in0=C.ps[7][:, :], scalar=inv, in1=var,
                                                     op0=ALU.mult, op1=ALU.subtract),
             reads=[("ps", 7), "var"], writes=["var"])
        p.op("act", lambda e: e.activation(out=rstd, in_=var, func=AF.Ln, bias=C.epsln, scale=1.0),
             reads=["var", "epsc"], writes=["rstd"])
        p.op("act", lambda e: e.activation(out=rstd, in_=rstd, func=AF.Exp, scale=-0.5), reads=["rstd"],
             writes=["rstd"])
        for oc in range(16):
            p.op("pool", lambda e, oc=oc: e.tensor_tensor(out=z[:, oc, :], in0=z[:, oc, :], in1=mean, op=ALU.subtract),
                 reads=[("z", oc), "mean"], writes=[("z", oc)])
            p.op("dve", lambda e, oc=oc: e.tensor_tensor(out=z[:, oc, :], in0=z[:, oc, :], in1=rstd, op=ALU.mult),
                 reads=[("z", oc), "rstd"], writes=[("z", oc)])
            oi = oidx % 2
            oidx += 1
            ot = ost[oi]
            gcol = C.vt[:, og + lnj * 16 + oc:og + lnj * 16 + oc + 1]
            bcol = C.vt[:, ol + lnj * 16 + oc:ol + lnj * 16 + oc + 1]
            p.op("act", lambda e, oc=oc, ot=ot, gcol=gcol, bcol=bcol: e.activation(
                out=ot, in_=z[:, oc, :], func=AF.Identity, scale=gcol, bias=bcol),
                reads=[("z", oc), "vecs"], writes=[("ost", oi)])
            p.dma("act", C.x2T[oc * 128:(oc + 1) * 128, t0:t0 + 512], ot,
                  reads=[("ost", oi)], writes=[("dst", "w3", ti, oc)], group=("ostd", oi))


_NC_CACHE = {}


def _pm(v):
    return np.ascontiguousarray(np.asarray(v, np.float32).reshape(-1, 128).T)


def _cmat():
    m = np.zeros((128, 7, 128), np.float32)
    r = np.arange(128)
    hb = (r[:, None] // 64) == (r[None, :] // 64)
    ti, tj = r[:, None] % 64, r[None, :] % 64
    m[:, 0, :] = np.eye(128)
    m[:, 1, :] = hb & (tj < ti)
    m[:, 2, :] = hb & (ti < tj)
    m[:, 3, :] = hb & (ti <= tj)
    m[:, 4, :] = hb
    m[:, 5, :] = 1.0
    m[:, 5, 0] = 0.0
    m[:64, 6, :] = 1.0
    return m


def _lora(inputs):
    m = np.zeros((128, 2, 2, 1024), np.float32)
    m[:64, :, 0, :] = np.transpose(inputs["w_up"][0], (1, 0, 2))
    m[64:, :, 1, :] = np.transpose(inputs["a_up"][0], (1, 0, 2))
    return m


def make_in_maps(inputs):
    x = np.asarray(inputs["x"], np.float32)
    ctx = np.asarray(inputs["ctx"], np.float32)
    maps = []
    shared = {
        "w_ada": np.ascontiguousarray(inputs["w_ada"][0], dtype=np.float32),
        "ffn_a_wi": np.ascontiguousarray(inputs["ffn_a_wi"][0], dtype=np.float32),
        "ffn_a_wo": np.ascontiguousarray(inputs["ffn_a_wo"][0], dtype=np.float32),
        "ffn_b_wi": np.ascontiguousarray(inputs["ffn_b_wi"][0], dtype=np.float32),
        "ffn_b_wo": np.ascontiguousarray(inputs["ffn_b_wo"][0], dtype=np.float32),
        "w_in": np.ascontiguousarray(inputs["w_in"][0], dtype=np.float32),
        "w_out": np.ascontiguousarray(inputs["w_out"][0], dtype=np.float32),
        "gm_ln_g": np.ascontiguousarray(inputs["gm_ln_g"][0], dtype=np.float32),
        "gm_ln_b": np.ascontiguousarray(inputs["gm_ln_b"][0], dtype=np.float32),
        "gm_bs": np.ascontiguousarray(inputs["gm_bs"][0].reshape(-1), dtype=np.float32),
        "gm_wsT": np.ascontiguousarray(np.transpose(inputs["gm_ws"][0], (2, 0, 1)), dtype=np.float32),
        "cmat": _cmat(),
        "lora": _lora(inputs),
        "g_up": np.ascontiguousarray(inputs["g_up"][0], dtype=np.float32),
    }
    for b in range(NCORES):
        vec = np.zeros((128, NV), np.float32)

        def put(name, arr):
            o, w = VC[name]
            assert arr.shape == (128, w), (name, arr.shape)
            vec[:, o:o + w] = arr

        put("c", _pm(inputs["c"][b]))
        put("cctx", _pm(inputs["c_ctx"]))
        put("b_ada", _pm(inputs["b_ada"][0]))
        put("ln_g", _pm(inputs["ln_g"][0].reshape(-1)))
        put("ln_b", _pm(inputs["ln_b"][0].reshape(-1)))
        put("mu", _pm(inputs["mu_shift"][0]))
        put("w0", _pm(inputs["w0"][0].reshape(-1)))
        put("a0", _pm(inputs["a0"][0].reshape(-1)))
        put("k_k", _pm(inputs["k_k"][0]))
        put("k_a", _pm(inputs["k_a"][0]))
        put("r_k", _pm(inputs["r_k"][0].reshape(-1)))
        put("gn_g", _pm(inputs["gn_g"][0]))
        put("gn_b", _pm(inputs["gn_b"][0]))
        put("sel4", (np.arange(128)[:, None] % 4 == np.arange(4)[None, :]).astype(np.float32))
        put("sel2", (np.arange(128)[:, None] % 2 == np.arange(2)[None, :]).astype(np.float32))
        xT = np.empty((D, TT), np.float32)
        xT[:, :T] = x[b].T
        xT[:, T:] = ctx[b].T
        m = dict(shared)
        m["xT"] = xT
        m["vecs"] = vec
        maps.append(m)
    return maps


def kernel(**inputs):
    if "nc" not in _NC_CACHE:
        _NC_CACHE["nc"] = build()
    nc = _NC_CACHE["nc"]
    maps = make_in_maps(inputs)
    res = run_bass_kernel_spmd(nc, maps, core_ids=list(range(NCORES)))
    out = np.empty((NCORES, T, D), np.float32)
    for b in range(NCORES):
        out[b] = res.results[b]["outT"].T
    return out
```
